# Optimizing a Trainium2 kernel written in Bass

```python
import math
import jax, jax.numpy as jnp
from jax import lax
import numpy as np

D_MODEL = 1024
BATCH = 4
SEQ = 8192
DEPTH = 2

N_META = 16
BLK = 128
META_BLK = BLK

MLA_HEADS = 6
MLA_Q_RANK = 256
MLA_KV_RANK = 128
MLA_NOPE = 64
MLA_ROPE = 32
MLA_V = 64
ROPE_THETA = 10000.0

DIFF_HEADS = 4
DIFF_QK = 32
DIFF_V = 2 * DIFF_QK

SWA_HEADS = 6
SWA_KV_HEADS = 2
SWA_HD = 64
WINDOW = 128

REL_BUCKETS = 32
REL_MAX_DIST = 128
N_BIAS_HEADS = DIFF_HEADS + SWA_HEADS

D_FF = -(-(8 * D_MODEL) // (3 * 256)) * 256
NEG_INF = -1e30

IN_WIDTHS = (MLA_Q_RANK, MLA_KV_RANK, MLA_ROPE,
             DIFF_HEADS * 2 * DIFF_QK, DIFF_HEADS * 2 * DIFF_QK, DIFF_HEADS * DIFF_V,
             SWA_HEADS * SWA_HD, SWA_KV_HEADS * SWA_HD, SWA_KV_HEADS * SWA_HD)
D_IN = sum(IN_WIDTHS)
IN_OFFSETS = tuple(sum(IN_WIDTHS[:i + 1]) for i in range(len(IN_WIDTHS) - 1))
D_MIX = MLA_HEADS * MLA_V + DIFF_HEADS * DIFF_V + SWA_HEADS * SWA_HD

kernel_name = "hymba_mla_diff_swa_hybrid"


def rms_norm(x, g, eps=1e-6):
    xf = x.astype(jnp.float32)
    y = xf * lax.rsqrt(jnp.mean(xf * xf, axis=-1, keepdims=True) + eps)
    return (y * g.astype(jnp.float32)).astype(x.dtype)


def rotate(x, cos, sin):
    x1, x2 = jnp.split(x, 2, axis=-1)
    return jnp.concatenate([x1 * cos - x2 * sin, x2 * cos + x1 * sin], axis=-1)


def t5_bucket(q_pos, k_pos):
    n = jnp.maximum(q_pos - k_pos, 0)
    max_exact = REL_BUCKETS // 2
    nf = jnp.maximum(n, max_exact).astype(jnp.float32)
    large = max_exact + (jnp.log(nf / max_exact) / math.log(REL_MAX_DIST / max_exact)
                         * (REL_BUCKETS - max_exact)).astype(jnp.int32)
    large = jnp.minimum(large, REL_BUCKETS - 1)
    return jnp.where(n < max_exact, n, large)


def masked_softmax(logits, mask):
    z = jnp.where(mask, logits.astype(jnp.float32), NEG_INF)
    return jax.nn.softmax(z, axis=-1)


def mla_mixer(c_q, c_kv, k_rope, q_norm, w_qb, kv_norm, w_kvb, cos, sin, idx, valid):
    B, L, _ = c_q.shape
    q = (rms_norm(c_q, q_norm) @ w_qb).reshape(B, L, MLA_HEADS, MLA_NOPE + MLA_ROPE)
    q_nope = q[..., :MLA_NOPE]
    q_rot = rotate(q[..., MLA_NOPE:], cos[None, :, None], sin[None, :, None])
    kv = (rms_norm(c_kv, kv_norm) @ w_kvb).reshape(B, L, MLA_HEADS, MLA_NOPE + MLA_V)
    k_nope, v = kv[..., :MLA_NOPE], kv[..., MLA_NOPE:]
    k_rot = rotate(k_rope, cos[None], sin[None])
    scale = (MLA_NOPE + MLA_ROPE) ** -0.5

    def one_block(i):
        s = i * BLK
        qn = lax.dynamic_slice_in_dim(q_nope, s, BLK, axis=1)
        qr = lax.dynamic_slice_in_dim(q_rot, s, BLK, axis=1)
        logits = (jnp.einsum('bqhd,bkhd->bhqk', qn, k_nope)
                  + jnp.einsum('bqhd,bkd->bhqk', qr, k_rot)) * scale
        q_idx = s + jnp.arange(BLK)
        mask = (idx[None, :] <= q_idx[:, None]) & valid[None, :]
        p = masked_softmax(logits, mask).astype(v.dtype)
        return jnp.einsum('bhqk,bkhd->bqhd', p, v)

    out = lax.map(one_block, jnp.arange(L // BLK))
    return out.transpose(1, 0, 2, 3, 4).reshape(B, L, MLA_HEADS * MLA_V)


def diff_mixer(q, k, v, lam_p, sub_g, lam_init, bias_tab, pos, idx, valid):
    B, L, _ = q.shape
    q = q.reshape(B, L, DIFF_HEADS, 2, DIFF_QK)
    k = k.reshape(B, L, DIFF_HEADS, 2, DIFF_QK)
    v = v.reshape(B, L, DIFF_HEADS, DIFF_V)
    lp = lam_p.astype(jnp.float32)
    lam = jnp.exp(jnp.sum(lp[0] * lp[1])) - jnp.exp(jnp.sum(lp[2] * lp[3])) + lam_init
    scale = DIFF_QK ** -0.5

    def one_block(i):
        s = i * BLK
        qb = lax.dynamic_slice_in_dim(q, s, BLK, axis=1)
        q_idx = s + jnp.arange(BLK)
        q_pos = lax.dynamic_slice_in_dim(pos, s, BLK)
        bias = bias_tab[t5_bucket(q_pos[:, None], pos[None, :])]
        bias = bias.transpose(2, 0, 1).astype(jnp.float32)
        logits = jnp.einsum('bqhcd,bkhcd->bchqk', qb, k).astype(jnp.float32) * scale + bias[None, None]
        mask = (idx[None, :] <= q_idx[:, None]) & valid[None, :]
        p = masked_softmax(logits, mask)
        attn = (p[:, 0] - lam * p[:, 1]).astype(v.dtype)
        return jnp.einsum('bhqk,bkhd->bqhd', attn, v)

    out = lax.map(one_block, jnp.arange(L // BLK))
    out = out.transpose(1, 0, 2, 3, 4).reshape(B, L, DIFF_HEADS, DIFF_V)
    out = rms_norm(out, sub_g) * (1.0 - lam_init)
    return out.reshape(B, L, DIFF_HEADS * DIFF_V)


def swa_mixer(q, k, v, sinks, bias_tab, pos, real):
    B, L, _ = q.shape
    nb = L // BLK
    G, R = SWA_KV_HEADS, SWA_HEADS // SWA_KV_HEADS
    qb = q.reshape(B, nb, BLK, G, R, SWA_HD)
    kb = k.reshape(B, nb, BLK, G, SWA_HD)
    vb = v.reshape(B, nb, BLK, G, SWA_HD)

    def band(t, meta):
        prev = jnp.concatenate([t[:, :1], t[:, :-1]], axis=1)
        meta_b = jnp.broadcast_to(meta[:, None], (B, nb) + meta.shape[1:])
        return jnp.concatenate([meta_b, prev, t], axis=2)

    keys = band(kb, k.reshape(B, L, G, SWA_HD)[:, :N_META])
    vals = band(vb, v.reshape(B, L, G, SWA_HD)[:, :N_META])
    K = N_META + 2 * BLK
    blk = jnp.arange(nb)
    ar = jnp.arange(BLK)
    q_idx = blk[:, None] * BLK + ar[None, :]
    prev_idx = jnp.maximum(blk - 1, 0)[:, None] * BLK + ar[None, :]
    k_idx = jnp.concatenate([jnp.broadcast_to(jnp.arange(N_META), (nb, N_META)), prev_idx, q_idx], axis=1)
    in_band = jnp.arange(K) >= N_META
    q_pos, k_pos = pos[q_idx], pos[k_idx]
    causal = k_idx[:, None, :] <= q_idx[:, :, None]
    window_ok = (q_pos[:, :, None] - k_pos[:, None, :]) < WINDOW
    mask = causal & jnp.where(in_band[None, None, :], real[k_idx][:, None, :] & window_ok, True)
    bias = bias_tab[t5_bucket(q_pos[:, :, None], k_pos[:, None, :])]
    bias = bias.transpose(0, 3, 1, 2).reshape(nb, G, R, BLK, K).astype(jnp.float32)
    logits = jnp.einsum('bnqgrd,bnkgd->bngrqk', qb, keys).astype(jnp.float32) * (SWA_HD ** -0.5) + bias[None]
    logits = jnp.where(mask[None, :, None, None], logits, NEG_INF)
    sink = sinks.astype(jnp.float32).reshape(G, R)[None, None, :, :, None, None]
    m = jnp.maximum(jnp.max(logits, axis=-1, keepdims=True), sink)
    p = jnp.exp(logits - m)
    p = p / (jnp.sum(p, axis=-1, keepdims=True) + jnp.exp(sink - m))
    out = jnp.einsum('bngrqk,bnkgd->bnqgrd', p.astype(v.dtype), vals)
    return out.reshape(B, L, SWA_HEADS * SWA_HD)


def setup_inputs(seed: int = 0) -> dict:
    key = jax.random.key(seed)
    ks = jax.random.split(key, 20)
    n = jax.random.normal
    f32 = jnp.float32
    return {
        "x": n(ks[0], (BATCH, SEQ, D_MODEL), f32),
        "meta_tokens": n(ks[1], (N_META, D_MODEL), f32),
        "rel_bias": 0.5 * n(ks[2], (REL_BUCKETS, N_BIAS_HEADS), f32),
        "attn_norm": 1.0 + 0.02 * n(ks[3], (DEPTH, D_MODEL), f32),
        "w_in": n(ks[4], (DEPTH, D_MODEL, D_IN), f32) * D_MODEL ** -0.5,
        "mla_q_norm": 1.0 + 0.02 * n(ks[5], (DEPTH, MLA_Q_RANK), f32),
        "mla_w_qb": n(ks[6], (DEPTH, MLA_Q_RANK, MLA_HEADS * (MLA_NOPE + MLA_ROPE)), f32) * MLA_Q_RANK ** -0.5,
        "mla_kv_norm": 1.0 + 0.02 * n(ks[7], (DEPTH, MLA_KV_RANK), f32),
        "mla_w_kvb": n(ks[8], (DEPTH, MLA_KV_RANK, MLA_HEADS * (MLA_NOPE + MLA_V)), f32) * MLA_KV_RANK ** -0.5,
        "diff_lambda": 0.1 * n(ks[9], (DEPTH, 4, DIFF_QK), f32),
        "diff_subln": 1.0 + 0.02 * n(ks[10], (DEPTH, DIFF_V), f32),
        "swa_sinks": n(ks[11], (DEPTH, SWA_HEADS), f32),
        "w_out": n(ks[12], (DEPTH, D_MIX, D_MODEL), f32) * D_MIX ** -0.5,
        "ffn_norm": 1.0 + 0.02 * n(ks[13], (DEPTH, D_MODEL), f32),
        "w_gate": n(ks[14], (DEPTH, D_MODEL, D_FF), f32) * D_MODEL ** -0.5,
        "w_up": n(ks[15], (DEPTH, D_MODEL, D_FF), f32) * D_MODEL ** -0.5,
        "w_down": n(ks[16], (DEPTH, D_FF, D_MODEL), f32) * D_FF ** -0.5,
        "final_norm": 1.0 + 0.02 * n(ks[17], (D_MODEL,), f32),
    }


def reference(x, meta_tokens, rel_bias, attn_norm, w_in, mla_q_norm, mla_w_qb, mla_kv_norm, mla_w_kvb,
              diff_lambda, diff_subln, swa_sinks, w_out, ffn_norm, w_gate, w_up, w_down, final_norm):
    B = x.shape[0]
    meta = jnp.broadcast_to(meta_tokens.astype(x.dtype)[None], (B, N_META, D_MODEL))
    pad = jnp.zeros((B, META_BLK - N_META, D_MODEL), x.dtype)
    h = jnp.concatenate([meta, pad, x], axis=1)
    L = h.shape[1]
    idx = jnp.arange(L)
    real = idx >= META_BLK
    valid = real | (idx < N_META)
    pos = jnp.where(real, idx - META_BLK + N_META, jnp.minimum(idx, N_META - 1))
    inv_freq = ROPE_THETA ** (-jnp.arange(0, MLA_ROPE, 2, dtype=jnp.float32) / MLA_ROPE)
    ang = pos.astype(jnp.float32)[:, None] * inv_freq[None, :]
    cos, sin = jnp.cos(ang).astype(x.dtype), jnp.sin(ang).astype(x.dtype)
    bias_b, bias_c = rel_bias[:, :DIFF_HEADS], rel_bias[:, DIFF_HEADS:]

    for l in range(DEPTH):
        hn = rms_norm(h, attn_norm[l])
        proj = hn @ w_in[l]
        c_q, c_kv, k_rope, dq, dk, dv, sq, sk, sv = jnp.split(proj, IN_OFFSETS, axis=-1)
        y_a = mla_mixer(c_q, c_kv, k_rope, mla_q_norm[l], mla_w_qb[l], mla_kv_norm[l], mla_w_kvb[l],
                        cos, sin, idx, valid)
        lam_init = 0.8 - 0.6 * math.exp(-0.3 * l)
        y_b = diff_mixer(dq, dk, dv, diff_lambda[l], diff_subln[l], lam_init, bias_b, pos, idx, valid)
        y_c = swa_mixer(sq, sk, sv, swa_sinks[l], bias_c, pos, real)
        h = h + jnp.concatenate([y_a, y_b, y_c], axis=-1) @ w_out[l]
        hn = rms_norm(h, ffn_norm[l])
        h = h + (jax.nn.silu(hn @ w_gate[l]) * (hn @ w_up[l])) @ w_down[l]

    h = rms_norm(h, final_norm)
    return h[:, META_BLK:]
```

```python
import math
import numpy as np
import concourse.bass as bass
import concourse.mybir as mybir
from concourse.bass_utils import run_bass_kernel_spmd

F32 = mybir.dt.float32
BF16 = mybir.dt.bfloat16
AF = mybir.ActivationFunctionType
ALU = mybir.AluOpType

D = 1024
DIN = 1824
DFF = 2816
NF = DFF // 128
NR = 2
NLOC = 33
T = NLOC * 128
VW = NLOC * 65
KVROWS = 1740
RM, RD, RS0, RS1 = 161, 129, 128, 130
OFF_M = 0
OFF_D = OFF_M + 6 * RM * T
OFF_S0 = OFF_D + 4 * RD * T
OFF_S1 = OFF_S0 + RS0 * T
assert OFF_S1 + RS1 * T == KVROWS * T
PIECES = [("m%d" % h, h * RM, RM) for h in range(6)] + [("d%d" % h, 6 * RM + h * RD, RD) for h in range(4)] + \
         [("s0", 6 * RM + 4 * RD, RS0), ("s1", 6 * RM + 4 * RD + RS0, RS1)]
NME = 41
SC_MLA = 96 ** -0.5
SC_DIFF = 32 ** -0.5
SC_SWA = 64 ** -0.5
EPS = 1e-6
LAM_INIT = [0.8 - 0.6 * math.exp(-0.3 * l) for l in range(2)]
WIN_SEGS = [(0, 928), (1184, 1696), (928, 1184), (1696, 1824)]
PGRP = [(0, 416), (416, 928), (928, 1440), (1440, 1824)]


class Buf:
    __slots__ = ("name", "ws", "r", "psum", "pr")

    def __init__(self, name):
        self.name = name
        self.ws = []
        self.r = []
        self.pr = []
        self.psum = False


class Op:
    __slots__ = ("e", "fn", "dma", "deps", "signal", "sem", "val", "idx", "cc")

    def __init__(self, e, fn, dma):
        self.e = e
        self.fn = fn
        self.dma = dma
        self.deps = []
        self.signal = False
        self.sem = None
        self.val = 0
        self.cc = None


class Tile:
    def __init__(self, ap, name):
        self.ap = ap
        self.buf = Buf(name)

    def __getitem__(self, k):
        return self.ap[k]


NDMASEM = 20


class Prog:
    ENG = ("pe", "act", "dve", "pool", "sp")

    def __init__(self, nc):
        self.nc = nc
        self.ops = {e: [] for e in self.ENG}
        self.dma_ops = {e: [] for e in self.ENG}
        self.bufs = []

    def buf(self, name):
        b = Buf(name)
        self.bufs.append(b)
        return b

    def tile(self, ap, name):
        t = Tile(ap, name)
        self.bufs.append(t.buf)
        return t

    def op(self, e, fn, reads=(), writes=(), dma=False):
        o = Op(e, fn, dma)
        deps = []
        seen = set()

        def add(d):
            if d is None or id(d) in seen:
                return
            seen.add(id(d))
            if d.e == "pe" and e == "pe" and not d.dma and not dma:
                return
            deps.append(d)

        rb = [t.buf if isinstance(t, Tile) else t for t in reads]
        wb = [t.buf if isinstance(t, Tile) else t for t in writes]
        for b in rb:
            for x in b.ws:
                add(x)
            if b.psum:
                for x in b.r:
                    if x.e != e:
                        add(x)
        for b in wb:
            for x in b.r:
                add(x)
            for x in b.pr:
                add(x)
            for x in b.ws:
                if not (x.dma and dma):
                    add(x)
        if dma:
            lst = self.dma_ops[e]
            k = len(lst)
            if k >= NDMASEM:
                add(lst[k - NDMASEM])
            lst.append(o)
        for d in deps:
            d.signal = True
        o.deps = deps
        wset = set(id(b) for b in wb)
        for b in wb:
            if b.r or any(b is x for x in rb):
                b.pr = list(b.r)
                b.ws = [o]
                b.r = []
            else:
                b.ws.append(o)
        for b in rb:
            if id(b) not in wset:
                b.r.append(o)
        self.ops[e].append(o)
        return o

    def barrier(self):
        lasts = []
        for e in self.ENG:
            for o in reversed(self.ops[e]):
                if o.fn is not None and not o.dma:
                    lasts.append(o)
                    break
            lasts.extend(self.dma_ops[e][-NDMASEM:])
        for e in self.ENG:
            o = Op(e, None, False)
            o.deps = [d for d in lasts]
            self.ops[e].append(o)
        for d in lasts:
            d.signal = True
        for b in self.bufs:
            b.ws = []
            b.r = []
            b.pr = []


def DAP(t, off, *dims):
    return bass.AP(t, off, [[s, n] for (s, n) in dims])


class Arena:
    def __init__(self, P, ap, size, name):
        self.P = P
        self.ap = ap
        self.size = size
        self.off = 0
        self.name = name

    def reset(self):
        self.off = 0

    def alloc(self, n, name, pat=None, **kw):
        n2 = (n + 15) // 16 * 16
        assert self.off + n2 <= self.size, (self.name, name, self.off, n2, self.size)
        ap = self.ap[:, self.off:self.off + n]
        self.off += n2
        if pat is not None:
            ap = ap.rearrange(pat, **kw)
        return self.P.tile(ap, name)


def t5_bucket_np(n):
    n = np.maximum(n, 0)
    nf = np.maximum(n, 16).astype(np.float32)
    large = 16 + (np.log(nf / np.float32(16)) / np.float32(math.log(128 / 16)) * np.float32(16)).astype(np.int32)
    large = np.minimum(large, 31)
    return np.where(n < 16, n, large)


def make_consts(r):
    c = {}
    inv_freq = (10000.0 ** (-np.arange(0, 32, 2, dtype=np.float32) / np.float32(32))).astype(np.float32)
    cs = np.zeros((128, NLOC, 64), np.float32)
    for i in range(NLOC):
        p = np.arange(128)
        if i == 0:
            pos = np.minimum(p, 15)
        else:
            G = NR * (i - 1) + 1 + r
            pos = (G - 1) * 128 + p + 16
        ang = pos.astype(np.float32)[:, None] * inv_freq[None, :]
        co = np.cos(ang).astype(np.float32)
        si = np.sin(ang).astype(np.float32)
        cs[:, i, 0:16] = co
        cs[:, i, 16:32] = co
        cs[:, i, 32:48] = -si
        cs[:, i, 48:64] = si
    c["cs_tab"] = cs.reshape(128, NLOC * 64)
    k = np.arange(128)[:, None]
    q = np.arange(128)[None, :]
    tri = (k <= q).astype(np.float32)
    one = np.ones((128, 128), np.float32)
    zero = np.zeros((128, 128), np.float32)
    mA = np.zeros((128, 9, 4, 128), np.float32)
    for s in range(8):
        for qb in range(4):
            j = 1 + s
            G = 2 * qb + 1 + r
            mA[:, s, qb, :] = one if j < G else (tri if j == G else zero)
    mA[:, 8, 0, :] = tri
    c["mA"] = mA.reshape(128, 9 * 512)
    cd = np.zeros((4, NME), np.float32)
    for s in range(9):
        for qb in range(4):
            j = s
            G = 2 * qb + 1 + r
            e = s * 4 + qb
            if j < G - 1:
                cd[0, e] = 1
            elif j == G - 1:
                cd[2, e] = 1
            elif j == G:
                cd[1, e] = 1
    for qb in range(4):
        G = 2 * qb + 1 + r
        if G == 1:
            cd[3, 36 + qb] = 1
        else:
            cd[0, 36 + qb] = 1
    cd[1, 40] = 1
    c["cdiff"] = np.broadcast_to(cd.reshape(1, 4 * NME), (128, 4 * NME)).copy()
    csw = np.zeros((4, 4), np.float32)
    if r == 0:
        csw[1, 0] = 1; csw[0, 1] = 1
        csw[3, 3] = 1
    else:
        csw[1, 1] = 1; csw[0, 2] = 1
        csw[2, 3] = 1
    c["cswa"] = np.broadcast_to(csw.reshape(1, 16), (128, 16)).copy()
    oh = np.zeros((32, 768), np.float32)
    al = np.zeros((10, 768), np.float32)
    for j in range(256):
        if j < 128:
            oh[t5_bucket_np(np.array(j))[()], j] = 1; al[:, j] = 1
        if j < 128:
            oh[31, 256 + j] = 1; al[0:4, 256 + j] = 1
        elif j > 128:
            oh[t5_bucket_np(np.array(j - 128))[()], 256 + j] = 1; al[:, 256 + j] = 1
        if j < 128:
            oh[t5_bucket_np(np.array(j + 16))[()], 512 + j] = 1; al[:, 512 + j] = 1
        elif j >= 241:
            oh[t5_bucket_np(np.array(j - 240))[()], 512 + j] = 1; al[:, 512 + j] = 1
    c["oh"] = oh
    c["aallow"] = al
    c["ident"] = np.eye(128, dtype=np.float32)
    return c


def build_program(debug=False, stop=None, dump=(), ncores=8, nblk=NLOC, cut=99):
    nc = bass.Bass("TRN2", target_bir_lowering=False)
    P = Prog(nc)

    def din(name, shape, dt=F32):
        return nc.dram_tensor(name, list(shape), dt, kind="ExternalInput")

    xin = din("xin", [32, 128, D])
    meta_tokens = din("meta_tokens", [16, D])
    rel_bias = din("rel_bias", [32, 10])
    attn_norm = din("attn_norm", [2, D])
    w_in = din("w_in", [2, D, DIN])
    mla_q_norm = din("mla_q_norm", [2, 256])
    mla_w_qb = din("mla_w_qb", [2, 256, 576])
    mla_kv_norm = din("mla_kv_norm", [2, 128])
    mla_w_kvb = din("mla_w_kvb", [2, 128, 768])
    diff_lambda = din("diff_lambda", [2, 4, 32])
    diff_subln = din("diff_subln", [2, 64])
    swa_sinks = din("swa_sinks", [2, 6])
    w_out = din("w_out", [2, D, D])
    ffn_norm = din("ffn_norm", [2, D])
    w_gate = din("w_gate", [2, D, DFF])
    w_up = din("w_up", [2, D, DFF])
    w_down = din("w_down", [2, DFF, D])
    final_norm = din("final_norm", [D])
    cs_tab = din("cs_tab", [128, NLOC * 64])
    mA_in = din("mA", [128, 9 * 512])
    cdiff_in = din("cdiff", [128, 4 * NME])
    cswa_in = din("cswa", [128, 16])
    oh_in = din("oh", [32, 768])
    aallow_in = din("aallow", [10, 768])
    ident_in = din("ident", [128, 128])
    out = nc.dram_tensor("out", [32, 128, D], F32, kind="ExternalOutput")

    def dscr(name, shape, dt):
        if debug and name in dump:
            return nc.dram_tensor(name, list(shape), dt, kind="ExternalOutput")
        return nc.dram_tensor(name, list(shape), dt)

    hres = dscr("hres", [NLOC, 128, D], F32)
    kvown = nc.dram_tensor("kvown", [KVROWS, T], BF16)
    kvall = {nm: nc.dram_tensor("kvall_" + nm, [NR * nr, T], BF16) for (nm, r0, nr) in PIECES}
    pbuf = {nm: P.buf("piece_" + nm) for (nm, r0, nr) in PIECES}
    QTm = dscr("QTm", [6 * 96, T], BF16)
    QTd = dscr("QTd", [256, T], BF16)
    QTs = dscr("QTs", [128, 3 * T], BF16)
    yT = dscr("yT", [D, T], BF16)
    Wb_in = dscr("Wb_in", [2, 128, 8 * DIN], BF16)
    Wb_qb = dscr("Wb_qb", [2, 128, 2 * 576], BF16)
    Wb_kvb = dscr("Wb_kvb", [2, 128, 768], BF16)
    Wb_out = dscr("Wb_out", [2, 128, 8 * D], BF16)
    Wb_g = dscr("Wb_g", [2, NF, 128, 8 * 128], BF16)
    Wb_u = dscr("Wb_u", [2, NF, 128, 8 * 128], BF16)
    Wb_d = dscr("Wb_d", [2, 128, NF * D], BF16)
    Gd = dscr("Gd", [10, 768], F32)
    zper = dscr("zper", [30, 129 * 256], F32)
    maskD = dscr("maskD", [4, 128, NME * 128], F32)
    maskS = dscr("maskS", [128, 2 * 4 * 384], F32)
    if debug and "kvown" in dump:
        kvodbg = nc.dram_tensor("kvodbg", [KVROWS, T], BF16, kind="ExternalOutput")

    ctx = []

    def enter(cm):
        ctx.append(cm)
        return cm.__enter__()

    NFA = 17 * 1024
    NBA = 56 * 1024
    arena_f_t = enter(nc.sbuf_tensor("arena_f", [128, NFA], F32))
    arena_b_t = enter(nc.sbuf_tensor("arena_b", [128, NBA], BF16))
    psum_t = enter(nc.psum_tensor("psum", [128, 8 * 512], F32))
    sems = {e: enter(nc.semaphore("sem_" + e)) for e in Prog.ENG}
    dsems = {e: [enter(nc.semaphore("dsem_%s_%d" % (e, i))) for i in range(NDMASEM)] for e in ("sp", "act", "pool")}
    ccsems = [enter(nc.semaphore("ccsem%d" % i)) for i in range(2 * len(PIECES))]
    blk = enter(nc.Block())

    AFa = Arena(P, arena_f_t[:, :], NFA, "f32")
    ABa = Arena(P, arena_b_t[:, :], NBA, "bf16")
    PS = [P.tile(psum_t[:, b * 512:(b + 1) * 512], "ps%d" % b) for b in range(8)]
    for t_ in PS:
        t_.buf.psum = True

    def psb(b):
        return psum_t[:, b * 512:(b + 1) * 512].bitcast(BF16)

    def dma(q, out_ap, in_ap, reads=(), writes=()):
        return P.op(q, lambda eng, o=out_ap, i=in_ap: eng.dma_start(out=o, in_=i), reads, writes, dma=True)

    def mm(out_ap, lhsT, rhs, start, stop, reads, writes, tp=None):
        if tp is not None:
            return P.op("pe", lambda eng: eng.matmul(out_ap, lhsT, rhs, start=start, stop=stop, tile_position=tp), reads, writes)
        return P.op("pe", lambda eng: eng.matmul(out_ap, lhsT, rhs, start=start, stop=stop), reads, writes)

    def tr(out_ap, in_ap, ident_ap, reads, writes):
        return P.op("pe", lambda eng: eng.transpose(out_ap, in_ap, ident_ap), reads, writes)

    def act(out_ap, in_ap, func, reads, writes, scale=1.0, bias=0.0, accum=None):
        if accum is not None:
            return P.op("act", lambda eng: eng.activation(out_ap, in_ap, func, bias=bias, scale=scale, accum_out=accum),
                        reads, writes)
        return P.op("act", lambda eng: eng.activation(out_ap, in_ap, func, bias=bias, scale=scale), reads, writes)

    def tt(e, out_ap, in0, in1, op, reads, writes):
        return P.op(e, lambda eng: eng.tensor_tensor(out_ap, in0, in1, op), reads, writes)

    def ts(e, out_ap, in0, s1, s2, op0, op1, reads, writes):
        if op1 is None:
            return P.op(e, lambda eng: eng.tensor_scalar(out_ap, in0, s1, None, op0), reads, writes)
        return P.op(e, lambda eng: eng.tensor_scalar(out_ap, in0, s1, s2, op0, op1), reads, writes)

    def stt(e, out_ap, in0, scalar, in1, op0, op1, reads, writes):
        return P.op(e, lambda eng: eng.scalar_tensor_tensor(out_ap, in0, scalar, in1, op0, op1), reads, writes)

    def cp(e, out_ap, in_ap, reads, writes):
        if e == "act":
            return P.op("act", lambda eng: eng.copy(out_ap, in_ap), reads, writes)
        return P.op(e, lambda eng: eng.tensor_copy(out_ap, in_ap), reads, writes)

    def ttr(jt, out_ap, in0, in1, acc, reads, writes):
        return act(out_ap, in0, AF.Square, list(reads) + [jt], list(writes) + [jt], accum=acc)

    def memset(e, ap, val, writes):
        return P.op(e, lambda eng: eng.memset(ap, val), (), writes)

    def rstd_ops(st, c0, n, reads_extra=()):
        act(st[:, c0 + 1:c0 + 2], st[:, c0:c0 + 1], AF.Ln, [st], [st], scale=1.0 / n, bias=EPS)
        act(st[:, c0 + 2:c0 + 3], st[:, c0 + 1:c0 + 2], AF.Exp, [st], [st], scale=-0.5)

    NPERS_F = 128 + 64 + 2 + 2 + 12 * 128 + 16
    pers_f = Arena(P, arena_f_t[:, NFA - 2048:NFA], 2048, "persf")
    NFA_USE = NFA - 2048
    AFa.size = NFA_USE
    pers_b = Arena(P, arena_b_t[:, NBA - 256:NBA], 256, "persb")
    ABa.size = NBA - 256
    ones_f = pers_f.alloc(128, "ones_f")
    lam_t = pers_f.alloc(16, "lam_t")
    gsub_t = pers_f.alloc(16, "gsub_t")
    sinkrow = pers_f.alloc(12 * 128, "sinkrow", "p (a b) -> p a b", b=128)
    ident_b = pers_b.alloc(128, "ident_b")
    sel_f = pers_f.alloc(64, "sel_f")

    def phase_weights():
        AFa.reset(); ABa.reset()
        stg = [AFa.alloc(2816, "wstg%d" % i) for i in range(2)]
        stb = [ABa.alloc(2816, "wstb%d" % i) for i in range(2)]
        cnt = [0]

        def one(src_aps, ncols, dst_ap):
            i = cnt[0] % 2
            cnt[0] += 1
            for (so, do, n, ap) in src_aps:
                dma("sp", stg[i][:, do:do + n], ap, [], [stg[i]])
            e = "dve" if (cnt[0] % 2 == 0) else "act"
            cp(e, stb[i][:, 0:ncols], stg[i][:, 0:ncols], [stg[i]], [stb[i]])
            dma("pool", dst_ap, stb[i][:, 0:ncols], [stb[i]], [])

        for l in range(2):
            for k in range(8):
                srcs = []
                do = 0
                for (a, b) in WIN_SEGS:
                    srcs.append((a, do, b - a, w_in[l, k * 128:(k + 1) * 128, a:b]))
                    do += b - a
                one(srcs, DIN, DAP(Wb_in, l * 128 * 8 * DIN + k * DIN, (8 * DIN, 128), (1, DIN)))
            for k in range(2):
                one([(0, 0, 576, mla_w_qb[l, k * 128:(k + 1) * 128, :])], 576,
                    DAP(Wb_qb, l * 128 * 1152 + k * 576, (1152, 128), (1, 576)))
            one([(0, 0, 768, mla_w_kvb[l, :, :])], 768, DAP(Wb_kvb, l * 128 * 768, (768, 128), (1, 768)))
            for k in range(8):
                one([(0, 0, D, w_out[l, k * 128:(k + 1) * 128, :])], D,
                    DAP(Wb_out, l * 128 * 8 * D + k * D, (8 * D, 128), (1, D)))

    def phase_masks():
        AFa.reset(); ABa.reset()
        memset("dve", ones_f[:, :], 1.0, [ones_f])
        idf = AFa.alloc(128, "idf")
        dma("sp", idf[:, :], ident_in[:, :], [], [idf])
        cp("dve", ident_b[:, :], idf[:, :], [idf], [ident_b])
        tt("dve", sel_f[:, 0:64], idf[:, 0:64], idf[:, 64:128], ALU.subtract, [idf], [sel_f])
        lp = AFa.alloc(256, "lp")
        dma("sp", lp[:, :], DAP(diff_lambda, 0, (0, 128), (1, 256)), [], [lp])
        junk = AFa.alloc(64, "junk")
        lst = AFa.alloc(16, "lst")
        for l in range(2):
            for pr in range(2):
                a = l * 128 + pr * 64
                tt("dve", junk[:, 0:32], lp[:, a:a + 32], lp[:, a + 32:a + 64], ALU.mult, [lp], [junk])
                P.op("dve", lambda eng, o_=lst[:, 2 * l + pr:2 * l + pr + 1]: eng.reduce_sum(o_, junk[:, 0:32], mybir.AxisListType.X),
                     [junk], [lst])
        lex = AFa.alloc(16, "lex")
        act(lex[:, 0:4], lst[:, 0:4], AF.Exp, [lst], [lex])
        for l in range(2):
            tt("dve", lam_t[:, l:l + 1], lex[:, 2 * l:2 * l + 1], lex[:, 2 * l + 1:2 * l + 2], ALU.subtract, [lex], [lam_t])
            ts("dve", lam_t[:, l:l + 1], lam_t[:, l:l + 1], LAM_INIT[l], None, ALU.add, None, [lam_t], [lam_t])
        gs0 = AFa.alloc(16, "gs0")
        for l in range(2):
            dma("sp", gs0[0:64, l:l + 1], DAP(diff_subln, 64 * l, (1, 64), (1, 1)), [], [gs0])
        for l in range(2):
            ts("dve", gsub_t[0:64, l:l + 1], gs0[0:64, l:l + 1], 1.0 - LAM_INIT[l], None, ALU.mult, None, [gs0], [gsub_t])
        rb = AFa.alloc(16, "rb")
        dma("sp", rb[0:32, 0:10], rel_bias[:, :], [], [rb])
        oh = AFa.alloc(768, "oh")
        dma("sp", oh[0:32, :], oh_in[:, :], [], [oh])
        al = AFa.alloc(768, "al")
        dma("sp", al[0:10, :], aallow_in[:, :], [], [al])
        b31 = AFa.alloc(16, "b31")
        dma("sp", b31[0:10, 0:1], DAP(rel_bias, 310, (1, 10), (1, 1)), [], [b31])
        ts("dve", b31[0:10, 1:2], b31[0:10, 0:1], -1.0, None, ALU.mult, None, [b31], [b31])
        mm(PS[0][0:10, 0:512], rb[0:32, 0:10], oh[0:32, 0:512], True, True, [rb, oh], [PS[0]])
        mm(PS[1][0:10, 0:256], rb[0:32, 0:10], oh[0:32, 512:768], True, True, [rb, oh], [PS[1]])
        gl = AFa.alloc(768, "gl")
        act(gl[0:10, 0:512], PS[0][0:10, 0:512], AF.Exp, [PS[0], b31], [gl], bias=b31[0:10, 1:2])
        act(gl[0:10, 512:768], PS[1][0:10, 0:256], AF.Exp, [PS[1], b31], [gl], bias=b31[0:10, 1:2])
        tt("dve", gl[0:10, :], gl[0:10, :], al[0:10, :], ALU.mult, [gl, al], [gl])
        o1 = dma("sp", Gd[:, :], gl[0:10, :], [gl], [])
        gdb = P.buf("gd")
        gdb.ws = [o1]
        zb = P.buf("zper")
        for hv in range(30):
            dma("pool", DAP(zper, hv * 129 * 256, (256, 129), (1, 256)), DAP(Gd, hv * 256, (0, 129), (1, 256)), [gdb], [zb])
        P.barrier()
        AFa.reset()
        Tall = AFa.alloc(30 * 128, "Tall", "p (a b) -> p a b", b=128)
        for hv in range(30):
            dma("sp" if hv % 2 == 0 else "pool", Tall[:, hv, :], DAP(zper, hv * 129 * 256, (255, 128), (1, 128)), [], [Tall])
        sk = AFa.alloc(32, "sk")
        dma("sp", sk[64:65, 0:12], DAP(swa_sinks, 0, (0, 1), (1, 12)), [], [sk])
        dma("sp", sk[64:65, 16:22], DAP(rel_bias, 314, (0, 1), (1, 6)), [], [sk])
        for l in range(2):
            tt("dve", sk[64:65, 6 * l:6 * l + 6], sk[64:65, 6 * l:6 * l + 6], sk[64:65, 16:22], ALU.subtract, [sk], [sk])
        act(sk[64:65, 0:12], sk[64:65, 0:12], AF.Exp, [sk], [sk])
        cp("dve", sinkrow[64:65, :, :], sk[64:65, 0:12].unsqueeze(2).broadcast_to([1, 12, 128]), [sk], [sinkrow])
        cdf = AFa.alloc(4 * NME, "cdf")
        dma("sp", cdf[:, :], cdiff_in[:, :], [], [cdf])
        NH = 21
        Mt = [AFa.alloc(NH * 128, "Mt%d" % i, "p (a b) -> p a b", b=128) for i in range(2)]

        for h in range(4):
            for (e0, e1) in ((0, NH), (NH, NME)):
                ne = e1 - e0

                def cb(kind):
                    return cdf[:, kind * NME + e0:kind * NME + e1].unsqueeze(2).broadcast_to([128, ne, 128])

                def tb(v):
                    return Tall[:, h * 3 + v, :].unsqueeze(1).broadcast_to([128, ne, 128])

                M, M2 = Mt[0][:, 0:ne, :], Mt[1][:, 0:ne, :]
                tt("dve", M, cb(1), tb(0), ALU.mult, [cdf, Tall], [Mt[0]])
                tt("pool", M2, cb(2), tb(1), ALU.mult, [cdf, Tall], [Mt[1]])
                tt("dve", M, M, M2, ALU.add, [Mt[0], Mt[1]], [Mt[0]])
                tt("pool", M2, cb(3), tb(2), ALU.mult, [cdf, Tall], [Mt[1]])
                tt("dve", M, M, M2, ALU.add, [Mt[0], Mt[1]], [Mt[0]])
                tt("dve", M, M, cb(0), ALU.add, [Mt[0], cdf], [Mt[0]])
                dma("sp", DAP(maskD, h * 128 * NME * 128 + e0 * 128, (NME * 128, 128), (1, ne * 128)),
                    M.rearrange("p a b -> p (a b)"), [Mt[0]], [])
        csf = AFa.alloc(16, "csf")
        dma("sp", csf[:, :], cswa_in[:, :], [], [csf])
        MS = AFa.alloc(2 * 4 * 384, "MSb", "p (g e r c) -> p g e r c", g=2, e=4, r=3)
        for g in range(2):
            h0 = 4 + 3 * g
            Tc = Tall[:, :, :].rearrange("p (h v) c -> p h v c", v=3)[:, h0:h0 + 3, 0, :]
            Tp = Tall[:, :, :].rearrange("p (h v) c -> p h v c", v=3)[:, h0:h0 + 3, 1, :]
            Tm = Tall[:, :, :].rearrange("p (h v) c -> p h v c", v=3)[:, h0:h0 + 3, 2, :]
            for e in range(3):
                ts("dve", MS[:, g, e, :, :], Tc, csf[:, 0 + e:0 + e + 1], None, ALU.mult, None, [Tall, csf], [MS])
                stt("dve", MS[:, g, e, :, :], Tp, csf[:, 4 + e:4 + e + 1], MS[:, g, e, :, :], ALU.mult, ALU.add,
                    [Tall, csf, MS], [MS])
            ts("dve", MS[:, g, 3, :, :], Tm, csf[:, 12 + 3:12 + 4], None, ALU.mult, None, [Tall, csf], [MS])
            ts("dve", MS[:, g, 3, :, :], MS[:, g, 3, :, :], csf[:, 8 + 3:8 + 4], None, ALU.add, None, [MS, csf], [MS])
        dma("sp", maskS[:, :], MS[:, :, :, :, :].rearrange("p g e r c -> p (g e r c)"), [MS], [])
        dma("sp", DAP(zper, 0, (6 * 128, 16), (128, 6), (1, 128)),
            Tall[:, :, :].rearrange("p (h v) c -> p h v c", v=3)[0:16, 4:10, 0, :], [Tall], [])

    def phase_init():
        AFa.reset(); ABa.reset()
        z = AFa.alloc(1024, "zinit")
        memset("dve", z[:, :], 0.0, [z])
        dma("sp", hres[0, 16:128, :], z[16:128, :], [z], [])
        dma("sp", hres[0, 0:16, :], meta_tokens[:, :], [], [])

    def hsrc(l, i):
        if l == 0 and i >= 1:
            return xin[i - 1, :, :]
        return hres[i, :, :]

    def phase_proj(l):
        AFa.reset(); ABa.reset()
        Win = ABa.alloc(8 * DIN, "Win", "p (k c) -> p k c", c=DIN)
        Wqb = ABa.alloc(2 * 576, "Wqb", "p (k c) -> p k c", c=576)
        Wkvb = ABa.alloc(768, "Wkvb")
        dma("sp", Win[:, :, :], DAP(Wb_in, l * 128 * 8 * DIN, (8 * DIN, 128), (DIN, 8), (1, DIN)), [], [Win])
        dma("sp", Wqb[:, :, :], DAP(Wb_qb, l * 128 * 1152, (1152, 128), (576, 2), (1, 576)), [], [Wqb])
        dma("sp", Wkvb[:, :], DAP(Wb_kvb, l * 128 * 768, (768, 128), (1, 768)), [], [Wkvb])
        Gat = AFa.alloc(1024, "Gat")
        Gq = AFa.alloc(256, "Gq")
        Gkv = AFa.alloc(128, "Gkv")
        dma("sp", Gat[:, :], DAP(attn_norm, l * D, (0, 128), (1, D)), [], [Gat])
        dma("sp", Gq[:, :], DAP(mla_q_norm, l * 256, (0, 128), (1, 256)), [], [Gq])
        dma("sp", Gkv[:, :], DAP(mla_kv_norm, l * 128, (0, 128), (1, 128)), [], [Gkv])
        cs = AFa.alloc(NLOC * 64, "cs", "p (i c) -> p i c", c=64)
        dma("sp", cs[:, :, :], cs_tab[:, :].rearrange("p (i c) -> p i c", c=64), [], [cs])
        junk = AFa.alloc(1024, "junk")
        Hh = [AFa.alloc(1024, "Hh%d" % i) for i in range(2)]
        st = [AFa.alloc(16, "st%d" % i) for i in range(2)]
        ra = [AFa.alloc(6 * 32, "ra%d" % i, "p (h c) -> p h c", c=32) for i in range(2)]
        rb_ = [AFa.alloc(6 * 32, "rb%d" % i, "p (h c) -> p h c", c=32) for i in range(2)]
        Craw = [AFa.alloc(416, "Craw%d" % i) for i in range(2)]
        Hn = [ABa.alloc(1024, "Hn%d" % i) for i in range(2)]
        HnT = [ABa.alloc(1024, "HnT%d" % i) for i in range(2)]
        Cn = [ABa.alloc(384, "Cn%d" % i) for i in range(2)]
        CT = [ABa.alloc(384, "CT%d" % i) for i in range(2)]
        Qf = [ABa.alloc(576, "Qf%d" % i, "p (h c) -> p h c", c=96) for i in range(2)]
        Kf = [ABa.alloc(576, "Kf%d" % i, "p (h c) -> p h c", c=96) for i in range(2)]
        Vf = [ABa.alloc(390, "Vf%d" % i, "p (h c) -> p h c", c=65) for i in range(2)]
        krot = [ABa.alloc(32, "krot%d" % i) for i in range(2)]
        QT = [ABa.alloc(768, "QT%d" % i) for i in range(2)]
        KT = [ABa.alloc(768, "KT%d" % i) for i in range(2)]
        Df = [ABa.alloc(512, "Df%d" % i) for i in range(2)]
        DT = [ABa.alloc(512, "DT%d" % i) for i in range(2)]
        Vdf = [ABa.alloc(260, "Vdf%d" % i, "p (h c) -> p h c", c=65) for i in range(2)]
        Sf = [ABa.alloc(512, "Sf%d" % i) for i in range(2)]
        ST = [ABa.alloc(512, "ST%d" % i) for i in range(2)]
        Vsf = [ABa.alloc(130, "Vsf%d" % i, "p (h c) -> p h c", c=65) for i in range(2)]
        for i in range(2):
            memset("pool", Vf[i][:, :, 64:65], 1.0, [Vf[i]])
            memset("pool", Vdf[i][:, :, 64:65], 1.0, [Vdf[i]])
            memset("pool", Vsf[i][:, :, 64:65], 1.0, [Vsf[i]])

        def rope(src3, dst3, H, i, db, src_t, dst_t):
            A = ra[db][:, 0:H, :]
            B = rb_[db][:, 0:H, :]
            cA = cs[:, i, 0:32].unsqueeze(1).broadcast_to([128, H, 32])
            sB0 = cs[:, i, 32:48].unsqueeze(1).broadcast_to([128, H, 16])
            sB1 = cs[:, i, 48:64].unsqueeze(1).broadcast_to([128, H, 16])
            tt("dve", A, src3, cA, ALU.mult, [src_t, cs], [ra[db]])
            tt("dve", B[:, :, 0:16], src3[:, :, 16:32], sB0, ALU.mult, [src_t, cs], [rb_[db]])
            tt("dve", B[:, :, 16:32], src3[:, :, 0:16], sB1, ALU.mult, [src_t, cs], [rb_[db]])
            tt("dve", dst3, A, B, ALU.add, [ra[db], rb_[db]], [dst_t])

        for i in range(nblk):
            db = i % 2
            H, S_, hn, hnT = Hh[db], st[db], Hn[db], HnT[db]
            dma("sp", H[:, :], hsrc(l, i), [], [H])
            ttr(junk, junk[:, :], H[:, :], H[:, :], S_[:, 0:1], [H], [S_])
            rstd_ops(S_, 0, 1024)
            stt("dve", hn[:, :], H[:, :], S_[:, 2:3], Gat[:, :], ALU.mult, ALU.mult, [H, S_, Gat], [hn])
            if cut <= 1:
                continue
            for k in range(8):
                tr(psb(6)[:, k * 128:(k + 1) * 128], hn[:, k * 128:(k + 1) * 128], ident_b[:, :], [hn, ident_b], [PS[6]])
            cp("act", hnT[:, :], psb(6)[:, :], [PS[6]], [hnT])
            if cut <= 2:
                continue
            for g, (c0, c1) in enumerate(PGRP):
                for k in range(8):
                    mm(PS[g][:, 0:c1 - c0], hnT[:, k * 128:(k + 1) * 128], Win[:, k, c0:c1], k == 0, k == 7,
                       [hnT, Win], [PS[g]])
            if cut <= 3:
                continue
            cp("act", Df[db][:, :], PS[1][:, 0:512], [PS[1]], [Df[db]])
            cp("act", Sf[db][:, 0:384].rearrange("p (r g c) -> p g r c", r=3, g=2),
               PS[2][:, 0:384].rearrange("p (g r c) -> p g r c", g=2, r=3), [PS[2]], [Sf[db]])
            cp("act", Sf[db][:, 384:512], PS[2][:, 384:512], [PS[2]], [Sf[db]])
            cp("act", Vdf[db][:, :, 0:64], PS[3][:, 0:256].rearrange("p (h c) -> p h c", c=64), [PS[3]], [Vdf[db]])
            cp("act", Vsf[db][:, :, 0:64], PS[3][:, 256:384].rearrange("p (h c) -> p h c", c=64), [PS[3]], [Vsf[db]])
            if cut <= 4:
                continue
            cr = Craw[db]
            cp("act", cr[:, 0:416], PS[0][:, 0:416], [PS[0]], [cr])
            ttr(junk, junk[:, 0:256], cr[:, 0:256], cr[:, 0:256], S_[:, 4:5], [cr], [S_])
            ttr(junk, junk[:, 0:128], cr[:, 256:384], cr[:, 256:384], S_[:, 8:9], [cr], [S_])
            rstd_ops(S_, 4, 256)
            rstd_ops(S_, 8, 128)
            stt("dve", Cn[db][:, 0:256], cr[:, 0:256], S_[:, 6:7], Gq[:, :], ALU.mult, ALU.mult, [cr, S_, Gq], [Cn[db]])
            stt("dve", Cn[db][:, 256:384], cr[:, 256:384], S_[:, 10:11], Gkv[:, :], ALU.mult, ALU.mult,
                [cr, S_, Gkv], [Cn[db]])
            rope(cr[:, 384:416].unsqueeze(1), krot[db][:, :].unsqueeze(1), 1, i, db, cr, krot[db])
            if cut <= 5:
                continue
            for k in range(3):
                tr(psb(7)[:, k * 128:(k + 1) * 128], Cn[db][:, k * 128:(k + 1) * 128], ident_b[:, :], [Cn[db], ident_b], [PS[7]])
            cp("act", CT[db][:, :], psb(7)[:, 0:384], [PS[7]], [CT[db]])
            if cut <= 6:
                continue
            for half in range(2):
                for k in range(2):
                    mm(PS[4 + half][:, 0:288], CT[db][:, k * 128:(k + 1) * 128], Wqb[:, k, half * 288:(half + 1) * 288],
                       k == 0, k == 1, [CT[db], Wqb], [PS[4 + half]])
            for half in range(2):
                mm(PS[half][:, 0:384], CT[db][:, 256:384], Wkvb[:, half * 384:(half + 1) * 384], True, True,
                   [CT[db], Wkvb], [PS[half]])
            if cut <= 7:
                continue
            for half in range(2):
                q3 = PS[4 + half][:, 0:288].rearrange("p (h c) -> p h c", c=96)
                cp("act", Qf[db][:, 3 * half:3 * half + 3, 0:64], q3[:, :, 0:64], [PS[4 + half]], [Qf[db]])
                if cut <= 7.2:
                    continue
                rope(q3[:, :, 64:96], Qf[db][:, 3 * half:3 * half + 3, 64:96], 3, i, db, PS[4 + half], Qf[db])
                if cut <= 7.4:
                    continue
                kv3 = PS[half][:, 0:384].rearrange("p (h c) -> p h c", c=128)
                cp("act", Kf[db][:, 3 * half:3 * half + 3, 0:64], kv3[:, :, 0:64], [PS[half]], [Kf[db]])
                cp("dve", Vf[db][:, 3 * half:3 * half + 3, 0:64], kv3[:, :, 64:128], [PS[half]], [Vf[db]])
            if cut <= 7.6:
                continue
            cp("pool", Kf[db][:, :, 64:96], krot[db][:, :].unsqueeze(1).broadcast_to([128, 6, 32]), [krot[db]], [Kf[db]])
            if cut <= 8:
                continue
            for h in range(6):
                tr(psb(6)[0:96, h * 128:(h + 1) * 128], Qf[db][:, h, :], ident_b[:, :], [Qf[db], ident_b], [PS[6]])
            cp("act", QT[db][0:96, :], psb(6)[0:96, 0:768], [PS[6]], [QT[db]])
            for h in range(6):
                tr(psb(7)[0:96, h * 128:(h + 1) * 128], Kf[db][:, h, :], ident_b[:, :], [Kf[db], ident_b], [PS[7]])
            cp("dve", KT[db][0:96, :], psb(7)[0:96, 0:768], [PS[7]], [KT[db]])
            if cut <= 9:
                continue
            dma("pool", DAP(QTm, i * 128, (T, 96), (96 * T, 6), (1, 128)),
                QT[db][0:96, :].rearrange("p (h c) -> p h c", c=128), [QT[db]], [])
            dma("pool", DAP(kvown, OFF_M + i * 128, (T, 96), (RM * T, 6), (1, 128)),
                KT[db][0:96, :].rearrange("p (h c) -> p h c", c=128), [KT[db]], [])
            dma("pool", DAP(kvown, OFF_M + 96 * T + i * 65, (VW, 128), (RM * T, 6), (1, 65)), Vf[db][:, :, :], [Vf[db]], [])
            if cut <= 10:
                continue
            for m in range(4):
                tr(psb(6)[:, m * 128:(m + 1) * 128], Df[db][:, m * 128:(m + 1) * 128], ident_b[:, :], [Df[db], ident_b], [PS[6]])
            cp("act", DT[db][:, :], psb(6)[:, 0:512], [PS[6]], [DT[db]])
            dma("pool", DAP(QTd, i * 128, (T, 128), (128 * T, 2), (1, 128)),
                DT[db][:, 0:256].rearrange("p (h c) -> p h c", c=128), [DT[db]], [])
            for ph in range(2):
                dma("pool", DAP(kvown, OFF_D + ph * RD * T + i * 128, (T, 64), (2 * RD * T, 2), (1, 128)),
                    DT[db][64 * ph:64 * ph + 64, 256:512].rearrange("p (h c) -> p h c", c=128), [DT[db]], [])
            dma("pool", DAP(kvown, OFF_D + 64 * T + i * 65, (VW, 128), (RD * T, 4), (1, 65)), Vdf[db][:, :, :], [Vdf[db]], [])
            if cut <= 11:
                continue
            for r in range(4):
                tr(psb(7)[:, r * 128:(r + 1) * 128], Sf[db][:, r * 128:(r + 1) * 128], ident_b[:, :], [Sf[db], ident_b], [PS[7]])
            cp("dve", ST[db][:, :], psb(7)[:, 0:512], [PS[7]], [ST[db]])
            dma("pool", DAP(QTs, i * 384, (3 * T, 128), (1, 384)), ST[db][:, 0:384], [ST[db]], [])
            dma("pool", DAP(kvown, OFF_S0 + i * 128, (T, 128), (1, 128)), ST[db][:, 384:512], [ST[db]], [])
            dma("pool", DAP(kvown, OFF_S1 + i * 130, (2 * VW, 128), (1, 130)),
                Vsf[db][:, :, :].rearrange("p h c -> p (h c)"), [Vsf[db]], [])

    def phase_gather(l):
        P.barrier()
        for k, (nm, r0, nr) in enumerate(PIECES):
            o = Op("pool", lambda eng, nm=nm, r0=r0, nr=nr: eng.collective_compute(
                "AllGather", ALU.bypass, replica_groups=[[2 * i, 2 * i + 1] for i in range(ncores // 2)],
                ins=[kvown[r0:r0 + nr, :].opt()], outs=[kvall[nm].ap().opt()]), False)
            o.signal = True
            o.cc = ccsems[l * len(PIECES) + k]
            P.ops["pool"].append(o)
            pbuf[nm].ws = [o]
            pbuf[nm].r = []

    def kloc(j):
        if j == 0:
            return 0, 0
        return (j - 1) % 2, 1 + (j - 1) // 2

    def ffn_cast_jobs(l):
        jobs = []
        for (wsrc, wdst) in ((w_gate, Wb_g), (w_up, Wb_u)):
            for k in range(8):
                jobs.append((wsrc[l, k * 128:(k + 1) * 128, :], DFF,
                             DAP(wdst, l * NF * 128 * 1024 + k * 128, (1024, 128), (128 * 1024, NF), (1, 128)), True))
        for f in range(NF):
            jobs.append((w_down[l, f * 128:(f + 1) * 128, :], D, DAP(Wb_d, l * 128 * NF * D + f * D, (NF * D, 128), (1, D)), False))
        return jobs

    def background_setup(l):
        stg = [AFa.alloc(2816, "bgstg%d" % i) for i in range(2)]
        stb = [ABa.alloc(2816, "bgstb%d" % i) for i in range(2)]
        return {"jobs": ffn_cast_jobs(l), "stg": stg, "stb": stb, "n": 0, "tick": 0}

    def bg_one(bg):
        if not bg["jobs"]:
            return
        src, ncols, dst, blocked = bg["jobs"].pop(0)
        i = bg["n"] % 2
        bg["n"] += 1
        stg, stb = bg["stg"][i], bg["stb"][i]
        dma("act", stg[:, 0:ncols], src, [], [stg])
        cp("pool", stb[:, 0:ncols], stg[:, 0:ncols], [stg], [stb])
        if blocked:
            dma("pool", dst, stb[:, 0:ncols].rearrange("p (f c) -> p f c", c=128), [stb], [])
        else:
            dma("pool", dst, stb[:, 0:ncols], [stb], [])

    def bg_step(bg):
        bg["tick"] += 1
        if bg["tick"] % 12 == 0:
            bg_one(bg)

    def bg_finish(bg):
        while bg["jobs"]:
            bg_one(bg)

    def attention_core(units, lookahead=2, post_delay=1):
        n = len(units)
        for t in range(n + lookahead + post_delay):
            if t < n:
                units[t]["s"]()
                units[t]["e"]()
            if 0 <= t - lookahead < n:
                units[t - lookahead]["o"]()
            tp_ = t - lookahead - post_delay
            if 0 <= tp_ < n and units[tp_].get("post"):
                units[tp_]["post"]()

    def groups():
        gl = [(-1, 0, 128, [0])]
        for g in range(8):
            gl.append((g, (1 + 4 * g) * 128, 512, list(range(0, 8 * g + 9))))
        return gl

    def pair_view(t_ap, nk, N):
        if N == 512:
            return t_ap[0:nk, 0:1024]
        return t_ap[0:nk, 0:1024].rearrange("p (c n) -> p c n", c=2)[:, :, 0:N]

    def phase_attn_mla(l):
        AFa.reset(); ABa.reset()
        mA = AFa.alloc(9 * 512, "mA", "p (s c) -> p s c", c=512)
        dma("sp", mA[:, :, :], mA_in[:, :].rearrange("p (s c) -> p s c", c=512), [], [mA])
        KTt = [ABa.alloc(2 * T, "KTt%d" % i) for i in range(2)]
        Vt = [ABa.alloc(2 * VW, "Vt%d" % i) for i in range(2)]
        QTt = [ABa.alloc(T, "QTt%d" % i) for i in range(2)]
        Pt = [ABa.alloc(1024, "Pt%d" % i) for i in range(3)]
        yo = [ABa.alloc(512, "yo%d" % i) for i in range(2)]
        rd = [AFa.alloc(512, "rd%d" % i) for i in range(2)]
        bcs = [AFa.alloc(512, "bcs%d" % i) for i in range(2)]
        bg = background_setup(l)
        cnt = {"s": 0, "p": 0, "o": 0, "y": 0}
        for h in range(6):
            hb = h % 2
            for r in range(NR):
                dma("sp", KTt[hb][0:96, r * T:(r + 1) * T], DAP(kvall["m%d" % h], r * RM * T, (T, 96), (1, T)),
                    [pbuf["m%d" % h]], [KTt[hb]])
                dma("sp", Vt[hb][:, r * VW:(r + 1) * VW], DAP(kvall["m%d" % h], r * RM * T + 96 * T, (VW, 128), (1, VW)),
                    [pbuf["m%d" % h]], [Vt[hb]])
            dma("sp", QTt[hb][0:96, :], DAP(QTm, h * 96 * T, (T, 96), (1, T)), [], [QTt[hb]])
            units = []
            for (g, q0, N, js) in groups():
                ob = cnt["o"] % 2
                cnt["o"] += 1
                Ob = PS[4 + ob]
                subs = [[js[0]]] + [js[k:k + 2] for k in range(1, len(js), 2)]
                for jl in subs:
                    sp_ = cnt["s"] % 2
                    cnt["s"] += 1
                    pt = Pt[cnt["p"] % 3]
                    cnt["p"] += 1
                    nk = 16 if jl[0] == 0 else 128
                    info = []
                    for j in jl:
                        rk, lc = kloc(j)
                        info.append((rk * T + lc * 128, rk * VW + lc * 65))
                    if g == -1:
                        mk = mA[0:nk, 8, 0:N]
                    elif jl[0] >= 8 * g + 1:
                        s0 = jl[0] - (8 * g + 1)
                        mk = mA[0:nk, s0:s0 + len(jl), :].rearrange("p s c -> p (s c)")
                    else:
                        mk = None
                    banks = [PS[2 * sp_ + bi] for bi in range(len(jl))]
                    u = {}

                    def s_(banks=banks, info=info, nk=nk, q0=q0, N=N, hb=hb):
                        for bi, (kc, vc) in enumerate(info):
                            mm(banks[bi][0:nk, 0:N], KTt[hb][0:96, kc:kc + nk], QTt[hb][0:96, q0:q0 + N], True, True,
                               [KTt[hb], QTt[hb]], [banks[bi]])

                    def e_(banks=banks, sp_=sp_, pt=pt, nk=nk, N=N, mk=mk, nb=len(jl)):
                        if nb == 2:
                            src = pair_view(psum_t[:, 2 * sp_ * 512:(2 * sp_ + 2) * 512], nk, N)
                            dst = pair_view(pt[:, :], nk, N)
                        else:
                            src = banks[0][0:nk, 0:N]
                            dst = pt[0:nk, 0:N]
                        act(dst, src, AF.Exp, banks, [pt], scale=SC_MLA)
                        if mk is not None:
                            tt("dve", dst, dst, mk, ALU.mult, [pt, mA], [pt])

                    def o_(Ob=Ob, pt=pt, nk=nk, info=info, N=N, hb=hb, jl=jl, js=js):
                        for bi, (kc, vc) in enumerate(info):
                            mm(Ob[0:65, 0:N], Vt[hb][0:nk, vc:vc + 65], pt[0:nk, bi * 512:bi * 512 + N],
                               jl[bi] == js[0], jl[bi] == js[-1], [Vt[hb], pt], [Ob])
                        bg_step(bg)

                    u["s"], u["e"], u["o"] = s_, e_, o_
                    if jl[-1] == js[-1]:
                        yb = cnt["y"] % 2
                        cnt["y"] += 1

                        def post(Ob=Ob, N=N, q0=q0, yb=yb, h=h):
                            act(rd[yb][64:65, 0:N], Ob[64:65, 0:N], AF.Ln, [Ob], [rd[yb]])
                            act(rd[yb][64:65, 0:N], rd[yb][64:65, 0:N], AF.Exp, [rd[yb]], [rd[yb]], scale=-1.0)
                            mm(PS[6][0:64, 0:N], ones_f[64:65, 0:64], rd[yb][64:65, 0:N], True, True, [ones_f, rd[yb]], [PS[6]])
                            cp("dve", bcs[yb][0:64, 0:N], PS[6][0:64, 0:N], [PS[6]], [bcs[yb]])
                            tt("dve", yo[yb][0:64, 0:N], Ob[0:64, 0:N], bcs[yb][0:64, 0:N], ALU.mult, [Ob, bcs[yb]], [yo[yb]])
                            dma("pool", DAP(yT, h * 64 * T + q0, (T, 64), (1, N)), yo[yb][0:64, 0:N], [yo[yb]], [])

                        u["post"] = post
                    units.append(u)
            attention_core(units, lookahead=1)
        bg_finish(bg)

    def phase_attn_diff(l):
        AFa.reset(); ABa.reset()
        KD = [ABa.alloc(2 * T, "KD%d" % i) for i in range(2)]
        VD = [ABa.alloc(2 * VW, "VD%d" % i) for i in range(2)]
        QD = [ABa.alloc(T, "QD%d" % i) for i in range(2)]
        Pt = [ABa.alloc(1024, "Pt%d" % i) for i in range(3)]
        yo = [ABa.alloc(512, "yo%d" % i) for i in range(2)]
        MD = AFa.alloc(NME * 128, "MD")
        Ef = [AFa.alloc(1024, "Ef%d" % i) for i in range(2)]
        acc = [AFa.alloc(1024, "acc%d" % i) for i in range(2)]
        r0 = AFa.alloc(512, "r0")
        r1 = AFa.alloc(512, "r1")
        X = AFa.alloc(512, "X")
        d0 = AFa.alloc(512, "d0")
        d1 = AFa.alloc(512, "d1")
        rs = AFa.alloc(512, "rs")
        cnt = {"s": 0, "p": 0, "e": 0, "o": 0}
        for h in range(4):
            hb = h % 2
            for r in range(NR):
                dma("sp", KD[hb][0:64, r * T:(r + 1) * T],
                    DAP(kvall["d%d" % h], r * RD * T, (T, 64), (1, T)), [pbuf["d%d" % h]], [KD[hb]])
                dma("sp", VD[hb][:, r * VW:(r + 1) * VW], DAP(kvall["d%d" % h], r * RD * T + 64 * T, (VW, 128), (1, VW)),
                    [pbuf["d%d" % h]], [VD[hb]])
            dma("sp", QD[hb][0:64, :], DAP(QTd, 64 * h * T, (T, 64), (1, T)), [], [QD[hb]])
            dma("sp", MD[:, :], DAP(maskD, h * 128 * NME * 128, (NME * 128, 128), (1, NME * 128)), [], [MD])
            units = []
            for (g, q0, N, js) in groups():
                ob = cnt["o"] % 2
                cnt["o"] += 1
                Ob = PS[4 + ob]
                ac = acc[ob]
                for j in js:
                    rk, lc = kloc(j)
                    nk = 16 if j == 0 else 128
                    kc = rk * T + lc * 128
                    vc = rk * VW + lc * 65
                    sp_ = cnt["s"] % 2
                    cnt["s"] += 1
                    pt = Pt[cnt["p"] % 3]
                    cnt["p"] += 1
                    if g == -1:
                        mk = MD[0:nk, 40 * 128:40 * 128 + N]
                    elif g == 0 and j == 0:
                        mk = MD[0:nk, 36 * 128:36 * 128 + N]
                    elif j >= 8 * g and j > 0:
                        s0 = j - 8 * g
                        mk = MD[0:nk, s0 * 512:s0 * 512 + N]
                    else:
                        mk = None
                    banks = [PS[2 * sp_], PS[2 * sp_ + 1]]
                    first = (j == js[0])
                    last = (j == js[-1])
                    u = {}

                    def s_(banks=banks, nk=nk, kc=kc, q0=q0, N=N, hb=hb):
                        for c in range(2):
                            mm(banks[c][0:nk, 0:N], KD[hb][32 * c:32 * c + 32, kc:kc + nk], QD[hb][32 * c:32 * c + 32, q0:q0 + N],
                               True, True, [KD[hb], QD[hb]], [banks[c]])

                    def e_(banks=banks, sp_=sp_, pt=pt, nk=nk, N=N, mk=mk, ac=ac, first=first):
                        src = pair_view(psum_t[:, 2 * sp_ * 512:(2 * sp_ + 2) * 512], nk, N)
                        dst = pair_view(pt[:, :], nk, N)
                        if mk is None:
                            act(dst, src, AF.Exp, banks, [pt], scale=SC_DIFF)
                        else:
                            ef = Ef[cnt["e"] % 2]
                            cnt["e"] += 1
                            efv = pair_view(ef[:, :], nk, N)
                            act(efv, src, AF.Exp, banks, [ef], scale=SC_DIFF)
                            efv3 = ef[0:nk, 0:1024].rearrange("p (c n) -> p c n", c=2)[:, :, 0:N]
                            dst3 = pt[0:nk, 0:1024].rearrange("p (c n) -> p c n", c=2)[:, :, 0:N]
                            tt("dve", dst3, efv3, mk.unsqueeze(1).broadcast_to([nk, 2, N]), ALU.mult, [ef, MD], [pt])
                        if first:
                            memset("pool", ac[:, :], 0.0, [ac])
                        av = pair_view(ac[:, :], nk, N)
                        tt("pool", av, av, dst, ALU.add, [ac, pt], [ac])

                    def o_(Ob=Ob, pt=pt, nk=nk, vc=vc, N=N, hb=hb, first=first, last=last):
                        for c in range(2):
                            mm(Ob[64 * c:64 * c + 64, 0:N], VD[hb][0:nk, vc:vc + 64], pt[0:nk, c * 512:c * 512 + N], first, last,
                               [VD[hb], pt], [Ob], tp=(0, 64 * c))

                    u["s"], u["e"], u["o"] = s_, e_, o_
                    if last:
                        def post(N=N, q0=q0, h=h, Ob=Ob, ac=ac):
                            mm(PS[6][:, 0:N], ones_f[:, 0:128], ac[:, 0:N], True, True, [ones_f, ac], [PS[6]])
                            mm(PS[7][:, 0:N], ones_f[:, 0:128], ac[:, 512:512 + N], True, True, [ones_f, ac], [PS[7]])
                            act(r0[0:64, 0:N], PS[6][0:64, 0:N], AF.Ln, [PS[6]], [r0])
                            act(r0[0:64, 0:N], r0[0:64, 0:N], AF.Exp, [r0], [r0], scale=-1.0)
                            act(r1[64:128, 0:N], PS[7][64:128, 0:N], AF.Ln, [PS[7]], [r1])
                            act(r1[64:128, 0:N], r1[64:128, 0:N], AF.Exp, [r1], [r1], scale=-1.0)
                            tt("dve", X[0:64, 0:N], Ob[0:64, 0:N], r0[0:64, 0:N], ALU.mult, [Ob, r0], [X])
                            stt("dve", X[64:128, 0:N], Ob[64:128, 0:N], lam_t[64:128, l:l + 1], r1[64:128, 0:N], ALU.mult, ALU.mult,
                                [Ob, lam_t, r1], [X])
                            mm(PS[6][0:64, 0:N], sel_f[:, 0:64], X[:, 0:N], True, True, [sel_f, X], [PS[6]])
                            cp("dve", d0[0:64, 0:N], PS[6][0:64, 0:N], [PS[6]], [d0])
                            tt("pool", d1[0:64, 0:N], d0[0:64, 0:N], d0[0:64, 0:N], ALU.mult, [d0], [d1])
                            mm(PS[7][0:64, 0:N], ones_f[0:64, 0:64], d1[0:64, 0:N], True, True, [ones_f, d1], [PS[7]])
                            act(rs[0:64, 0:N], PS[7][0:64, 0:N], AF.Ln, [PS[7]], [rs], scale=1.0 / 64, bias=EPS)
                            act(rs[0:64, 0:N], rs[0:64, 0:N], AF.Exp, [rs], [rs], scale=-0.5)
                            yb = yo[(q0 // 128) % 2]
                            stt("dve", yb[0:64, 0:N], d0[0:64, 0:N], gsub_t[0:64, l:l + 1], rs[0:64, 0:N], ALU.mult, ALU.mult,
                                [d0, gsub_t, rs], [yb])
                            dma("pool", DAP(yT, (384 + h * 64) * T + q0, (T, 64), (1, N)), yb[0:64, 0:N], [yb], [])

                        u["post"] = post
                    units.append(u)
            attention_core(units, lookahead=1, post_delay=1)

    def phase_attn_swa(l):
        AFa.reset(); ABa.reset()
        KS = ABa.alloc(2 * T, "KS")
        VS = ABa.alloc(2 * 2 * VW, "VS")
        QS = ABa.alloc(3 * T, "QS")
        Pt = [ABa.alloc(384, "Pt%d" % i) for i in range(4)]
        yo = [ABa.alloc(384, "yo%d" % i) for i in range(2)]
        MS = AFa.alloc(2 * 4 * 384, "MS", "p (g e c) -> p g e c", g=2, e=4)
        MSq = AFa.alloc(6 * 128, "MSq")
        Ef = [AFa.alloc(384, "Ef%d" % i) for i in range(2)]
        rd = [AFa.alloc(384, "rd%d" % i) for i in range(2)]
        bcs = [AFa.alloc(384, "bcs%d" % i) for i in range(2)]
        for r in range(NR):
            dma("sp", KS[:, r * T:(r + 1) * T], DAP(kvall["s0"], r * RS0 * T, (T, 128), (1, T)), [pbuf["s0"]], [KS])
            dma("sp", VS[:, r * 2 * VW:(r + 1) * 2 * VW], DAP(kvall["s1"], r * RS1 * T, (2 * VW, 128), (1, 2 * VW)), [pbuf["s1"]], [VS])
        dma("sp", QS[:, :], DAP(QTs, 0, (3 * T, 128), (1, 3 * T)), [], [QS])
        dma("sp", MS[:, :, :, :], maskS[:, :].rearrange("p (g e c) -> p g e c", g=2, e=4), [], [MS])
        dma("sp", MSq[0:16, :], DAP(zper, 0, (768, 16), (1, 768)), [], [MSq])
        cnt = {"s": 0, "p": 0, "e": 0, "o": 0, "y": 0}
        units = []
        for i in range(NLOC):
            for g in range(2):
                kts = []
                if i == 0:
                    kts.append((0, MSq[0:16, 3 * g * 128:(3 * g + 3) * 128]))
                else:
                    kts.append((0, MS[0:16, g, 3, :] if i == 1 else None))
                    for s in range(3):
                        j = 2 * (i - 1) + s
                        if j == 0 or j > 64:
                            continue
                        kts.append((j, MS[:, g, s, :]))
                ob = cnt["o"] % 2
                cnt["o"] += 1
                Ob = PS[3 + ob]
                for ti, (j, mk) in enumerate(kts):
                    rk, lc = kloc(j)
                    nk = 16 if j == 0 else 128
                    kc = rk * T + lc * 128
                    vc = rk * 2 * VW + lc * 130 + g * 65
                    sb = PS[cnt["s"] % 3]
                    cnt["s"] += 1
                    pt = Pt[cnt["p"] % 4]
                    cnt["p"] += 1
                    u = {}

                    def s_(sb=sb, nk=nk, kc=kc, i=i, g=g):
                        mm(sb[0:nk, 0:384], KS[64 * g:64 * g + 64, kc:kc + nk], QS[64 * g:64 * g + 64, i * 384:(i + 1) * 384],
                           True, True, [KS, QS], [sb])

                    def e_(sb=sb, pt=pt, nk=nk, mk=mk):
                        if mk is None:
                            act(pt[0:nk, 0:384], sb[0:nk, 0:384], AF.Exp, [sb], [pt], scale=SC_SWA)
                        else:
                            ef = Ef[cnt["e"] % 2]
                            cnt["e"] += 1
                            act(ef[0:nk, 0:384], sb[0:nk, 0:384], AF.Exp, [sb], [ef], scale=SC_SWA)
                            tt("dve", pt[0:nk, 0:384], ef[0:nk, 0:384], mk, ALU.mult, [ef, MS, MSq], [pt])

                    def o_(Ob=Ob, pt=pt, nk=nk, vc=vc, first=(ti == 0), last=(ti == len(kts) - 1)):
                        mm(Ob[0:65, 0:384], VS[0:nk, vc:vc + 65], pt[0:nk, 0:384], first, last, [VS, pt], [Ob])

                    u["s"], u["e"], u["o"] = s_, e_, o_
                    if ti == len(kts) - 1:
                        yb = cnt["y"] % 2
                        cnt["y"] += 1

                        def post(Ob=Ob, i=i, g=g, yb=yb):
                            sr = sinkrow[64:65, l * 6 + 3 * g:l * 6 + 3 * g + 3, :].rearrange("p a b -> p (a b)")
                            tt("dve", rd[yb][64:65, 0:384], Ob[64:65, 0:384], sr, ALU.add, [Ob, sinkrow], [rd[yb]])
                            P.op("dve", lambda eng: eng.reciprocal(rd[yb][64:65, 0:384], rd[yb][64:65, 0:384]), [rd[yb]], [rd[yb]])
                            mm(PS[5][0:64, 0:384], ones_f[64:65, 0:64], rd[yb][64:65, 0:384], True, True, [ones_f, rd[yb]], [PS[5]])
                            cp("dve", bcs[yb][0:64, 0:384], PS[5][0:64, 0:384], [PS[5]], [bcs[yb]])
                            tt("dve", yo[yb][0:64, 0:384], Ob[0:64, 0:384], bcs[yb][0:64, 0:384], ALU.mult, [Ob, bcs[yb]], [yo[yb]])
                            dma("pool", DAP(yT, (640 + 3 * g * 64) * T + i * 128, (T, 64), (64 * T, 3), (1, 128)),
                                yo[yb][0:64, 0:384].rearrange("p (r c) -> p r c", c=128), [yo[yb]], [])

                        u["post"] = post
                    units.append(u)
        attention_core(units)

    def phase_ffn(l, last):
        AFa.reset(); ABa.reset()
        Wd = ABa.alloc(NF * D, "Wd", "p (f c) -> p f c", c=D)
        Wo = ABa.alloc(8 * D, "Wo", "p (k c) -> p k c", c=D)
        for f0 in range(0, NF, 6):
            f1 = min(NF, f0 + 6)
            dma("sp", Wd[:, f0:f1, :], DAP(Wb_d, l * 128 * NF * D + f0 * D, (NF * D, 128), (D, f1 - f0), (1, D)), [], [Wd])
        dma("sp", Wo[:, :, :], DAP(Wb_out, l * 128 * 8 * D, (8 * D, 128), (D, 8), (1, D)), [], [Wo])
        AT = ABa.alloc(NF * 512, "AT", "p (f c) -> p f c", c=512)
        HnT = ABa.alloc(8 * 512, "HnT", "p (k c) -> p k c", c=512)
        WGU = [ABa.alloc(2 * 1024, "WGU%d" % i, "p (u k c) -> p u k c", u=2, k=8) for i in range(2)]
        YT = [ABa.alloc(1024, "YT%d" % i, "p (k c) -> p k c", c=128) for i in range(2)]
        Hn = [ABa.alloc(1024, "Hn%d" % i) for i in range(2)]
        HT = AFa.alloc(4 * 1024, "HT", "p (b c) -> p b c", c=1024)
        Gff = AFa.alloc(1024, "Gff")
        dma("sp", Gff[:, :], DAP(ffn_norm, l * D, (0, 128), (1, D)), [], [Gff])
        if last:
            Gfin = AFa.alloc(1024, "Gfin")
            dma("sp", Gfin[:, :], DAP(final_norm, 0, (0, 128), (1, D)), [], [Gfin])
            OUTT = [AFa.alloc(1024, "OUTT%d" % i) for i in range(2)]
        SG = [AFa.alloc(512, "SG%d" % i) for i in range(2)]
        junk = AFa.alloc(1024, "junk")
        st = [AFa.alloc(16, "st%d" % i) for i in range(2)]
        tiles = [[0]] + [list(range(1 + 4 * t, 5 + 4 * t)) for t in range(8)]
        cnt = {"y": 0, "w": 0, "g": 0, "o": 0}
        for bl in tiles:
            nb = len(bl)
            NT = nb * 128
            for bi, i in enumerate(bl):
                yb = cnt["y"] % 2
                cnt["y"] += 1
                dma("sp", HT[:, bi, :], hsrc(l, i), [], [HT])
                dma("sp", YT[yb][:, :, :], DAP(yT, i * 128, (T, 128), (128 * T, 8), (1, 128)), [], [YT[yb]])
                for half in range(2):
                    for k in range(8):
                        mm(PS[half][:, 0:512], YT[yb][:, k, :], Wo[:, k, half * 512:(half + 1) * 512], k == 0, k == 7,
                           [YT[yb], Wo], [PS[half]])
                for half in range(2):
                    tt("dve", HT[:, bi, half * 512:(half + 1) * 512], HT[:, bi, half * 512:(half + 1) * 512], PS[half][:, 0:512],
                       ALU.add, [HT, PS[half]], [HT])
                S_ = st[yb]
                ttr(junk, junk[:, :], HT[:, bi, :], HT[:, bi, :], S_[:, 0:1], [HT], [S_])
                rstd_ops(S_, 0, 1024)
                stt("dve", Hn[yb][:, :], HT[:, bi, :], S_[:, 2:3], Gff[:, :], ALU.mult, ALU.mult, [HT, S_, Gff], [Hn[yb]])
                for k in range(8):
                    tr(psb(2)[:, k * 128:(k + 1) * 128], Hn[yb][:, k * 128:(k + 1) * 128], ident_b[:, :], [Hn[yb], ident_b], [PS[2]])
                cp("act", HnT[:, :, bi * 128:(bi + 1) * 128], psb(2)[:, :].rearrange("p (k c) -> p k c", c=128), [PS[2]], [HnT])
            for f in range(NF):
                wb = cnt["w"] % 2
                cnt["w"] += 1
                dma("sp", WGU[wb][:, 0, :, :], DAP(Wb_g, (l * NF + f) * 128 * 1024, (1024, 128), (128, 8), (1, 128)), [], [WGU[wb]])
                dma("sp", WGU[wb][:, 1, :, :], DAP(Wb_u, (l * NF + f) * 128 * 1024, (1024, 128), (128, 8), (1, 128)), [], [WGU[wb]])
                gb = cnt["g"] % 2
                cnt["g"] += 1
                PG, PU = PS[3 + 2 * gb], PS[4 + 2 * gb]
                for k in range(8):
                    mm(PG[:, 0:NT], WGU[wb][:, 0, k, :], HnT[:, k, 0:NT], k == 0, k == 7, [WGU[wb], HnT], [PG])
                for k in range(8):
                    mm(PU[:, 0:NT], WGU[wb][:, 1, k, :], HnT[:, k, 0:NT], k == 0, k == 7, [WGU[wb], HnT], [PU])
                act(SG[gb][:, 0:NT], PG[:, 0:NT], AF.Silu, [PG], [SG[gb]])
                tt("dve", AT[:, f, 0:NT], SG[gb][:, 0:NT], PU[:, 0:NT], ALU.mult, [SG[gb], PU], [AT])
            for bi, i in enumerate(bl):
                for half in range(2):
                    for f in range(NF):
                        mm(PS[half][:, 0:512], AT[:, f, bi * 128:(bi + 1) * 128], Wd[:, f, half * 512:(half + 1) * 512],
                           f == 0, f == NF - 1, [AT, Wd], [PS[half]])
                for half in range(2):
                    tt("dve", HT[:, bi, half * 512:(half + 1) * 512], HT[:, bi, half * 512:(half + 1) * 512], PS[half][:, 0:512],
                       ALU.add, [HT, PS[half]], [HT])
                if not last:
                    dma("pool", hres[i, :, :], HT[:, bi, :], [HT], [])
                if last and i >= 1:
                    ob = cnt["o"] % 2
                    cnt["o"] += 1
                    S_ = st[ob]
                    ttr(junk, junk[:, :], HT[:, bi, :], HT[:, bi, :], S_[:, 4:5], [HT], [S_])
                    rstd_ops(S_, 4, 1024)
                    stt("dve", OUTT[ob][:, :], HT[:, bi, :], S_[:, 6:7], Gfin[:, :], ALU.mult, ALU.mult, [HT, S_, Gfin], [OUTT[ob]])
                    dma("pool", out[i - 1, :, :], OUTT[ob][:, :], [OUTT[ob]], [])

    steps = [("weights", phase_weights), ("masks", lambda: (phase_masks(), phase_init()))]
    for l in range(2):
        steps.append(("proj%d" % l, lambda l=l: phase_proj(l)))
        steps.append(("gather%d" % l, lambda l=l: phase_gather(l)))
        steps.append(("mla%d" % l, lambda l=l: phase_attn_mla(l)))
        steps.append(("diff%d" % l, lambda l=l: phase_attn_diff(l)))
        steps.append(("swa%d" % l, lambda l=l: phase_attn_swa(l)))
        steps.append(("ffn%d" % l, lambda l=l: phase_ffn(l, l == 1)))
    barrier_before = ("masks", "proj0", "proj1", "ffn0", "ffn1", "diff0", "diff1", "swa0", "swa1")
    for si, (name, fn) in enumerate(steps):
        if stop is not None and si >= stop:
            break
        if name in barrier_before:
            P.barrier()
        fn()
    P.barrier()
    if debug and "kvown" in dump:
        dma("sp", kvodbg[:, :], kvown[:, :], [], [])
        P.barrier()

    ninst = emit_all(P, sems, dsems)
    for cm in reversed(ctx):
        cm.__exit__(None, None, None)
    return nc, ninst


def emit_all(P, sems, dsems):
    nc = P.nc
    engs = {"pe": nc.tensor, "act": nc.scalar, "dve": nc.vector, "pool": nc.gpsimd, "sp": nc.sync}
    for e in Prog.ENG:
        c = 0
        nd = 0
        for o in P.ops[e]:
            if o.dma:
                o.sem = dsems[e][nd % NDMASEM]
                o.val = 16 * (nd // NDMASEM + 1)
                nd += 1
            elif o.cc is not None:
                o.sem = o.cc
                o.val = 1
            elif o.signal:
                c += 1
                o.sem = sems[e]
                o.val = c
    ninst = 0
    for e in Prog.ENG:
        eng = engs[e]
        waited = {}
        for o in P.ops[e]:
            for d in o.deps:
                key = id(d.sem)
                if waited.get(key, 0) < d.val:
                    eng.wait_ge(d.sem, d.val)
                    waited[key] = d.val
                    ninst += 1
            if o.fn is not None:
                ins = o.fn(eng)
                ninst += 1
                if o.dma:
                    ins.then_inc(o.sem, 16)
                elif o.cc is not None:
                    ins.then_inc(o.sem)
                elif o.signal:
                    ins.then_inc(o.sem, 1)
    return ninst


_CACHE = {}


def _get_program(debug=False):
    if debug not in _CACHE:
        _CACHE[debug] = build_program(debug)
    return _CACHE[debug]


PARAM_NAMES = ["meta_tokens", "rel_bias", "attn_norm", "w_in", "mla_q_norm", "mla_w_qb", "mla_kv_norm", "mla_w_kvb",
               "diff_lambda", "diff_subln", "swa_sinks", "w_out", "ffn_norm", "w_gate", "w_up", "w_down", "final_norm"]


def make_in_maps(inputs):
    x = np.ascontiguousarray(np.asarray(inputs["x"], dtype=np.float32))
    params = {k: np.ascontiguousarray(np.asarray(inputs[k], dtype=np.float32)) for k in PARAM_NAMES}
    consts = [make_consts(r) for r in range(NR)]
    in_maps = []
    for c in range(8):
        b, r = c // 2, c % 2
        xb = x[b].reshape(64, 128, D)[r::2]
        m = {"xin": np.ascontiguousarray(xb)}
        m.update(params)
        m.update(consts[r])
        in_maps.append(m)
    return in_maps


def kernel(**inputs):
    nc, _ = _get_program(False)
    in_maps = make_in_maps(inputs)
    res = run_bass_kernel_spmd(nc, in_maps, core_ids=list(range(8)))
    outp = np.empty((4, 64, 128, D), np.float32)
    for c in range(8):
        b, r = c // 2, c % 2
        outp[b, r::2] = np.asarray(res.results[c]["out"]).reshape(32, 128, D)
    return outp.reshape(4, 8192, D)
```

```python
import math
import numpy as np
import concourse.bass as bass
import concourse.mybir as mybir
from concourse.bass_utils import run_bass_kernel_spmd

F32 = mybir.dt.float32
BF16 = mybir.dt.bfloat16
AF = mybir.ActivationFunctionType
ALU = mybir.AluOpType

D = 1024
DIN = 1824
DFF = 2816
NF = DFF // 128
NR = 2
NLOC = 33
T = NLOC * 128
VW = NLOC * 65
KVROWS = 1740
RM, RD, RS0, RS1 = 161, 129, 128, 130
OFF_M = 0
OFF_D = OFF_M + 6 * RM * T
OFF_S0 = OFF_D + 4 * RD * T
OFF_S1 = OFF_S0 + RS0 * T
assert OFF_S1 + RS1 * T == KVROWS * T
PIECES = [("m%d" % h, h * RM, RM) for h in range(6)] + [("d%d" % h, 6 * RM + h * RD, RD) for h in range(4)] + \
         [("s0", 6 * RM + 4 * RD, RS0), ("s1", 6 * RM + 4 * RD + RS0, RS1)]
NME = 41
SC_MLA = 96 ** -0.5
SC_DIFF = 32 ** -0.5
SC_SWA = 64 ** -0.5
EPS = 1e-6
LAM_INIT = [0.8 - 0.6 * math.exp(-0.3 * l) for l in range(2)]
WIN_SEGS = [(0, 928), (1184, 1696), (928, 1184), (1696, 1824)]
PGRP = [(0, 416), (416, 928), (928, 1440), (1440, 1824)]


class Buf:
    __slots__ = ("name", "ws", "r", "psum", "pr")

    def __init__(self, name):
        self.name = name
        self.ws = []
        self.r = []
        self.pr = []
        self.psum = False


class Op:
    __slots__ = ("e", "fn", "dma", "deps", "signal", "sem", "val", "idx", "cc")

    def __init__(self, e, fn, dma):
        self.e = e
        self.fn = fn
        self.dma = dma
        self.deps = []
        self.signal = False
        self.sem = None
        self.val = 0
        self.cc = None


class Tile:
    def __init__(self, ap, name):
        self.ap = ap
        self.buf = Buf(name)

    def __getitem__(self, k):
        return self.ap[k]


NDMASEM = 20


class Prog:
    ENG = ("pe", "act", "dve", "pool", "sp")

    def __init__(self, nc):
        self.nc = nc
        self.ops = {e: [] for e in self.ENG}
        self.dma_ops = {e: [] for e in self.ENG}
        self.bufs = []

    def buf(self, name):
        b = Buf(name)
        self.bufs.append(b)
        return b

    def tile(self, ap, name):
        t = Tile(ap, name)
        self.bufs.append(t.buf)
        return t

    def op(self, e, fn, reads=(), writes=(), dma=False):
        o = Op(e, fn, dma)
        deps = []
        seen = set()

        def add(d):
            if d is None or id(d) in seen:
                return
            seen.add(id(d))
            if d.e == "pe" and e == "pe" and not d.dma and not dma:
                return
            deps.append(d)

        rb = [t.buf if isinstance(t, Tile) else t for t in reads]
        wb = [t.buf if isinstance(t, Tile) else t for t in writes]
        for b in rb:
            for x in b.ws:
                add(x)
            if b.psum:
                for x in b.r:
                    if x.e != e:
                        add(x)
        for b in wb:
            for x in b.r:
                add(x)
            for x in b.pr:
                add(x)
            for x in b.ws:
                if not (x.dma and dma):
                    add(x)
        if dma:
            lst = self.dma_ops[e]
            k = len(lst)
            if k >= NDMASEM:
                add(lst[k - NDMASEM])
            lst.append(o)
        for d in deps:
            d.signal = True
        o.deps = deps
        wset = set(id(b) for b in wb)
        for b in wb:
            if b.r or any(b is x for x in rb):
                b.pr = list(b.r)
                b.ws = [o]
                b.r = []
            else:
                b.ws.append(o)
        for b in rb:
            if id(b) not in wset:
                b.r.append(o)
        self.ops[e].append(o)
        return o

    def barrier(self):
        lasts = []
        for e in self.ENG:
            for o in reversed(self.ops[e]):
                if o.fn is not None and not o.dma:
                    lasts.append(o)
                    break
            lasts.extend(self.dma_ops[e][-NDMASEM:])
        for e in self.ENG:
            o = Op(e, None, False)
            o.deps = [d for d in lasts]
            self.ops[e].append(o)
        for d in lasts:
            d.signal = True
        for b in self.bufs:
            b.ws = []
            b.r = []
            b.pr = []


def DAP(t, off, *dims):
    return bass.AP(t, off, [[s, n] for (s, n) in dims])


class Arena:
    def __init__(self, P, ap, size, name):
        self.P = P
        self.ap = ap
        self.size = size
        self.off = 0
        self.name = name

    def reset(self):
        self.off = 0

    def alloc(self, n, name, pat=None, **kw):
        n2 = (n + 15) // 16 * 16
        assert self.off + n2 <= self.size, (self.name, name, self.off, n2, self.size)
        ap = self.ap[:, self.off:self.off + n]
        self.off += n2
        if pat is not None:
            ap = ap.rearrange(pat, **kw)
        return self.P.tile(ap, name)


def t5_bucket_np(n):
    n = np.maximum(n, 0)
    nf = np.maximum(n, 16).astype(np.float32)
    large = 16 + (np.log(nf / np.float32(16)) / np.float32(math.log(128 / 16)) * np.float32(16)).astype(np.int32)
    large = np.minimum(large, 31)
    return np.where(n < 16, n, large)


def make_consts(r):
    c = {}
    inv_freq = (10000.0 ** (-np.arange(0, 32, 2, dtype=np.float32) / np.float32(32))).astype(np.float32)
    cs = np.zeros((128, NLOC, 64), np.float32)
    for i in range(NLOC):
        p = np.arange(128)
        if i == 0:
            pos = np.minimum(p, 15)
        else:
            G = NR * (i - 1) + 1 + r
            pos = (G - 1) * 128 + p + 16
        ang = pos.astype(np.float32)[:, None] * inv_freq[None, :]
        co = np.cos(ang).astype(np.float32)
        si = np.sin(ang).astype(np.float32)
        cs[:, i, 0:16] = co
        cs[:, i, 16:32] = co
        cs[:, i, 32:48] = -si
        cs[:, i, 48:64] = si
    c["cs_tab"] = cs.reshape(128, NLOC * 64)
    k = np.arange(128)[:, None]
    q = np.arange(128)[None, :]
    tri = (k <= q).astype(np.float32)
    one = np.ones((128, 128), np.float32)
    zero = np.zeros((128, 128), np.float32)
    mA = np.zeros((128, 9, 4, 128), np.float32)
    for s in range(8):
        for qb in range(4):
            j = 1 + s
            G = 2 * qb + 1 + r
            mA[:, s, qb, :] = one if j < G else (tri if j == G else zero)
    mA[:, 8, 0, :] = tri
    c["mA"] = mA.reshape(128, 9 * 512)
    cd = np.zeros((4, NME), np.float32)
    for s in range(9):
        for qb in range(4):
            j = s
            G = 2 * qb + 1 + r
            e = s * 4 + qb
            if j < G - 1:
                cd[0, e] = 1
            elif j == G - 1:
                cd[2, e] = 1
            elif j == G:
                cd[1, e] = 1
    for qb in range(4):
        G = 2 * qb + 1 + r
        if G == 1:
            cd[3, 36 + qb] = 1
        else:
            cd[0, 36 + qb] = 1
    cd[1, 40] = 1
    c["cdiff"] = np.broadcast_to(cd.reshape(1, 4 * NME), (128, 4 * NME)).copy()
    csw = np.zeros((4, 4), np.float32)
    if r == 0:
        csw[1, 0] = 1; csw[0, 1] = 1
        csw[3, 3] = 1
    else:
        csw[1, 1] = 1; csw[0, 2] = 1
        csw[2, 3] = 1
    c["cswa"] = np.broadcast_to(csw.reshape(1, 16), (128, 16)).copy()
    oh = np.zeros((32, 768), np.float32)
    al = np.zeros((10, 768), np.float32)
    for j in range(256):
        if j < 128:
            oh[t5_bucket_np(np.array(j))[()], j] = 1; al[:, j] = 1
        if j < 128:
            oh[31, 256 + j] = 1; al[0:4, 256 + j] = 1
        elif j > 128:
            oh[t5_bucket_np(np.array(j - 128))[()], 256 + j] = 1; al[:, 256 + j] = 1
        if j < 128:
            oh[t5_bucket_np(np.array(j + 16))[()], 512 + j] = 1; al[:, 512 + j] = 1
        elif j >= 241:
            oh[t5_bucket_np(np.array(j - 240))[()], 512 + j] = 1; al[:, 512 + j] = 1
    c["oh"] = oh
    c["aallow"] = al
    c["ident"] = np.eye(128, dtype=np.float32)
    return c


def build_program(debug=False, stop=None, dump=(), ncores=8, nblk=NLOC, cut=99):
    nc = bass.Bass("TRN2", target_bir_lowering=False)
    P = Prog(nc)

    def din(name, shape, dt=F32):
        return nc.dram_tensor(name, list(shape), dt, kind="ExternalInput")

    xin = din("xin", [32, 128, D])
    meta_tokens = din("meta_tokens", [16, D])
    rel_bias = din("rel_bias", [32, 10])
    attn_norm = din("attn_norm", [2, D])
    w_in = din("w_in", [2, D, DIN])
    mla_q_norm = din("mla_q_norm", [2, 256])
    mla_w_qb = din("mla_w_qb", [2, 256, 576])
    mla_kv_norm = din("mla_kv_norm", [2, 128])
    mla_w_kvb = din("mla_w_kvb", [2, 128, 768])
    diff_lambda = din("diff_lambda", [2, 4, 32])
    diff_subln = din("diff_subln", [2, 64])
    swa_sinks = din("swa_sinks", [2, 6])
    w_out = din("w_out", [2, D, D])
    ffn_norm = din("ffn_norm", [2, D])
    w_gate = din("w_gate", [2, D, DFF])
    w_up = din("w_up", [2, D, DFF])
    w_down = din("w_down", [2, DFF, D])
    final_norm = din("final_norm", [D])
    cs_tab = din("cs_tab", [128, NLOC * 64])
    mA_in = din("mA", [128, 9 * 512])
    cdiff_in = din("cdiff", [128, 4 * NME])
    cswa_in = din("cswa", [128, 16])
    oh_in = din("oh", [32, 768])
    aallow_in = din("aallow", [10, 768])
    ident_in = din("ident", [128, 128])
    out = nc.dram_tensor("out", [32, 128, D], F32, kind="ExternalOutput")

    def dscr(name, shape, dt):
        if debug and name in dump:
            return nc.dram_tensor(name, list(shape), dt, kind="ExternalOutput")
        return nc.dram_tensor(name, list(shape), dt)

    hres = dscr("hres", [NLOC, 128, D], F32)
    kvown = nc.dram_tensor("kvown", [KVROWS, T], BF16)
    kvall = {nm: nc.dram_tensor("kvall_" + nm, [NR * nr, T], BF16) for (nm, r0, nr) in PIECES}
    pbuf = {nm: P.buf("piece_" + nm) for (nm, r0, nr) in PIECES}
    QTm = dscr("QTm", [6 * 96, T], BF16)
    QTd = dscr("QTd", [256, T], BF16)
    QTs = dscr("QTs", [128, 3 * T], BF16)
    yT = dscr("yT", [D, T], BF16)
    Wb_in = dscr("Wb_in", [2, 128, 8 * DIN], BF16)
    Wb_qb = dscr("Wb_qb", [2, 128, 2 * 576], BF16)
    Wb_kvb = dscr("Wb_kvb", [2, 128, 768], BF16)
    Wb_out = dscr("Wb_out", [2, 128, 8 * D], BF16)
    Wb_g = dscr("Wb_g", [2, NF, 128, 8 * 128], BF16)
    Wb_u = dscr("Wb_u", [2, NF, 128, 8 * 128], BF16)
    Wb_d = dscr("Wb_d", [2, 128, NF * D], BF16)
    Gd = dscr("Gd", [10, 768], F32)
    zper = dscr("zper", [30, 129 * 256], F32)
    maskD = dscr("maskD", [4, 128, NME * 128], F32)
    maskS = dscr("maskS", [128, 2 * 4 * 384], F32)
    if debug and "kvown" in dump:
        kvodbg = nc.dram_tensor("kvodbg", [KVROWS, T], BF16, kind="ExternalOutput")

    ctx = []

    def enter(cm):
        ctx.append(cm)
        return cm.__enter__()

    NFA = 17 * 1024
    NBA = 56 * 1024
    arena_f_t = enter(nc.sbuf_tensor("arena_f", [128, NFA], F32))
    arena_b_t = enter(nc.sbuf_tensor("arena_b", [128, NBA], BF16))
    psum_t = enter(nc.psum_tensor("psum", [128, 8 * 512], F32))
    sems = {e: enter(nc.semaphore("sem_" + e)) for e in Prog.ENG}
    dsems = {e: [enter(nc.semaphore("dsem_%s_%d" % (e, i))) for i in range(NDMASEM)] for e in ("sp", "act", "pool")}
    ccsems = [enter(nc.semaphore("ccsem%d" % i)) for i in range(2 * len(PIECES))]
    blk = enter(nc.Block())

    AFa = Arena(P, arena_f_t[:, :], NFA, "f32")
    ABa = Arena(P, arena_b_t[:, :], NBA, "bf16")
    PS = [P.tile(psum_t[:, b * 512:(b + 1) * 512], "ps%d" % b) for b in range(8)]
    for t_ in PS:
        t_.buf.psum = True

    def psb(b):
        return psum_t[:, b * 512:(b + 1) * 512].bitcast(BF16)

    def dma(q, out_ap, in_ap, reads=(), writes=()):
        return P.op(q, lambda eng, o=out_ap, i=in_ap: eng.dma_start(out=o, in_=i), reads, writes, dma=True)

    def mm(out_ap, lhsT, rhs, start, stop, reads, writes, tp=None):
        if tp is not None:
            return P.op("pe", lambda eng: eng.matmul(out_ap, lhsT, rhs, start=start, stop=stop, tile_position=tp), reads, writes)
        return P.op("pe", lambda eng: eng.matmul(out_ap, lhsT, rhs, start=start, stop=stop), reads, writes)

    def tr(out_ap, in_ap, ident_ap, reads, writes):
        return P.op("pe", lambda eng: eng.transpose(out_ap, in_ap, ident_ap), reads, writes)

    def act(out_ap, in_ap, func, reads, writes, scale=1.0, bias=0.0, accum=None):
        if accum is not None:
            return P.op("act", lambda eng: eng.activation(out_ap, in_ap, func, bias=bias, scale=scale, accum_out=accum),
                        reads, writes)
        return P.op("act", lambda eng: eng.activation(out_ap, in_ap, func, bias=bias, scale=scale), reads, writes)

    def tt(e, out_ap, in0, in1, op, reads, writes):
        return P.op(e, lambda eng: eng.tensor_tensor(out_ap, in0, in1, op), reads, writes)

    def ts(e, out_ap, in0, s1, s2, op0, op1, reads, writes):
        if op1 is None:
            return P.op(e, lambda eng: eng.tensor_scalar(out_ap, in0, s1, None, op0), reads, writes)
        return P.op(e, lambda eng: eng.tensor_scalar(out_ap, in0, s1, s2, op0, op1), reads, writes)

    def stt(e, out_ap, in0, scalar, in1, op0, op1, reads, writes):
        return P.op(e, lambda eng: eng.scalar_tensor_tensor(out_ap, in0, scalar, in1, op0, op1), reads, writes)

    def cp(e, out_ap, in_ap, reads, writes):
        if e == "act":
            return P.op("act", lambda eng: eng.copy(out_ap, in_ap), reads, writes)
        return P.op(e, lambda eng: eng.tensor_copy(out_ap, in_ap), reads, writes)

    def ttr(jt, out_ap, in0, in1, acc, reads, writes):
        return act(out_ap, in0, AF.Square, list(reads) + [jt], list(writes) + [jt], accum=acc)

    def memset(e, ap, val, writes):
        return P.op(e, lambda eng: eng.memset(ap, val), (), writes)

    def rstd_ops(st, c0, n, reads_extra=()):
        act(st[:, c0 + 1:c0 + 2], st[:, c0:c0 + 1], AF.Ln, [st], [st], scale=1.0 / n, bias=EPS)
        act(st[:, c0 + 2:c0 + 3], st[:, c0 + 1:c0 + 2], AF.Exp, [st], [st], scale=-0.5)

    NPERS_F = 128 + 64 + 2 + 2 + 12 * 128 + 16
    pers_f = Arena(P, arena_f_t[:, NFA - 2048:NFA], 2048, "persf")
    NFA_USE = NFA - 2048
    AFa.size = NFA_USE
    pers_b = Arena(P, arena_b_t[:, NBA - 256:NBA], 256, "persb")
    ABa.size = NBA - 256
    ones_f = pers_f.alloc(128, "ones_f")
    lam_t = pers_f.alloc(16, "lam_t")
    gsub_t = pers_f.alloc(16, "gsub_t")
    sinkrow = pers_f.alloc(12 * 128, "sinkrow", "p (a b) -> p a b", b=128)
    ident_b = pers_b.alloc(128, "ident_b")
    sel_f = pers_f.alloc(64, "sel_f")

    def phase_weights():
        AFa.reset(); ABa.reset()
        stg = [AFa.alloc(2816, "wstg%d" % i) for i in range(2)]
        stb = [ABa.alloc(2816, "wstb%d" % i) for i in range(2)]
        cnt = [0]

        def one(src_aps, ncols, dst_ap):
            i = cnt[0] % 2
            cnt[0] += 1
            for (so, do, n, ap) in src_aps:
                dma("sp", stg[i][:, do:do + n], ap, [], [stg[i]])
            e = "dve" if (cnt[0] % 2 == 0) else "act"
            cp(e, stb[i][:, 0:ncols], stg[i][:, 0:ncols], [stg[i]], [stb[i]])
            dma("pool", dst_ap, stb[i][:, 0:ncols], [stb[i]], [])

        for l in range(2):
            for k in range(8):
                srcs = []
                do = 0
                for (a, b) in WIN_SEGS:
                    srcs.append((a, do, b - a, w_in[l, k * 128:(k + 1) * 128, a:b]))
                    do += b - a
                one(srcs, DIN, DAP(Wb_in, l * 128 * 8 * DIN + k * DIN, (8 * DIN, 128), (1, DIN)))
            for k in range(2):
                one([(0, 0, 576, mla_w_qb[l, k * 128:(k + 1) * 128, :])], 576,
                    DAP(Wb_qb, l * 128 * 1152 + k * 576, (1152, 128), (1, 576)))
            one([(0, 0, 768, mla_w_kvb[l, :, :])], 768, DAP(Wb_kvb, l * 128 * 768, (768, 128), (1, 768)))
            for k in range(8):
                one([(0, 0, D, w_out[l, k * 128:(k + 1) * 128, :])], D,
                    DAP(Wb_out, l * 128 * 8 * D + k * D, (8 * D, 128), (1, D)))

    def phase_masks():
        AFa.reset(); ABa.reset()
        memset("dve", ones_f[:, :], 1.0, [ones_f])
        idf = AFa.alloc(128, "idf")
        dma("sp", idf[:, :], ident_in[:, :], [], [idf])
        cp("dve", ident_b[:, :], idf[:, :], [idf], [ident_b])
        tt("dve", sel_f[:, 0:64], idf[:, 0:64], idf[:, 64:128], ALU.subtract, [idf], [sel_f])
        lp = AFa.alloc(256, "lp")
        dma("sp", lp[:, :], DAP(diff_lambda, 0, (0, 128), (1, 256)), [], [lp])
        junk = AFa.alloc(64, "junk")
        lst = AFa.alloc(16, "lst")
        for l in range(2):
            for pr in range(2):
                a = l * 128 + pr * 64
                tt("dve", junk[:, 0:32], lp[:, a:a + 32], lp[:, a + 32:a + 64], ALU.mult, [lp], [junk])
                P.op("dve", lambda eng, o_=lst[:, 2 * l + pr:2 * l + pr + 1]: eng.reduce_sum(o_, junk[:, 0:32], mybir.AxisListType.X),
                     [junk], [lst])
        lex = AFa.alloc(16, "lex")
        act(lex[:, 0:4], lst[:, 0:4], AF.Exp, [lst], [lex])
        for l in range(2):
            tt("dve", lam_t[:, l:l + 1], lex[:, 2 * l:2 * l + 1], lex[:, 2 * l + 1:2 * l + 2], ALU.subtract, [lex], [lam_t])
            ts("dve", lam_t[:, l:l + 1], lam_t[:, l:l + 1], LAM_INIT[l], None, ALU.add, None, [lam_t], [lam_t])
        gs0 = AFa.alloc(16, "gs0")
        for l in range(2):
            dma("sp", gs0[0:64, l:l + 1], DAP(diff_subln, 64 * l, (1, 64), (1, 1)), [], [gs0])
        for l in range(2):
            ts("dve", gsub_t[0:64, l:l + 1], gs0[0:64, l:l + 1], 1.0 - LAM_INIT[l], None, ALU.mult, None, [gs0], [gsub_t])
        rb = AFa.alloc(16, "rb")
        dma("sp", rb[0:32, 0:10], rel_bias[:, :], [], [rb])
        oh = AFa.alloc(768, "oh")
        dma("sp", oh[0:32, :], oh_in[:, :], [], [oh])
        al = AFa.alloc(768, "al")
        dma("sp", al[0:10, :], aallow_in[:, :], [], [al])
        b31 = AFa.alloc(16, "b31")
        dma("sp", b31[0:10, 0:1], DAP(rel_bias, 310, (1, 10), (1, 1)), [], [b31])
        ts("dve", b31[0:10, 1:2], b31[0:10, 0:1], -1.0, None, ALU.mult, None, [b31], [b31])
        mm(PS[0][0:10, 0:512], rb[0:32, 0:10], oh[0:32, 0:512], True, True, [rb, oh], [PS[0]])
        mm(PS[1][0:10, 0:256], rb[0:32, 0:10], oh[0:32, 512:768], True, True, [rb, oh], [PS[1]])
        gl = AFa.alloc(768, "gl")
        act(gl[0:10, 0:512], PS[0][0:10, 0:512], AF.Exp, [PS[0], b31], [gl], bias=b31[0:10, 1:2])
        act(gl[0:10, 512:768], PS[1][0:10, 0:256], AF.Exp, [PS[1], b31], [gl], bias=b31[0:10, 1:2])
        tt("dve", gl[0:10, :], gl[0:10, :], al[0:10, :], ALU.mult, [gl, al], [gl])
        o1 = dma("sp", Gd[:, :], gl[0:10, :], [gl], [])
        gdb = P.buf("gd")
        gdb.ws = [o1]
        zb = P.buf("zper")
        for hv in range(30):
            dma("pool", DAP(zper, hv * 129 * 256, (256, 129), (1, 256)), DAP(Gd, hv * 256, (0, 129), (1, 256)), [gdb], [zb])
        P.barrier()
        AFa.reset()
        Tall = AFa.alloc(30 * 128, "Tall", "p (a b) -> p a b", b=128)
        for hv in range(30):
            dma("sp" if hv % 2 == 0 else "pool", Tall[:, hv, :], DAP(zper, hv * 129 * 256, (255, 128), (1, 128)), [], [Tall])
        sk = AFa.alloc(32, "sk")
        dma("sp", sk[64:65, 0:12], DAP(swa_sinks, 0, (0, 1), (1, 12)), [], [sk])
        dma("sp", sk[64:65, 16:22], DAP(rel_bias, 314, (0, 1), (1, 6)), [], [sk])
        for l in range(2):
            tt("dve", sk[64:65, 6 * l:6 * l + 6], sk[64:65, 6 * l:6 * l + 6], sk[64:65, 16:22], ALU.subtract, [sk], [sk])
        act(sk[64:65, 0:12], sk[64:65, 0:12], AF.Exp, [sk], [sk])
        cp("dve", sinkrow[64:65, :, :], sk[64:65, 0:12].unsqueeze(2).broadcast_to([1, 12, 128]), [sk], [sinkrow])
        cdf = AFa.alloc(4 * NME, "cdf")
        dma("sp", cdf[:, :], cdiff_in[:, :], [], [cdf])
        NH = 21
        Mt = [AFa.alloc(NH * 128, "Mt%d" % i, "p (a b) -> p a b", b=128) for i in range(2)]

        for h in range(4):
            for (e0, e1) in ((0, NH), (NH, NME)):
                ne = e1 - e0

                def cb(kind):
                    return cdf[:, kind * NME + e0:kind * NME + e1].unsqueeze(2).broadcast_to([128, ne, 128])

                def tb(v):
                    return Tall[:, h * 3 + v, :].unsqueeze(1).broadcast_to([128, ne, 128])

                M, M2 = Mt[0][:, 0:ne, :], Mt[1][:, 0:ne, :]
                tt("dve", M, cb(1), tb(0), ALU.mult, [cdf, Tall], [Mt[0]])
                tt("pool", M2, cb(2), tb(1), ALU.mult, [cdf, Tall], [Mt[1]])
                tt("dve", M, M, M2, ALU.add, [Mt[0], Mt[1]], [Mt[0]])
                tt("pool", M2, cb(3), tb(2), ALU.mult, [cdf, Tall], [Mt[1]])
                tt("dve", M, M, M2, ALU.add, [Mt[0], Mt[1]], [Mt[0]])
                tt("dve", M, M, cb(0), ALU.add, [Mt[0], cdf], [Mt[0]])
                dma("sp", DAP(maskD, h * 128 * NME * 128 + e0 * 128, (NME * 128, 128), (1, ne * 128)),
                    M.rearrange("p a b -> p (a b)"), [Mt[0]], [])
        csf = AFa.alloc(16, "csf")
        dma("sp", csf[:, :], cswa_in[:, :], [], [csf])
        MS = AFa.alloc(2 * 4 * 384, "MSb", "p (g e r c) -> p g e r c", g=2, e=4, r=3)
        for g in range(2):
            h0 = 4 + 3 * g
            Tc = Tall[:, :, :].rearrange("p (h v) c -> p h v c", v=3)[:, h0:h0 + 3, 0, :]
            Tp = Tall[:, :, :].rearrange("p (h v) c -> p h v c", v=3)[:, h0:h0 + 3, 1, :]
            Tm = Tall[:, :, :].rearrange("p (h v) c -> p h v c", v=3)[:, h0:h0 + 3, 2, :]
            for e in range(3):
                ts("dve", MS[:, g, e, :, :], Tc, csf[:, 0 + e:0 + e + 1], None, ALU.mult, None, [Tall, csf], [MS])
                stt("dve", MS[:, g, e, :, :], Tp, csf[:, 4 + e:4 + e + 1], MS[:, g, e, :, :], ALU.mult, ALU.add,
                    [Tall, csf, MS], [MS])
            ts("dve", MS[:, g, 3, :, :], Tm, csf[:, 12 + 3:12 + 4], None, ALU.mult, None, [Tall, csf], [MS])
            ts("dve", MS[:, g, 3, :, :], MS[:, g, 3, :, :], csf[:, 8 + 3:8 + 4], None, ALU.add, None, [MS, csf], [MS])
        dma("sp", maskS[:, :], MS[:, :, :, :, :].rearrange("p g e r c -> p (g e r c)"), [MS], [])
        dma("sp", DAP(zper, 0, (6 * 128, 16), (128, 6), (1, 128)),
            Tall[:, :, :].rearrange("p (h v) c -> p h v c", v=3)[0:16, 4:10, 0, :], [Tall], [])

    def phase_init():
        AFa.reset(); ABa.reset()
        z = AFa.alloc(1024, "zinit")
        memset("dve", z[:, :], 0.0, [z])
        dma("sp", hres[0, 16:128, :], z[16:128, :], [z], [])
        dma("sp", hres[0, 0:16, :], meta_tokens[:, :], [], [])

    def hsrc(l, i):
        if l == 0 and i >= 1:
            return xin[i - 1, :, :]
        return hres[i, :, :]

    def phase_proj(l):
        AFa.reset(); ABa.reset()
        Win = ABa.alloc(8 * DIN, "Win", "p (k c) -> p k c", c=DIN)
        Wqb = ABa.alloc(2 * 576, "Wqb", "p (k c) -> p k c", c=576)
        Wkvb = ABa.alloc(768, "Wkvb")
        dma("sp", Win[:, :, :], DAP(Wb_in, l * 128 * 8 * DIN, (8 * DIN, 128), (DIN, 8), (1, DIN)), [], [Win])
        dma("sp", Wqb[:, :, :], DAP(Wb_qb, l * 128 * 1152, (1152, 128), (576, 2), (1, 576)), [], [Wqb])
        dma("sp", Wkvb[:, :], DAP(Wb_kvb, l * 128 * 768, (768, 128), (1, 768)), [], [Wkvb])
        Gat = AFa.alloc(1024, "Gat")
        Gq = AFa.alloc(256, "Gq")
        Gkv = AFa.alloc(128, "Gkv")
        dma("sp", Gat[:, :], DAP(attn_norm, l * D, (0, 128), (1, D)), [], [Gat])
        dma("sp", Gq[:, :], DAP(mla_q_norm, l * 256, (0, 128), (1, 256)), [], [Gq])
        dma("sp", Gkv[:, :], DAP(mla_kv_norm, l * 128, (0, 128), (1, 128)), [], [Gkv])
        cs = AFa.alloc(NLOC * 64, "cs", "p (i c) -> p i c", c=64)
        dma("sp", cs[:, :, :], cs_tab[:, :].rearrange("p (i c) -> p i c", c=64), [], [cs])
        junk = AFa.alloc(1024, "junk")
        Hh = [AFa.alloc(1024, "Hh%d" % i) for i in range(2)]
        st = [AFa.alloc(16, "st%d" % i) for i in range(2)]
        ra = [AFa.alloc(6 * 32, "ra%d" % i, "p (h c) -> p h c", c=32) for i in range(2)]
        rb_ = [AFa.alloc(6 * 32, "rb%d" % i, "p (h c) -> p h c", c=32) for i in range(2)]
        Craw = [AFa.alloc(416, "Craw%d" % i) for i in range(2)]
        Hn = [ABa.alloc(1024, "Hn%d" % i) for i in range(2)]
        HnT = [ABa.alloc(1024, "HnT%d" % i) for i in range(2)]
        Cn = [ABa.alloc(384, "Cn%d" % i) for i in range(2)]
        CT = [ABa.alloc(384, "CT%d" % i) for i in range(2)]
        Qf = [ABa.alloc(576, "Qf%d" % i, "p (h c) -> p h c", c=96) for i in range(2)]
        Kf = [ABa.alloc(576, "Kf%d" % i, "p (h c) -> p h c", c=96) for i in range(2)]
        Vf = [ABa.alloc(390, "Vf%d" % i, "p (h c) -> p h c", c=65) for i in range(2)]
        krot = [ABa.alloc(32, "krot%d" % i) for i in range(2)]
        QT = [ABa.alloc(768, "QT%d" % i) for i in range(2)]
        KT = [ABa.alloc(768, "KT%d" % i) for i in range(2)]
        Df = [ABa.alloc(512, "Df%d" % i) for i in range(2)]
        DT = [ABa.alloc(512, "DT%d" % i) for i in range(2)]
        Vdf = [ABa.alloc(260, "Vdf%d" % i, "p (h c) -> p h c", c=65) for i in range(2)]
        Sf = [ABa.alloc(512, "Sf%d" % i) for i in range(2)]
        ST = [ABa.alloc(512, "ST%d" % i) for i in range(2)]
        Vsf = [ABa.alloc(130, "Vsf%d" % i, "p (h c) -> p h c", c=65) for i in range(2)]
        for i in range(2):
            memset("pool", Vf[i][:, :, 64:65], 1.0, [Vf[i]])
            memset("pool", Vdf[i][:, :, 64:65], 1.0, [Vdf[i]])
            memset("pool", Vsf[i][:, :, 64:65], 1.0, [Vsf[i]])

        def rope(src3, dst3, H, i, db, src_t, dst_t):
            A = ra[db][:, 0:H, :]
            B = rb_[db][:, 0:H, :]
            cA = cs[:, i, 0:32].unsqueeze(1).broadcast_to([128, H, 32])
            sB0 = cs[:, i, 32:48].unsqueeze(1).broadcast_to([128, H, 16])
            sB1 = cs[:, i, 48:64].unsqueeze(1).broadcast_to([128, H, 16])
            tt("dve", A, src3, cA, ALU.mult, [src_t, cs], [ra[db]])
            tt("dve", B[:, :, 0:16], src3[:, :, 16:32], sB0, ALU.mult, [src_t, cs], [rb_[db]])
            tt("dve", B[:, :, 16:32], src3[:, :, 0:16], sB1, ALU.mult, [src_t, cs], [rb_[db]])
            tt("dve", dst3, A, B, ALU.add, [ra[db], rb_[db]], [dst_t])

        for i in range(nblk):
            db = i % 2
            H, S_, hn, hnT = Hh[db], st[db], Hn[db], HnT[db]
            dma("sp", H[:, :], hsrc(l, i), [], [H])
            ttr(junk, junk[:, :], H[:, :], H[:, :], S_[:, 0:1], [H], [S_])
            rstd_ops(S_, 0, 1024)
            stt("dve", hn[:, :], H[:, :], S_[:, 2:3], Gat[:, :], ALU.mult, ALU.mult, [H, S_, Gat], [hn])
            if cut <= 1:
                continue
            for k in range(8):
                tr(psb(6)[:, k * 128:(k + 1) * 128], hn[:, k * 128:(k + 1) * 128], ident_b[:, :], [hn, ident_b], [PS[6]])
            cp("act", hnT[:, :], psb(6)[:, :], [PS[6]], [hnT])
            if cut <= 2:
                continue
            for g, (c0, c1) in enumerate(PGRP):
                for k in range(8):
                    mm(PS[g][:, 0:c1 - c0], hnT[:, k * 128:(k + 1) * 128], Win[:, k, c0:c1], k == 0, k == 7,
                       [hnT, Win], [PS[g]])
            if cut <= 3:
                continue
            cp("act", Df[db][:, :], PS[1][:, 0:512], [PS[1]], [Df[db]])
            cp("act", Sf[db][:, 0:384].rearrange("p (r g c) -> p g r c", r=3, g=2),
               PS[2][:, 0:384].rearrange("p (g r c) -> p g r c", g=2, r=3), [PS[2]], [Sf[db]])
            cp("act", Sf[db][:, 384:512], PS[2][:, 384:512], [PS[2]], [Sf[db]])
            cp("act", Vdf[db][:, :, 0:64], PS[3][:, 0:256].rearrange("p (h c) -> p h c", c=64), [PS[3]], [Vdf[db]])
            cp("act", Vsf[db][:, :, 0:64], PS[3][:, 256:384].rearrange("p (h c) -> p h c", c=64), [PS[3]], [Vsf[db]])
            if cut <= 4:
                continue
            cr = Craw[db]
            cp("act", cr[:, 0:416], PS[0][:, 0:416], [PS[0]], [cr])
            ttr(junk, junk[:, 0:256], cr[:, 0:256], cr[:, 0:256], S_[:, 4:5], [cr], [S_])
            ttr(junk, junk[:, 0:128], cr[:, 256:384], cr[:, 256:384], S_[:, 8:9], [cr], [S_])
            rstd_ops(S_, 4, 256)
            rstd_ops(S_, 8, 128)
            stt("dve", Cn[db][:, 0:256], cr[:, 0:256], S_[:, 6:7], Gq[:, :], ALU.mult, ALU.mult, [cr, S_, Gq], [Cn[db]])
            stt("dve", Cn[db][:, 256:384], cr[:, 256:384], S_[:, 10:11], Gkv[:, :], ALU.mult, ALU.mult,
                [cr, S_, Gkv], [Cn[db]])
            rope(cr[:, 384:416].unsqueeze(1), krot[db][:, :].unsqueeze(1), 1, i, db, cr, krot[db])
            if cut <= 5:
                continue
            for k in range(3):
                tr(psb(7)[:, k * 128:(k + 1) * 128], Cn[db][:, k * 128:(k + 1) * 128], ident_b[:, :], [Cn[db], ident_b], [PS[7]])
            cp("act", CT[db][:, :], psb(7)[:, 0:384], [PS[7]], [CT[db]])
            if cut <= 6:
                continue
            for half in range(2):
                for k in range(2):
                    mm(PS[4 + half][:, 0:288], CT[db][:, k * 128:(k + 1) * 128], Wqb[:, k, half * 288:(half + 1) * 288],
                       k == 0, k == 1, [CT[db], Wqb], [PS[4 + half]])
            for half in range(2):
                mm(PS[half][:, 0:384], CT[db][:, 256:384], Wkvb[:, half * 384:(half + 1) * 384], True, True,
                   [CT[db], Wkvb], [PS[half]])
            if cut <= 7:
                continue
            for half in range(2):
                q3 = PS[4 + half][:, 0:288].rearrange("p (h c) -> p h c", c=96)
                cp("act", Qf[db][:, 3 * half:3 * half + 3, 0:64], q3[:, :, 0:64], [PS[4 + half]], [Qf[db]])
                if cut <= 7.2:
                    continue
                rope(q3[:, :, 64:96], Qf[db][:, 3 * half:3 * half + 3, 64:96], 3, i, db, PS[4 + half], Qf[db])
                if cut <= 7.4:
                    continue
                kv3 = PS[half][:, 0:384].rearrange("p (h c) -> p h c", c=128)
                cp("act", Kf[db][:, 3 * half:3 * half + 3, 0:64], kv3[:, :, 0:64], [PS[half]], [Kf[db]])
                cp("dve", Vf[db][:, 3 * half:3 * half + 3, 0:64], kv3[:, :, 64:128], [PS[half]], [Vf[db]])
            if cut <= 7.6:
                continue
            cp("pool", Kf[db][:, :, 64:96], krot[db][:, :].unsqueeze(1).broadcast_to([128, 6, 32]), [krot[db]], [Kf[db]])
            if cut <= 8:
                continue
            for h in range(6):
                tr(psb(6)[0:96, h * 128:(h + 1) * 128], Qf[db][:, h, :], ident_b[:, :], [Qf[db], ident_b], [PS[6]])
            cp("act", QT[db][0:96, :], psb(6)[0:96, 0:768], [PS[6]], [QT[db]])
            for h in range(6):
                tr(psb(7)[0:96, h * 128:(h + 1) * 128], Kf[db][:, h, :], ident_b[:, :], [Kf[db], ident_b], [PS[7]])
            cp("dve", KT[db][0:96, :], psb(7)[0:96, 0:768], [PS[7]], [KT[db]])
            if cut <= 9:
                continue
            dma("pool", DAP(QTm, i * 128, (T, 96), (96 * T, 6), (1, 128)),
                QT[db][0:96, :].rearrange("p (h c) -> p h c", c=128), [QT[db]], [])
            dma("pool", DAP(kvown, OFF_M + i * 128, (T, 96), (RM * T, 6), (1, 128)),
                KT[db][0:96, :].rearrange("p (h c) -> p h c", c=128), [KT[db]], [])
            dma("pool", DAP(kvown, OFF_M + 96 * T + i * 65, (VW, 128), (RM * T, 6), (1, 65)), Vf[db][:, :, :], [Vf[db]], [])
            if cut <= 10:
                continue
            for m in range(4):
                tr(psb(6)[:, m * 128:(m + 1) * 128], Df[db][:, m * 128:(m + 1) * 128], ident_b[:, :], [Df[db], ident_b], [PS[6]])
            cp("act", DT[db][:, :], psb(6)[:, 0:512], [PS[6]], [DT[db]])
            dma("pool", DAP(QTd, i * 128, (T, 128), (128 * T, 2), (1, 128)),
                DT[db][:, 0:256].rearrange("p (h c) -> p h c", c=128), [DT[db]], [])
            for ph in range(2):
                dma("pool", DAP(kvown, OFF_D + ph * RD * T + i * 128, (T, 64), (2 * RD * T, 2), (1, 128)),
                    DT[db][64 * ph:64 * ph + 64, 256:512].rearrange("p (h c) -> p h c", c=128), [DT[db]], [])
            dma("pool", DAP(kvown, OFF_D + 64 * T + i * 65, (VW, 128), (RD * T, 4), (1, 65)), Vdf[db][:, :, :], [Vdf[db]], [])
            if cut <= 11:
                continue
            for r in range(4):
                tr(psb(7)[:, r * 128:(r + 1) * 128], Sf[db][:, r * 128:(r + 1) * 128], ident_b[:, :], [Sf[db], ident_b], [PS[7]])
            cp("dve", ST[db][:, :], psb(7)[:, 0:512], [PS[7]], [ST[db]])
            dma("pool", DAP(QTs, i * 384, (3 * T, 128), (1, 384)), ST[db][:, 0:384], [ST[db]], [])
            dma("pool", DAP(kvown, OFF_S0 + i * 128, (T, 128), (1, 128)), ST[db][:, 384:512], [ST[db]], [])
            dma("pool", DAP(kvown, OFF_S1 + i * 130, (2 * VW, 128), (1, 130)),
                Vsf[db][:, :, :].rearrange("p h c -> p (h c)"), [Vsf[db]], [])

    def phase_gather(l):
        P.barrier()
        for k, (nm, r0, nr) in enumerate(PIECES):
            o = Op("pool", lambda eng, nm=nm, r0=r0, nr=nr: eng.collective_compute(
                "AllGather", ALU.bypass, replica_groups=[[2 * i, 2 * i + 1] for i in range(ncores // 2)],
                ins=[kvown[r0:r0 + nr, :].opt()], outs=[kvall[nm].ap().opt()]), False)
            o.signal = True
            o.cc = ccsems[l * len(PIECES) + k]
            P.ops["pool"].append(o)
            pbuf[nm].ws = [o]
            pbuf[nm].r = []

    def kloc(j):
        if j == 0:
            return 0, 0
        return (j - 1) % 2, 1 + (j - 1) // 2

    def ffn_cast_jobs(l):
        jobs = []
        for (wsrc, wdst) in ((w_gate, Wb_g), (w_up, Wb_u)):
            for k in range(8):
                jobs.append((wsrc[l, k * 128:(k + 1) * 128, :], DFF,
                             DAP(wdst, l * NF * 128 * 1024 + k * 128, (1024, 128), (128 * 1024, NF), (1, 128)), True))
        for f in range(NF):
            jobs.append((w_down[l, f * 128:(f + 1) * 128, :], D, DAP(Wb_d, l * 128 * NF * D + f * D, (NF * D, 128), (1, D)), False))
        return jobs

    def background_setup(l):
        stg = [AFa.alloc(2816, "bgstg%d" % i) for i in range(2)]
        stb = [ABa.alloc(2816, "bgstb%d" % i) for i in range(2)]
        return {"jobs": ffn_cast_jobs(l), "stg": stg, "stb": stb, "n": 0, "tick": 0}

    def bg_one(bg):
        if not bg["jobs"]:
            return
        src, ncols, dst, blocked = bg["jobs"].pop(0)
        i = bg["n"] % 2
        bg["n"] += 1
        stg, stb = bg["stg"][i], bg["stb"][i]
        dma("act", stg[:, 0:ncols], src, [], [stg])
        cp("pool", stb[:, 0:ncols], stg[:, 0:ncols], [stg], [stb])
        if blocked:
            dma("pool", dst, stb[:, 0:ncols].rearrange("p (f c) -> p f c", c=128), [stb], [])
        else:
            dma("pool", dst, stb[:, 0:ncols], [stb], [])

    def bg_step(bg):
        bg["tick"] += 1
        if bg["tick"] % 12 == 0:
            bg_one(bg)

    def bg_finish(bg):
        while bg["jobs"]:
            bg_one(bg)

    def attention_core(units, lookahead=2, post_delay=1):
        n = len(units)
        for t in range(n + lookahead + post_delay):
            if t < n:
                units[t]["s"]()
                units[t]["e"]()
            if 0 <= t - lookahead < n:
                units[t - lookahead]["o"]()
            tp_ = t - lookahead - post_delay
            if 0 <= tp_ < n and units[tp_].get("post"):
                units[tp_]["post"]()

    def groups():
        gl = [(-1, 0, 128, [0])]
        for g in range(8):
            gl.append((g, (1 + 4 * g) * 128, 512, list(range(0, 8 * g + 9))))
        return gl

    def pair_view(t_ap, nk, N):
        if N == 512:
            return t_ap[0:nk, 0:1024]
        return t_ap[0:nk, 0:1024].rearrange("p (c n) -> p c n", c=2)[:, :, 0:N]

    def phase_attn_mla(l):
        AFa.reset(); ABa.reset()
        mA = AFa.alloc(9 * 512, "mA", "p (s c) -> p s c", c=512)
        dma("sp", mA[:, :, :], mA_in[:, :].rearrange("p (s c) -> p s c", c=512), [], [mA])
        KTt = [ABa.alloc(2 * T, "KTt%d" % i) for i in range(2)]
        Vt = [ABa.alloc(2 * VW, "Vt%d" % i) for i in range(2)]
        QTt = [ABa.alloc(T, "QTt%d" % i) for i in range(2)]
        Pt = [ABa.alloc(1024, "Pt%d" % i) for i in range(3)]
        yo = [ABa.alloc(512, "yo%d" % i) for i in range(2)]
        rd = [AFa.alloc(512, "rd%d" % i) for i in range(2)]
        bcs = [AFa.alloc(512, "bcs%d" % i) for i in range(2)]
        bg = background_setup(l)
        cnt = {"s": 0, "p": 0, "o": 0, "y": 0}
        for h in range(6):
            hb = h % 2
            for r in range(NR):
                dma("sp", KTt[hb][0:96, r * T:(r + 1) * T], DAP(kvall["m%d" % h], r * RM * T, (T, 96), (1, T)),
                    [pbuf["m%d" % h]], [KTt[hb]])
                dma("sp", Vt[hb][:, r * VW:(r + 1) * VW], DAP(kvall["m%d" % h], r * RM * T + 96 * T, (VW, 128), (1, VW)),
                    [pbuf["m%d" % h]], [Vt[hb]])
            dma("sp", QTt[hb][0:96, :], DAP(QTm, h * 96 * T, (T, 96), (1, T)), [], [QTt[hb]])
            units = []
            for (g, q0, N, js) in groups():
                ob = cnt["o"] % 2
                cnt["o"] += 1
                Ob = PS[4 + ob]
                subs = [[js[0]]] + [js[k:k + 2] for k in range(1, len(js), 2)]
                for jl in subs:
                    sp_ = cnt["s"] % 2
                    cnt["s"] += 1
                    pt = Pt[cnt["p"] % 3]
                    cnt["p"] += 1
                    nk = 16 if jl[0] == 0 else 128
                    info = []
                    for j in jl:
                        rk, lc = kloc(j)
                        info.append((rk * T + lc * 128, rk * VW + lc * 65))
                    if g == -1:
                        mk = mA[0:nk, 8, 0:N]
                    elif jl[0] >= 8 * g + 1:
                        s0 = jl[0] - (8 * g + 1)
                        mk = mA[0:nk, s0:s0 + len(jl), :].rearrange("p s c -> p (s c)")
                    else:
                        mk = None
                    banks = [PS[2 * sp_ + bi] for bi in range(len(jl))]
                    u = {}

                    def s_(banks=banks, info=info, nk=nk, q0=q0, N=N, hb=hb):
                        for bi, (kc, vc) in enumerate(info):
                            mm(banks[bi][0:nk, 0:N], KTt[hb][0:96, kc:kc + nk], QTt[hb][0:96, q0:q0 + N], True, True,
                               [KTt[hb], QTt[hb]], [banks[bi]])

                    def e_(banks=banks, sp_=sp_, pt=pt, nk=nk, N=N, mk=mk, nb=len(jl)):
                        if nb == 2:
                            src = pair_view(psum_t[:, 2 * sp_ * 512:(2 * sp_ + 2) * 512], nk, N)
                            dst = pair_view(pt[:, :], nk, N)
                        else:
                            src = banks[0][0:nk, 0:N]
                            dst = pt[0:nk, 0:N]
                        act(dst, src, AF.Exp, banks, [pt], scale=SC_MLA)
                        if mk is not None:
                            tt("dve", dst, dst, mk, ALU.mult, [pt, mA], [pt])

                    def o_(Ob=Ob, pt=pt, nk=nk, info=info, N=N, hb=hb, jl=jl, js=js):
                        for bi, (kc, vc) in enumerate(info):
                            mm(Ob[0:65, 0:N], Vt[hb][0:nk, vc:vc + 65], pt[0:nk, bi * 512:bi * 512 + N],
                               jl[bi] == js[0], jl[bi] == js[-1], [Vt[hb], pt], [Ob])
                        bg_step(bg)

                    u["s"], u["e"], u["o"] = s_, e_, o_
                    if jl[-1] == js[-1]:
                        yb = cnt["y"] % 2
                        cnt["y"] += 1

                        def post(Ob=Ob, N=N, q0=q0, yb=yb, h=h):
                            act(rd[yb][64:65, 0:N], Ob[64:65, 0:N], AF.Ln, [Ob], [rd[yb]])
                            act(rd[yb][64:65, 0:N], rd[yb][64:65, 0:N], AF.Exp, [rd[yb]], [rd[yb]], scale=-1.0)
                            mm(PS[6][0:64, 0:N], ones_f[64:65, 0:64], rd[yb][64:65, 0:N], True, True, [ones_f, rd[yb]], [PS[6]])
                            cp("dve", bcs[yb][0:64, 0:N], PS[6][0:64, 0:N], [PS[6]], [bcs[yb]])
                            tt("dve", yo[yb][0:64, 0:N], Ob[0:64, 0:N], bcs[yb][0:64, 0:N], ALU.mult, [Ob, bcs[yb]], [yo[yb]])
                            dma("pool", DAP(yT, h * 64 * T + q0, (T, 64), (1, N)), yo[yb][0:64, 0:N], [yo[yb]], [])

                        u["post"] = post
                    units.append(u)
            attention_core(units, lookahead=1)
        bg_finish(bg)

    def phase_attn_diff(l):
        AFa.reset(); ABa.reset()
        KD = [ABa.alloc(2 * T, "KD%d" % i) for i in range(2)]
        VD = [ABa.alloc(2 * VW, "VD%d" % i) for i in range(2)]
        QD = [ABa.alloc(T, "QD%d" % i) for i in range(2)]
        Pt = [ABa.alloc(1024, "Pt%d" % i) for i in range(3)]
        yo = [ABa.alloc(512, "yo%d" % i) for i in range(2)]
        MD = AFa.alloc(NME * 128, "MD")
        Ef = [AFa.alloc(1024, "Ef%d" % i) for i in range(2)]
        acc = [AFa.alloc(1024, "acc%d" % i) for i in range(2)]
        accd = [AFa.alloc(1024, "accd%d" % i) for i in range(2)]
        r0 = AFa.alloc(512, "r0")
        r1 = AFa.alloc(512, "r1")
        X = AFa.alloc(512, "X")
        d0 = AFa.alloc(512, "d0")
        d1 = AFa.alloc(512, "d1")
        rs = AFa.alloc(512, "rs")
        cnt = {"s": 0, "p": 0, "e": 0, "o": 0}
        for h in range(4):
            hb = h % 2
            for r in range(NR):
                dma("sp", KD[hb][0:64, r * T:(r + 1) * T],
                    DAP(kvall["d%d" % h], r * RD * T, (T, 64), (1, T)), [pbuf["d%d" % h]], [KD[hb]])
                dma("sp", VD[hb][:, r * VW:(r + 1) * VW], DAP(kvall["d%d" % h], r * RD * T + 64 * T, (VW, 128), (1, VW)),
                    [pbuf["d%d" % h]], [VD[hb]])
            dma("sp", QD[hb][0:64, :], DAP(QTd, 64 * h * T, (T, 64), (1, T)), [], [QD[hb]])
            dma("sp", MD[:, :], DAP(maskD, h * 128 * NME * 128, (NME * 128, 128), (1, NME * 128)), [], [MD])
            units = []
            for (g, q0, N, js) in groups():
                ob = cnt["o"] % 2
                cnt["o"] += 1
                Ob = PS[4 + ob]
                ac = acc[ob]
                acd = accd[ob]
                for ji, j in enumerate(js):
                    rk, lc = kloc(j)
                    nk = 16 if j == 0 else 128
                    kc = rk * T + lc * 128
                    vc = rk * VW + lc * 65
                    sp_ = cnt["s"] % 2
                    cnt["s"] += 1
                    pt = Pt[cnt["p"] % 3]
                    cnt["p"] += 1
                    if g == -1:
                        mk = MD[0:nk, 40 * 128:40 * 128 + N]
                    elif g == 0 and j == 0:
                        mk = MD[0:nk, 36 * 128:36 * 128 + N]
                    elif j >= 8 * g and j > 0:
                        s0 = j - 8 * g
                        mk = MD[0:nk, s0 * 512:s0 * 512 + N]
                    else:
                        mk = None
                    banks = [PS[2 * sp_], PS[2 * sp_ + 1]]
                    first = (j == js[0])
                    last = (j == js[-1])
                    u = {}

                    def s_(banks=banks, nk=nk, kc=kc, q0=q0, N=N, hb=hb):
                        for c in range(2):
                            mm(banks[c][0:nk, 0:N], KD[hb][32 * c:32 * c + 32, kc:kc + nk], QD[hb][32 * c:32 * c + 32, q0:q0 + N],
                               True, True, [KD[hb], QD[hb]], [banks[c]])

                    def e_(banks=banks, sp_=sp_, pt=pt, nk=nk, N=N, mk=mk, ac=ac, acd=acd, first=first, ji=ji):
                        src = pair_view(psum_t[:, 2 * sp_ * 512:(2 * sp_ + 2) * 512], nk, N)
                        dst = pair_view(pt[:, :], nk, N)
                        if mk is None:
                            act(dst, src, AF.Exp, banks, [pt], scale=SC_DIFF)
                        else:
                            ef = Ef[cnt["e"] % 2]
                            cnt["e"] += 1
                            efv = pair_view(ef[:, :], nk, N)
                            act(efv, src, AF.Exp, banks, [ef], scale=SC_DIFF)
                            for c in range(2):
                                tt("dve", pt[0:nk, c * 512:c * 512 + N], ef[0:nk, c * 512:c * 512 + N], mk, ALU.mult, [ef, MD], [pt])
                        if first:
                            memset("pool", ac[:, :], 0.0, [ac])
                            memset("dve", acd[:, :], 0.0, [acd])
                        if ji % 2 == 0:
                            av = pair_view(ac[:, :], nk, N)
                            tt("pool", av, av, dst, ALU.add, [ac, pt], [ac])
                        else:
                            av = pair_view(acd[:, :], nk, N)
                            tt("dve", av, av, dst, ALU.add, [acd, pt], [acd])

                    def o_(Ob=Ob, pt=pt, nk=nk, vc=vc, N=N, hb=hb, first=first, last=last):
                        for c in range(2):
                            mm(Ob[64 * c:64 * c + 64, 0:N], VD[hb][0:nk, vc:vc + 64], pt[0:nk, c * 512:c * 512 + N], first, last,
                               [VD[hb], pt], [Ob], tp=(0, 64 * c))

                    u["s"], u["e"], u["o"] = s_, e_, o_
                    if last:
                        def post(N=N, q0=q0, h=h, Ob=Ob, ac=ac, acd=acd):
                            mm(PS[6][:, 0:N], ones_f[:, 0:128], ac[:, 0:N], True, False, [ones_f, ac], [PS[6]])
                            mm(PS[6][:, 0:N], ones_f[:, 0:128], acd[:, 0:N], False, True, [ones_f, acd], [PS[6]])
                            mm(PS[7][:, 0:N], ones_f[:, 0:128], ac[:, 512:512 + N], True, False, [ones_f, ac], [PS[7]])
                            mm(PS[7][:, 0:N], ones_f[:, 0:128], acd[:, 512:512 + N], False, True, [ones_f, acd], [PS[7]])
                            act(r0[0:64, 0:N], PS[6][0:64, 0:N], AF.Ln, [PS[6]], [r0])
                            act(r0[0:64, 0:N], r0[0:64, 0:N], AF.Exp, [r0], [r0], scale=-1.0)
                            act(r1[64:128, 0:N], PS[7][64:128, 0:N], AF.Ln, [PS[7]], [r1])
                            act(r1[64:128, 0:N], r1[64:128, 0:N], AF.Exp, [r1], [r1], scale=-1.0)
                            tt("dve", X[0:64, 0:N], Ob[0:64, 0:N], r0[0:64, 0:N], ALU.mult, [Ob, r0], [X])
                            stt("dve", X[64:128, 0:N], Ob[64:128, 0:N], lam_t[64:128, l:l + 1], r1[64:128, 0:N], ALU.mult, ALU.mult,
                                [Ob, lam_t, r1], [X])
                            mm(PS[6][0:64, 0:N], sel_f[:, 0:64], X[:, 0:N], True, True, [sel_f, X], [PS[6]])
                            cp("dve", d0[0:64, 0:N], PS[6][0:64, 0:N], [PS[6]], [d0])
                            tt("pool", d1[0:64, 0:N], d0[0:64, 0:N], d0[0:64, 0:N], ALU.mult, [d0], [d1])
                            mm(PS[7][0:64, 0:N], ones_f[0:64, 0:64], d1[0:64, 0:N], True, True, [ones_f, d1], [PS[7]])
                            act(rs[0:64, 0:N], PS[7][0:64, 0:N], AF.Ln, [PS[7]], [rs], scale=1.0 / 64, bias=EPS)
                            act(rs[0:64, 0:N], rs[0:64, 0:N], AF.Exp, [rs], [rs], scale=-0.5)
                            yb = yo[(q0 // 128) % 2]
                            stt("dve", yb[0:64, 0:N], d0[0:64, 0:N], gsub_t[0:64, l:l + 1], rs[0:64, 0:N], ALU.mult, ALU.mult,
                                [d0, gsub_t, rs], [yb])
                            dma("pool", DAP(yT, (384 + h * 64) * T + q0, (T, 64), (1, N)), yb[0:64, 0:N], [yb], [])

                        u["post"] = post
                    units.append(u)
            attention_core(units, lookahead=1, post_delay=1)

    def phase_attn_swa(l):
        AFa.reset(); ABa.reset()
        KS = ABa.alloc(2 * T, "KS")
        VS = ABa.alloc(2 * 2 * VW, "VS")
        QS = ABa.alloc(3 * T, "QS")
        Pt = [ABa.alloc(384, "Pt%d" % i) for i in range(4)]
        yo = [ABa.alloc(384, "yo%d" % i) for i in range(2)]
        MS = AFa.alloc(2 * 4 * 384, "MS", "p (g e c) -> p g e c", g=2, e=4)
        MSq = AFa.alloc(6 * 128, "MSq")
        Ef = [AFa.alloc(384, "Ef%d" % i) for i in range(2)]
        rd = [AFa.alloc(384, "rd%d" % i) for i in range(2)]
        bcs = [AFa.alloc(384, "bcs%d" % i) for i in range(2)]
        for r in range(NR):
            dma("sp", KS[:, r * T:(r + 1) * T], DAP(kvall["s0"], r * RS0 * T, (T, 128), (1, T)), [pbuf["s0"]], [KS])
            dma("sp", VS[:, r * 2 * VW:(r + 1) * 2 * VW], DAP(kvall["s1"], r * RS1 * T, (2 * VW, 128), (1, 2 * VW)), [pbuf["s1"]], [VS])
        dma("sp", QS[:, :], DAP(QTs, 0, (3 * T, 128), (1, 3 * T)), [], [QS])
        dma("sp", MS[:, :, :, :], maskS[:, :].rearrange("p (g e c) -> p g e c", g=2, e=4), [], [MS])
        dma("sp", MSq[0:16, :], DAP(zper, 0, (768, 16), (1, 768)), [], [MSq])
        cnt = {"s": 0, "p": 0, "e": 0, "o": 0, "y": 0}
        units = []
        for i in range(NLOC):
            for g in range(2):
                kts = []
                if i == 0:
                    kts.append((0, MSq[0:16, 3 * g * 128:(3 * g + 3) * 128]))
                else:
                    kts.append((0, MS[0:16, g, 3, :] if i == 1 else None))
                    for s in range(3):
                        j = 2 * (i - 1) + s
                        if j == 0 or j > 64:
                            continue
                        kts.append((j, MS[:, g, s, :]))
                ob = cnt["o"] % 2
                cnt["o"] += 1
                Ob = PS[3 + ob]
                for ti, (j, mk) in enumerate(kts):
                    rk, lc = kloc(j)
                    nk = 16 if j == 0 else 128
                    kc = rk * T + lc * 128
                    vc = rk * 2 * VW + lc * 130 + g * 65
                    sb = PS[cnt["s"] % 3]
                    cnt["s"] += 1
                    pt = Pt[cnt["p"] % 4]
                    cnt["p"] += 1
                    u = {}

                    def s_(sb=sb, nk=nk, kc=kc, i=i, g=g):
                        mm(sb[0:nk, 0:384], KS[64 * g:64 * g + 64, kc:kc + nk], QS[64 * g:64 * g + 64, i * 384:(i + 1) * 384],
                           True, True, [KS, QS], [sb])

                    def e_(sb=sb, pt=pt, nk=nk, mk=mk):
                        if mk is None:
                            act(pt[0:nk, 0:384], sb[0:nk, 0:384], AF.Exp, [sb], [pt], scale=SC_SWA)
                        else:
                            ef = Ef[cnt["e"] % 2]
                            cnt["e"] += 1
                            act(ef[0:nk, 0:384], sb[0:nk, 0:384], AF.Exp, [sb], [ef], scale=SC_SWA)
                            tt("dve", pt[0:nk, 0:384], ef[0:nk, 0:384], mk, ALU.mult, [ef, MS, MSq], [pt])

                    def o_(Ob=Ob, pt=pt, nk=nk, vc=vc, first=(ti == 0), last=(ti == len(kts) - 1)):
                        mm(Ob[0:65, 0:384], VS[0:nk, vc:vc + 65], pt[0:nk, 0:384], first, last, [VS, pt], [Ob])

                    u["s"], u["e"], u["o"] = s_, e_, o_
                    if ti == len(kts) - 1:
                        yb = cnt["y"] % 2
                        cnt["y"] += 1

                        def post(Ob=Ob, i=i, g=g, yb=yb):
                            sr = sinkrow[64:65, l * 6 + 3 * g:l * 6 + 3 * g + 3, :].rearrange("p a b -> p (a b)")
                            tt("dve", rd[yb][64:65, 0:384], Ob[64:65, 0:384], sr, ALU.add, [Ob, sinkrow], [rd[yb]])
                            P.op("dve", lambda eng: eng.reciprocal(rd[yb][64:65, 0:384], rd[yb][64:65, 0:384]), [rd[yb]], [rd[yb]])
                            mm(PS[5][0:64, 0:384], ones_f[64:65, 0:64], rd[yb][64:65, 0:384], True, True, [ones_f, rd[yb]], [PS[5]])
                            cp("dve", bcs[yb][0:64, 0:384], PS[5][0:64, 0:384], [PS[5]], [bcs[yb]])
                            tt("dve", yo[yb][0:64, 0:384], Ob[0:64, 0:384], bcs[yb][0:64, 0:384], ALU.mult, [Ob, bcs[yb]], [yo[yb]])
                            dma("pool", DAP(yT, (640 + 3 * g * 64) * T + i * 128, (T, 64), (64 * T, 3), (1, 128)),
                                yo[yb][0:64, 0:384].rearrange("p (r c) -> p r c", c=128), [yo[yb]], [])

                        u["post"] = post
                    units.append(u)
        attention_core(units)

    def phase_ffn(l, last):
        AFa.reset(); ABa.reset()
        Wd = ABa.alloc(NF * D, "Wd", "p (f c) -> p f c", c=D)
        Wo = ABa.alloc(8 * D, "Wo", "p (k c) -> p k c", c=D)
        for f0 in range(0, NF, 6):
            f1 = min(NF, f0 + 6)
            dma("sp", Wd[:, f0:f1, :], DAP(Wb_d, l * 128 * NF * D + f0 * D, (NF * D, 128), (D, f1 - f0), (1, D)), [], [Wd])
        dma("sp", Wo[:, :, :], DAP(Wb_out, l * 128 * 8 * D, (8 * D, 128), (D, 8), (1, D)), [], [Wo])
        AT = ABa.alloc(NF * 512, "AT", "p (f c) -> p f c", c=512)
        HnT = ABa.alloc(8 * 512, "HnT", "p (k c) -> p k c", c=512)
        WGU = [ABa.alloc(2 * 1024, "WGU%d" % i, "p (u k c) -> p u k c", u=2, k=8) for i in range(2)]
        YT = [ABa.alloc(1024, "YT%d" % i, "p (k c) -> p k c", c=128) for i in range(2)]
        Hn = [ABa.alloc(1024, "Hn%d" % i) for i in range(2)]
        HT = AFa.alloc(4 * 1024, "HT", "p (b c) -> p b c", c=1024)
        Gff = AFa.alloc(1024, "Gff")
        dma("sp", Gff[:, :], DAP(ffn_norm, l * D, (0, 128), (1, D)), [], [Gff])
        if last:
            Gfin = AFa.alloc(1024, "Gfin")
            dma("sp", Gfin[:, :], DAP(final_norm, 0, (0, 128), (1, D)), [], [Gfin])
            OUTT = [AFa.alloc(1024, "OUTT%d" % i) for i in range(2)]
        SG = [AFa.alloc(512, "SG%d" % i) for i in range(2)]
        junk = AFa.alloc(1024, "junk")
        st = [AFa.alloc(16, "st%d" % i) for i in range(2)]
        tiles = [[0]] + [list(range(1 + 4 * t, 5 + 4 * t)) for t in range(8)]
        cnt = {"y": 0, "w": 0, "g": 0, "o": 0}
        for bl in tiles:
            nb = len(bl)
            NT = nb * 128
            for bi, i in enumerate(bl):
                yb = cnt["y"] % 2
                cnt["y"] += 1
                dma("sp", HT[:, bi, :], hsrc(l, i), [], [HT])
                dma("sp", YT[yb][:, :, :], DAP(yT, i * 128, (T, 128), (128 * T, 8), (1, 128)), [], [YT[yb]])
                for half in range(2):
                    for k in range(8):
                        mm(PS[half][:, 0:512], YT[yb][:, k, :], Wo[:, k, half * 512:(half + 1) * 512], k == 0, k == 7,
                           [YT[yb], Wo], [PS[half]])
                for half in range(2):
                    tt("dve", HT[:, bi, half * 512:(half + 1) * 512], HT[:, bi, half * 512:(half + 1) * 512], PS[half][:, 0:512],
                       ALU.add, [HT, PS[half]], [HT])
                S_ = st[yb]
                ttr(junk, junk[:, :], HT[:, bi, :], HT[:, bi, :], S_[:, 0:1], [HT], [S_])
                rstd_ops(S_, 0, 1024)
                stt("dve", Hn[yb][:, :], HT[:, bi, :], S_[:, 2:3], Gff[:, :], ALU.mult, ALU.mult, [HT, S_, Gff], [Hn[yb]])
                for k in range(8):
                    tr(psb(2)[:, k * 128:(k + 1) * 128], Hn[yb][:, k * 128:(k + 1) * 128], ident_b[:, :], [Hn[yb], ident_b], [PS[2]])
                cp("act", HnT[:, :, bi * 128:(bi + 1) * 128], psb(2)[:, :].rearrange("p (k c) -> p k c", c=128), [PS[2]], [HnT])
            for f in range(NF):
                wb = cnt["w"] % 2
                cnt["w"] += 1
                dma("sp", WGU[wb][:, 0, :, :], DAP(Wb_g, (l * NF + f) * 128 * 1024, (1024, 128), (128, 8), (1, 128)), [], [WGU[wb]])
                dma("sp", WGU[wb][:, 1, :, :], DAP(Wb_u, (l * NF + f) * 128 * 1024, (1024, 128), (128, 8), (1, 128)), [], [WGU[wb]])
                gb = cnt["g"] % 2
                cnt["g"] += 1
                PG, PU = PS[3 + 2 * gb], PS[4 + 2 * gb]
                for k in range(8):
                    mm(PG[:, 0:NT], WGU[wb][:, 0, k, :], HnT[:, k, 0:NT], k == 0, k == 7, [WGU[wb], HnT], [PG])
                for k in range(8):
                    mm(PU[:, 0:NT], WGU[wb][:, 1, k, :], HnT[:, k, 0:NT], k == 0, k == 7, [WGU[wb], HnT], [PU])
                act(SG[gb][:, 0:NT], PG[:, 0:NT], AF.Silu, [PG], [SG[gb]])
                tt("dve", AT[:, f, 0:NT], SG[gb][:, 0:NT], PU[:, 0:NT], ALU.mult, [SG[gb], PU], [AT])
            for bi, i in enumerate(bl):
                for half in range(2):
                    for f in range(NF):
                        mm(PS[half][:, 0:512], AT[:, f, bi * 128:(bi + 1) * 128], Wd[:, f, half * 512:(half + 1) * 512],
                           f == 0, f == NF - 1, [AT, Wd], [PS[half]])
                for half in range(2):
                    tt("dve", HT[:, bi, half * 512:(half + 1) * 512], HT[:, bi, half * 512:(half + 1) * 512], PS[half][:, 0:512],
                       ALU.add, [HT, PS[half]], [HT])
                if not last:
                    dma("pool", hres[i, :, :], HT[:, bi, :], [HT], [])
                if last and i >= 1:
                    ob = cnt["o"] % 2
                    cnt["o"] += 1
                    S_ = st[ob]
                    ttr(junk, junk[:, :], HT[:, bi, :], HT[:, bi, :], S_[:, 4:5], [HT], [S_])
                    rstd_ops(S_, 4, 1024)
                    stt("dve", OUTT[ob][:, :], HT[:, bi, :], S_[:, 6:7], Gfin[:, :], ALU.mult, ALU.mult, [HT, S_, Gfin], [OUTT[ob]])
                    dma("pool", out[i - 1, :, :], OUTT[ob][:, :], [OUTT[ob]], [])

    steps = [("weights", phase_weights), ("masks", lambda: (phase_masks(), phase_init()))]
    for l in range(2):
        steps.append(("proj%d" % l, lambda l=l: phase_proj(l)))
        steps.append(("gather%d" % l, lambda l=l: phase_gather(l)))
        steps.append(("mla%d" % l, lambda l=l: phase_attn_mla(l)))
        steps.append(("diff%d" % l, lambda l=l: phase_attn_diff(l)))
        steps.append(("swa%d" % l, lambda l=l: phase_attn_swa(l)))
        steps.append(("ffn%d" % l, lambda l=l: phase_ffn(l, l == 1)))
    barrier_before = ("masks", "proj0", "proj1", "ffn0", "ffn1", "diff0", "diff1", "swa0", "swa1")
    for si, (name, fn) in enumerate(steps):
        if stop is not None and si >= stop:
            break
        if name in barrier_before:
            P.barrier()
        fn()
    P.barrier()
    if debug and "kvown" in dump:
        dma("sp", kvodbg[:, :], kvown[:, :], [], [])
        P.barrier()

    ninst = emit_all(P, sems, dsems)
    for cm in reversed(ctx):
        cm.__exit__(None, None, None)
    return nc, ninst


def emit_all(P, sems, dsems):
    nc = P.nc
    engs = {"pe": nc.tensor, "act": nc.scalar, "dve": nc.vector, "pool": nc.gpsimd, "sp": nc.sync}
    for e in Prog.ENG:
        c = 0
        nd = 0
        for o in P.ops[e]:
            if o.dma:
                o.sem = dsems[e][nd % NDMASEM]
                o.val = 16 * (nd // NDMASEM + 1)
                nd += 1
            elif o.cc is not None:
                o.sem = o.cc
                o.val = 1
            elif o.signal:
                c += 1
                o.sem = sems[e]
                o.val = c
    ninst = 0
    for e in Prog.ENG:
        eng = engs[e]
        waited = {}
        for o in P.ops[e]:
            for d in o.deps:
                key = id(d.sem)
                if waited.get(key, 0) < d.val:
                    eng.wait_ge(d.sem, d.val)
                    waited[key] = d.val
                    ninst += 1
            if o.fn is not None:
                ins = o.fn(eng)
                ninst += 1
                if o.dma:
                    ins.then_inc(o.sem, 16)
                elif o.cc is not None:
                    ins.then_inc(o.sem)
                elif o.signal:
                    ins.then_inc(o.sem, 1)
    return ninst


_CACHE = {}


def _get_program(debug=False):
    if debug not in _CACHE:
        _CACHE[debug] = build_program(debug)
    return _CACHE[debug]


PARAM_NAMES = ["meta_tokens", "rel_bias", "attn_norm", "w_in", "mla_q_norm", "mla_w_qb", "mla_kv_norm", "mla_w_kvb",
               "diff_lambda", "diff_subln", "swa_sinks", "w_out", "ffn_norm", "w_gate", "w_up", "w_down", "final_norm"]


def make_in_maps(inputs):
    x = np.ascontiguousarray(np.asarray(inputs["x"], dtype=np.float32))
    params = {k: np.ascontiguousarray(np.asarray(inputs[k], dtype=np.float32)) for k in PARAM_NAMES}
    consts = [make_consts(r) for r in range(NR)]
    in_maps = []
    for c in range(8):
        b, r = c // 2, c % 2
        xb = x[b].reshape(64, 128, D)[r::2]
        m = {"xin": np.ascontiguousarray(xb)}
        m.update(params)
        m.update(consts[r])
        in_maps.append(m)
    return in_maps


def kernel(**inputs):
    nc, _ = _get_program(False)
    in_maps = make_in_maps(inputs)
    res = run_bass_kernel_spmd(nc, in_maps, core_ids=list(range(8)))
    outp = np.empty((4, 64, 128, D), np.float32)
    for c in range(8):
        b, r = c // 2, c % 2
        outp[b, r::2] = np.asarray(res.results[c]["out"]).reshape(32, 128, D)
    return outp.reshape(4, 8192, D)
```

```python
import math
import numpy as np
import concourse.bass as bass
import concourse.mybir as mybir
from concourse.bass_utils import run_bass_kernel_spmd

F32 = mybir.dt.float32
BF16 = mybir.dt.bfloat16
AF = mybir.ActivationFunctionType
ALU = mybir.AluOpType

D = 1024
DIN = 1824
DFF = 2816
NF = DFF // 128
NR = 2
NLOC = 33
T = NLOC * 128
VW = NLOC * 65
KVROWS = 1740
RM, RD, RS0, RS1 = 161, 129, 128, 130
OFF_M = 0
OFF_D = OFF_M + 6 * RM * T
OFF_S0 = OFF_D + 4 * RD * T
OFF_S1 = OFF_S0 + RS0 * T
assert OFF_S1 + RS1 * T == KVROWS * T
PIECES = [("m%d" % h, h * RM, RM) for h in range(6)] + [("d%d" % h, 6 * RM + h * RD, RD) for h in range(4)] + \
         [("s0", 6 * RM + 4 * RD, RS0), ("s1", 6 * RM + 4 * RD + RS0, RS1)]
NME = 41
SC_MLA = 96 ** -0.5
SC_DIFF = 32 ** -0.5
SC_SWA = 64 ** -0.5
EPS = 1e-6
LAM_INIT = [0.8 - 0.6 * math.exp(-0.3 * l) for l in range(2)]
WIN_SEGS = [(0, 928), (1184, 1696), (928, 1184), (1696, 1824)]
PGRP = [(0, 416), (416, 928), (928, 1440), (1440, 1824)]


class Buf:
    __slots__ = ("name", "ws", "r", "psum", "pr")

    def __init__(self, name):
        self.name = name
        self.ws = []
        self.r = []
        self.pr = []
        self.psum = False


class Op:
    __slots__ = ("e", "fn", "dma", "deps", "signal", "sem", "val", "idx", "cc")

    def __init__(self, e, fn, dma):
        self.e = e
        self.fn = fn
        self.dma = dma
        self.deps = []
        self.signal = False
        self.sem = None
        self.val = 0
        self.cc = None


class Tile:
    def __init__(self, ap, name):
        self.ap = ap
        self.buf = Buf(name)

    def __getitem__(self, k):
        return self.ap[k]


NDMASEM = 20


class Prog:
    ENG = ("pe", "act", "dve", "pool", "sp")

    def __init__(self, nc):
        self.nc = nc
        self.ops = {e: [] for e in self.ENG}
        self.dma_ops = {e: [] for e in self.ENG}
        self.bufs = []

    def buf(self, name):
        b = Buf(name)
        self.bufs.append(b)
        return b

    def tile(self, ap, name):
        t = Tile(ap, name)
        self.bufs.append(t.buf)
        return t

    def op(self, e, fn, reads=(), writes=(), dma=False):
        o = Op(e, fn, dma)
        deps = []
        seen = set()

        def add(d):
            if d is None or id(d) in seen:
                return
            seen.add(id(d))
            if d.e == "pe" and e == "pe" and not d.dma and not dma:
                return
            deps.append(d)

        rb = [t.buf if isinstance(t, Tile) else t for t in reads]
        wb = [t.buf if isinstance(t, Tile) else t for t in writes]
        for b in rb:
            for x in b.ws:
                add(x)
            if b.psum:
                for x in b.r:
                    if x.e != e:
                        add(x)
        for b in wb:
            for x in b.r:
                add(x)
            for x in b.pr:
                add(x)
            for x in b.ws:
                if not (x.dma and dma):
                    add(x)
        if dma:
            lst = self.dma_ops[e]
            k = len(lst)
            if k >= NDMASEM:
                add(lst[k - NDMASEM])
            lst.append(o)
        for d in deps:
            d.signal = True
        o.deps = deps
        wset = set(id(b) for b in wb)
        for b in wb:
            if b.r or any(b is x for x in rb):
                b.pr = list(b.r)
                b.ws = [o]
                b.r = []
            else:
                b.ws.append(o)
        for b in rb:
            if id(b) not in wset:
                b.r.append(o)
        self.ops[e].append(o)
        return o

    def barrier(self):
        lasts = []
        for e in self.ENG:
            for o in reversed(self.ops[e]):
                if o.fn is not None and not o.dma:
                    lasts.append(o)
                    break
            lasts.extend(self.dma_ops[e][-NDMASEM:])
        for e in self.ENG:
            o = Op(e, None, False)
            o.deps = [d for d in lasts]
            self.ops[e].append(o)
        for d in lasts:
            d.signal = True
        for b in self.bufs:
            b.ws = []
            b.r = []
            b.pr = []


def DAP(t, off, *dims):
    return bass.AP(t, off, [[s, n] for (s, n) in dims])


class Arena:
    def __init__(self, P, ap, size, name):
        self.P = P
        self.ap = ap
        self.size = size
        self.off = 0
        self.name = name

    def reset(self):
        self.off = 0

    def alloc(self, n, name, pat=None, **kw):
        n2 = (n + 15) // 16 * 16
        assert self.off + n2 <= self.size, (self.name, name, self.off, n2, self.size)
        ap = self.ap[:, self.off:self.off + n]
        self.off += n2
        if pat is not None:
            ap = ap.rearrange(pat, **kw)
        return self.P.tile(ap, name)


def t5_bucket_np(n):
    n = np.maximum(n, 0)
    nf = np.maximum(n, 16).astype(np.float32)
    large = 16 + (np.log(nf / np.float32(16)) / np.float32(math.log(128 / 16)) * np.float32(16)).astype(np.int32)
    large = np.minimum(large, 31)
    return np.where(n < 16, n, large)


def make_consts(r):
    c = {}
    inv_freq = (10000.0 ** (-np.arange(0, 32, 2, dtype=np.float32) / np.float32(32))).astype(np.float32)
    cs = np.zeros((128, NLOC, 64), np.float32)
    for i in range(NLOC):
        p = np.arange(128)
        if i == 0:
            pos = np.minimum(p, 15)
        else:
            G = NR * (i - 1) + 1 + r
            pos = (G - 1) * 128 + p + 16
        ang = pos.astype(np.float32)[:, None] * inv_freq[None, :]
        co = np.cos(ang).astype(np.float32)
        si = np.sin(ang).astype(np.float32)
        cs[:, i, 0:16] = co
        cs[:, i, 16:32] = co
        cs[:, i, 32:48] = -si
        cs[:, i, 48:64] = si
    c["cs_tab"] = cs.reshape(128, NLOC * 64)
    k = np.arange(128)[:, None]
    q = np.arange(128)[None, :]
    tri = (k <= q).astype(np.float32)
    one = np.ones((128, 128), np.float32)
    zero = np.zeros((128, 128), np.float32)
    mA = np.zeros((128, 9, 4, 128), np.float32)
    for s in range(8):
        for qb in range(4):
            j = 1 + s
            G = 2 * qb + 1 + r
            mA[:, s, qb, :] = one if j < G else (tri if j == G else zero)
    mA[:, 8, 0, :] = tri
    c["mA"] = mA.reshape(128, 9 * 512)
    cd = np.zeros((4, NME), np.float32)
    for s in range(9):
        for qb in range(4):
            j = s
            G = 2 * qb + 1 + r
            e = s * 4 + qb
            if j < G - 1:
                cd[0, e] = 1
            elif j == G - 1:
                cd[2, e] = 1
            elif j == G:
                cd[1, e] = 1
    for qb in range(4):
        G = 2 * qb + 1 + r
        if G == 1:
            cd[3, 36 + qb] = 1
        else:
            cd[0, 36 + qb] = 1
    cd[1, 40] = 1
    c["cdiff"] = np.broadcast_to(cd.reshape(1, 4 * NME), (128, 4 * NME)).copy()
    csw = np.zeros((4, 4), np.float32)
    if r == 0:
        csw[1, 0] = 1; csw[0, 1] = 1
        csw[3, 3] = 1
    else:
        csw[1, 1] = 1; csw[0, 2] = 1
        csw[2, 3] = 1
    c["cswa"] = np.broadcast_to(csw.reshape(1, 16), (128, 16)).copy()
    oh = np.zeros((32, 768), np.float32)
    al = np.zeros((10, 768), np.float32)
    for j in range(256):
        if j < 128:
            oh[t5_bucket_np(np.array(j))[()], j] = 1; al[:, j] = 1
        if j < 128:
            oh[31, 256 + j] = 1; al[0:4, 256 + j] = 1
        elif j > 128:
            oh[t5_bucket_np(np.array(j - 128))[()], 256 + j] = 1; al[:, 256 + j] = 1
        if j < 128:
            oh[t5_bucket_np(np.array(j + 16))[()], 512 + j] = 1; al[:, 512 + j] = 1
        elif j >= 241:
            oh[t5_bucket_np(np.array(j - 240))[()], 512 + j] = 1; al[:, 512 + j] = 1
    c["oh"] = oh
    c["aallow"] = al
    c["ident"] = np.eye(128, dtype=np.float32)
    return c


def build_program(debug=False, stop=None, dump=(), ncores=8, nblk=NLOC, cut=99):
    nc = bass.Bass("TRN2", target_bir_lowering=False)
    P = Prog(nc)

    def din(name, shape, dt=F32):
        return nc.dram_tensor(name, list(shape), dt, kind="ExternalInput")

    xin = din("xin", [32, 128, D])
    meta_tokens = din("meta_tokens", [16, D])
    rel_bias = din("rel_bias", [32, 10])
    attn_norm = din("attn_norm", [2, D])
    w_in = din("w_in", [2, D, DIN])
    mla_q_norm = din("mla_q_norm", [2, 256])
    mla_w_qb = din("mla_w_qb", [2, 256, 576])
    mla_kv_norm = din("mla_kv_norm", [2, 128])
    mla_w_kvb = din("mla_w_kvb", [2, 128, 768])
    diff_lambda = din("diff_lambda", [2, 4, 32])
    diff_subln = din("diff_subln", [2, 64])
    swa_sinks = din("swa_sinks", [2, 6])
    w_out = din("w_out", [2, D, D])
    ffn_norm = din("ffn_norm", [2, D])
    w_gate = din("w_gate", [2, D, DFF])
    w_up = din("w_up", [2, D, DFF])
    w_down = din("w_down", [2, DFF, D])
    final_norm = din("final_norm", [D])
    cs_tab = din("cs_tab", [128, NLOC * 64])
    mA_in = din("mA", [128, 9 * 512])
    cdiff_in = din("cdiff", [128, 4 * NME])
    cswa_in = din("cswa", [128, 16])
    oh_in = din("oh", [32, 768])
    aallow_in = din("aallow", [10, 768])
    ident_in = din("ident", [128, 128])
    out = nc.dram_tensor("out", [32, 128, D], F32, kind="ExternalOutput")

    def dscr(name, shape, dt):
        if debug and name in dump:
            return nc.dram_tensor(name, list(shape), dt, kind="ExternalOutput")
        return nc.dram_tensor(name, list(shape), dt)

    hres = dscr("hres", [NLOC, 128, D], F32)
    kvown = nc.dram_tensor("kvown", [KVROWS, T], BF16)
    kvall = {nm: nc.dram_tensor("kvall_" + nm, [NR * nr, T], BF16) for (nm, r0, nr) in PIECES}
    pbuf = {nm: P.buf("piece_" + nm) for (nm, r0, nr) in PIECES}
    QTm = dscr("QTm", [6 * 96, T], BF16)
    QTd = dscr("QTd", [256, T], BF16)
    QTs = dscr("QTs", [128, 3 * T], BF16)
    yT = dscr("yT", [D, T], BF16)
    Wb_in = dscr("Wb_in", [2, 128, 8 * DIN], BF16)
    Wb_qb = dscr("Wb_qb", [2, 128, 2 * 576], BF16)
    Wb_kvb = dscr("Wb_kvb", [2, 128, 768], BF16)
    Wb_out = dscr("Wb_out", [2, 128, 8 * D], BF16)
    Wb_g = dscr("Wb_g", [2, NF, 128, 8 * 128], BF16)
    Wb_u = dscr("Wb_u", [2, NF, 128, 8 * 128], BF16)
    Wb_d = dscr("Wb_d", [2, 128, NF * D], BF16)
    Gd = dscr("Gd", [10, 768], F32)
    zper = dscr("zper", [30, 129 * 256], F32)
    maskD = dscr("maskD", [4, 128, NME * 128], F32)
    maskS = dscr("maskS", [128, 2 * 4 * 384], F32)
    if debug and "kvown" in dump:
        kvodbg = nc.dram_tensor("kvodbg", [KVROWS, T], BF16, kind="ExternalOutput")

    ctx = []

    def enter(cm):
        ctx.append(cm)
        return cm.__enter__()

    NFA = 17 * 1024
    NBA = 56 * 1024
    arena_f_t = enter(nc.sbuf_tensor("arena_f", [128, NFA], F32))
    arena_b_t = enter(nc.sbuf_tensor("arena_b", [128, NBA], BF16))
    psum_t = enter(nc.psum_tensor("psum", [128, 8 * 512], F32))
    sems = {e: enter(nc.semaphore("sem_" + e)) for e in Prog.ENG}
    dsems = {e: [enter(nc.semaphore("dsem_%s_%d" % (e, i))) for i in range(NDMASEM)] for e in ("sp", "act", "pool")}
    ccsems = [enter(nc.semaphore("ccsem%d" % i)) for i in range(2 * len(PIECES))]
    blk = enter(nc.Block())

    AFa = Arena(P, arena_f_t[:, :], NFA, "f32")
    ABa = Arena(P, arena_b_t[:, :], NBA, "bf16")
    PS = [P.tile(psum_t[:, b * 512:(b + 1) * 512], "ps%d" % b) for b in range(8)]
    for t_ in PS:
        t_.buf.psum = True

    def psb(b):
        return psum_t[:, b * 512:(b + 1) * 512].bitcast(BF16)

    def dma(q, out_ap, in_ap, reads=(), writes=()):
        return P.op(q, lambda eng, o=out_ap, i=in_ap: eng.dma_start(out=o, in_=i), reads, writes, dma=True)

    def mm(out_ap, lhsT, rhs, start, stop, reads, writes, tp=None):
        if tp is not None:
            return P.op("pe", lambda eng: eng.matmul(out_ap, lhsT, rhs, start=start, stop=stop, tile_position=tp), reads, writes)
        return P.op("pe", lambda eng: eng.matmul(out_ap, lhsT, rhs, start=start, stop=stop), reads, writes)

    def tr(out_ap, in_ap, ident_ap, reads, writes):
        return P.op("pe", lambda eng: eng.transpose(out_ap, in_ap, ident_ap), reads, writes)

    def act(out_ap, in_ap, func, reads, writes, scale=1.0, bias=0.0, accum=None):
        if accum is not None:
            return P.op("act", lambda eng: eng.activation(out_ap, in_ap, func, bias=bias, scale=scale, accum_out=accum),
                        reads, writes)
        return P.op("act", lambda eng: eng.activation(out_ap, in_ap, func, bias=bias, scale=scale), reads, writes)

    def tt(e, out_ap, in0, in1, op, reads, writes):
        return P.op(e, lambda eng: eng.tensor_tensor(out_ap, in0, in1, op), reads, writes)

    def ts(e, out_ap, in0, s1, s2, op0, op1, reads, writes):
        if op1 is None:
            return P.op(e, lambda eng: eng.tensor_scalar(out_ap, in0, s1, None, op0), reads, writes)
        return P.op(e, lambda eng: eng.tensor_scalar(out_ap, in0, s1, s2, op0, op1), reads, writes)

    def stt(e, out_ap, in0, scalar, in1, op0, op1, reads, writes):
        return P.op(e, lambda eng: eng.scalar_tensor_tensor(out_ap, in0, scalar, in1, op0, op1), reads, writes)

    def cp(e, out_ap, in_ap, reads, writes):
        if e == "act":
            return P.op("act", lambda eng: eng.copy(out_ap, in_ap), reads, writes)
        return P.op(e, lambda eng: eng.tensor_copy(out_ap, in_ap), reads, writes)

    def ttr(jt, out_ap, in0, in1, acc, reads, writes):
        return act(out_ap, in0, AF.Square, list(reads) + [jt], list(writes) + [jt], accum=acc)

    def memset(e, ap, val, writes):
        return P.op(e, lambda eng: eng.memset(ap, val), (), writes)

    def rstd_ops(st, c0, n, reads_extra=()):
        act(st[:, c0 + 1:c0 + 2], st[:, c0:c0 + 1], AF.Ln, [st], [st], scale=1.0 / n, bias=EPS)
        act(st[:, c0 + 2:c0 + 3], st[:, c0 + 1:c0 + 2], AF.Exp, [st], [st], scale=-0.5)

    NPERS_F = 128 + 64 + 2 + 2 + 12 * 128 + 16
    pers_f = Arena(P, arena_f_t[:, NFA - 2048:NFA], 2048, "persf")
    NFA_USE = NFA - 2048
    AFa.size = NFA_USE
    pers_b = Arena(P, arena_b_t[:, NBA - 256:NBA], 256, "persb")
    ABa.size = NBA - 256
    ones_f = pers_f.alloc(128, "ones_f")
    lam_t = pers_f.alloc(16, "lam_t")
    gsub_t = pers_f.alloc(16, "gsub_t")
    sinkrow = pers_f.alloc(12 * 128, "sinkrow", "p (a b) -> p a b", b=128)
    ident_b = pers_b.alloc(128, "ident_b")
    sel_f = pers_f.alloc(64, "sel_f")

    def phase_weights():
        AFa.reset(); ABa.reset()
        stg = [AFa.alloc(2816, "wstg%d" % i) for i in range(2)]
        stb = [ABa.alloc(2816, "wstb%d" % i) for i in range(2)]
        cnt = [0]

        def one(src_aps, ncols, dst_ap):
            i = cnt[0] % 2
            cnt[0] += 1
            for (so, do, n, ap) in src_aps:
                dma("sp", stg[i][:, do:do + n], ap, [], [stg[i]])
            e = "dve" if (cnt[0] % 2 == 0) else "act"
            cp(e, stb[i][:, 0:ncols], stg[i][:, 0:ncols], [stg[i]], [stb[i]])
            dma("pool", dst_ap, stb[i][:, 0:ncols], [stb[i]], [])

        for l in range(2):
            for k in range(8):
                srcs = []
                do = 0
                for (a, b) in WIN_SEGS:
                    srcs.append((a, do, b - a, w_in[l, k * 128:(k + 1) * 128, a:b]))
                    do += b - a
                one(srcs, DIN, DAP(Wb_in, l * 128 * 8 * DIN + k * DIN, (8 * DIN, 128), (1, DIN)))
            for k in range(2):
                one([(0, 0, 576, mla_w_qb[l, k * 128:(k + 1) * 128, :])], 576,
                    DAP(Wb_qb, l * 128 * 1152 + k * 576, (1152, 128), (1, 576)))
            one([(0, 0, 768, mla_w_kvb[l, :, :])], 768, DAP(Wb_kvb, l * 128 * 768, (768, 128), (1, 768)))
            for k in range(8):
                one([(0, 0, D, w_out[l, k * 128:(k + 1) * 128, :])], D,
                    DAP(Wb_out, l * 128 * 8 * D + k * D, (8 * D, 128), (1, D)))

    def phase_masks():
        AFa.reset(); ABa.reset()
        memset("dve", ones_f[:, :], 1.0, [ones_f])
        idf = AFa.alloc(128, "idf")
        dma("sp", idf[:, :], ident_in[:, :], [], [idf])
        cp("dve", ident_b[:, :], idf[:, :], [idf], [ident_b])
        tt("dve", sel_f[:, 0:64], idf[:, 0:64], idf[:, 64:128], ALU.subtract, [idf], [sel_f])
        lp = AFa.alloc(256, "lp")
        dma("sp", lp[:, :], DAP(diff_lambda, 0, (0, 128), (1, 256)), [], [lp])
        junk = AFa.alloc(64, "junk")
        lst = AFa.alloc(16, "lst")
        for l in range(2):
            for pr in range(2):
                a = l * 128 + pr * 64
                tt("dve", junk[:, 0:32], lp[:, a:a + 32], lp[:, a + 32:a + 64], ALU.mult, [lp], [junk])
                P.op("dve", lambda eng, o_=lst[:, 2 * l + pr:2 * l + pr + 1]: eng.reduce_sum(o_, junk[:, 0:32], mybir.AxisListType.X),
                     [junk], [lst])
        lex = AFa.alloc(16, "lex")
        act(lex[:, 0:4], lst[:, 0:4], AF.Exp, [lst], [lex])
        for l in range(2):
            tt("dve", lam_t[:, l:l + 1], lex[:, 2 * l:2 * l + 1], lex[:, 2 * l + 1:2 * l + 2], ALU.subtract, [lex], [lam_t])
            ts("dve", lam_t[:, l:l + 1], lam_t[:, l:l + 1], LAM_INIT[l], None, ALU.add, None, [lam_t], [lam_t])
        gs0 = AFa.alloc(16, "gs0")
        for l in range(2):
            dma("sp", gs0[0:64, l:l + 1], DAP(diff_subln, 64 * l, (1, 64), (1, 1)), [], [gs0])
        for l in range(2):
            ts("dve", gsub_t[0:64, l:l + 1], gs0[0:64, l:l + 1], 1.0 - LAM_INIT[l], None, ALU.mult, None, [gs0], [gsub_t])
        rb = AFa.alloc(16, "rb")
        dma("sp", rb[0:32, 0:10], rel_bias[:, :], [], [rb])
        oh = AFa.alloc(768, "oh")
        dma("sp", oh[0:32, :], oh_in[:, :], [], [oh])
        al = AFa.alloc(768, "al")
        dma("sp", al[0:10, :], aallow_in[:, :], [], [al])
        b31 = AFa.alloc(16, "b31")
        dma("sp", b31[0:10, 0:1], DAP(rel_bias, 310, (1, 10), (1, 1)), [], [b31])
        ts("dve", b31[0:10, 1:2], b31[0:10, 0:1], -1.0, None, ALU.mult, None, [b31], [b31])
        mm(PS[0][0:10, 0:512], rb[0:32, 0:10], oh[0:32, 0:512], True, True, [rb, oh], [PS[0]])
        mm(PS[1][0:10, 0:256], rb[0:32, 0:10], oh[0:32, 512:768], True, True, [rb, oh], [PS[1]])
        gl = AFa.alloc(768, "gl")
        act(gl[0:10, 0:512], PS[0][0:10, 0:512], AF.Exp, [PS[0], b31], [gl], bias=b31[0:10, 1:2])
        act(gl[0:10, 512:768], PS[1][0:10, 0:256], AF.Exp, [PS[1], b31], [gl], bias=b31[0:10, 1:2])
        tt("dve", gl[0:10, :], gl[0:10, :], al[0:10, :], ALU.mult, [gl, al], [gl])
        o1 = dma("sp", Gd[:, :], gl[0:10, :], [gl], [])
        gdb = P.buf("gd")
        gdb.ws = [o1]
        zb = P.buf("zper")
        for hv in range(30):
            dma("pool", DAP(zper, hv * 129 * 256, (256, 129), (1, 256)), DAP(Gd, hv * 256, (0, 129), (1, 256)), [gdb], [zb])
        P.barrier()
        AFa.reset()
        Tall = AFa.alloc(30 * 128, "Tall", "p (a b) -> p a b", b=128)
        for hv in range(30):
            dma("sp" if hv % 2 == 0 else "pool", Tall[:, hv, :], DAP(zper, hv * 129 * 256, (255, 128), (1, 128)), [], [Tall])
        sk = AFa.alloc(32, "sk")
        dma("sp", sk[64:65, 0:12], DAP(swa_sinks, 0, (0, 1), (1, 12)), [], [sk])
        dma("sp", sk[64:65, 16:22], DAP(rel_bias, 314, (0, 1), (1, 6)), [], [sk])
        for l in range(2):
            tt("dve", sk[64:65, 6 * l:6 * l + 6], sk[64:65, 6 * l:6 * l + 6], sk[64:65, 16:22], ALU.subtract, [sk], [sk])
        act(sk[64:65, 0:12], sk[64:65, 0:12], AF.Exp, [sk], [sk])
        cp("dve", sinkrow[64:65, :, :], sk[64:65, 0:12].unsqueeze(2).broadcast_to([1, 12, 128]), [sk], [sinkrow])
        cdf = AFa.alloc(4 * NME, "cdf")
        dma("sp", cdf[:, :], cdiff_in[:, :], [], [cdf])
        NH = 21
        Mt = [AFa.alloc(NH * 128, "Mt%d" % i, "p (a b) -> p a b", b=128) for i in range(2)]

        for h in range(4):
            for (e0, e1) in ((0, NH), (NH, NME)):
                ne = e1 - e0

                def cb(kind):
                    return cdf[:, kind * NME + e0:kind * NME + e1].unsqueeze(2).broadcast_to([128, ne, 128])

                def tb(v):
                    return Tall[:, h * 3 + v, :].unsqueeze(1).broadcast_to([128, ne, 128])

                M, M2 = Mt[0][:, 0:ne, :], Mt[1][:, 0:ne, :]
                tt("dve", M, cb(1), tb(0), ALU.mult, [cdf, Tall], [Mt[0]])
                tt("pool", M2, cb(2), tb(1), ALU.mult, [cdf, Tall], [Mt[1]])
                tt("dve", M, M, M2, ALU.add, [Mt[0], Mt[1]], [Mt[0]])
                tt("pool", M2, cb(3), tb(2), ALU.mult, [cdf, Tall], [Mt[1]])
                tt("dve", M, M, M2, ALU.add, [Mt[0], Mt[1]], [Mt[0]])
                tt("dve", M, M, cb(0), ALU.add, [Mt[0], cdf], [Mt[0]])
                dma("sp", DAP(maskD, h * 128 * NME * 128 + e0 * 128, (NME * 128, 128), (1, ne * 128)),
                    M.rearrange("p a b -> p (a b)"), [Mt[0]], [])
        csf = AFa.alloc(16, "csf")
        dma("sp", csf[:, :], cswa_in[:, :], [], [csf])
        MS = AFa.alloc(2 * 4 * 384, "MSb", "p (g e r c) -> p g e r c", g=2, e=4, r=3)
        for g in range(2):
            h0 = 4 + 3 * g
            Tc = Tall[:, :, :].rearrange("p (h v) c -> p h v c", v=3)[:, h0:h0 + 3, 0, :]
            Tp = Tall[:, :, :].rearrange("p (h v) c -> p h v c", v=3)[:, h0:h0 + 3, 1, :]
            Tm = Tall[:, :, :].rearrange("p (h v) c -> p h v c", v=3)[:, h0:h0 + 3, 2, :]
            for e in range(3):
                ts("dve", MS[:, g, e, :, :], Tc, csf[:, 0 + e:0 + e + 1], None, ALU.mult, None, [Tall, csf], [MS])
                stt("dve", MS[:, g, e, :, :], Tp, csf[:, 4 + e:4 + e + 1], MS[:, g, e, :, :], ALU.mult, ALU.add,
                    [Tall, csf, MS], [MS])
            ts("dve", MS[:, g, 3, :, :], Tm, csf[:, 12 + 3:12 + 4], None, ALU.mult, None, [Tall, csf], [MS])
            ts("dve", MS[:, g, 3, :, :], MS[:, g, 3, :, :], csf[:, 8 + 3:8 + 4], None, ALU.add, None, [MS, csf], [MS])
        dma("sp", maskS[:, :], MS[:, :, :, :, :].rearrange("p g e r c -> p (g e r c)"), [MS], [])
        dma("sp", DAP(zper, 0, (6 * 128, 16), (128, 6), (1, 128)),
            Tall[:, :, :].rearrange("p (h v) c -> p h v c", v=3)[0:16, 4:10, 0, :], [Tall], [])

    def phase_init():
        AFa.reset(); ABa.reset()
        z = AFa.alloc(1024, "zinit")
        memset("dve", z[:, :], 0.0, [z])
        dma("sp", hres[0, 16:128, :], z[16:128, :], [z], [])
        dma("sp", hres[0, 0:16, :], meta_tokens[:, :], [], [])

    def hsrc(l, i):
        if l == 0 and i >= 1:
            return xin[i - 1, :, :]
        return hres[i, :, :]

    def phase_proj(l):
        AFa.reset(); ABa.reset()
        Win = ABa.alloc(8 * DIN, "Win", "p (k c) -> p k c", c=DIN)
        Wqb = ABa.alloc(2 * 576, "Wqb", "p (k c) -> p k c", c=576)
        Wkvb = ABa.alloc(768, "Wkvb")
        dma("sp", Win[:, :, :], DAP(Wb_in, l * 128 * 8 * DIN, (8 * DIN, 128), (DIN, 8), (1, DIN)), [], [Win])
        dma("sp", Wqb[:, :, :], DAP(Wb_qb, l * 128 * 1152, (1152, 128), (576, 2), (1, 576)), [], [Wqb])
        dma("sp", Wkvb[:, :], DAP(Wb_kvb, l * 128 * 768, (768, 128), (1, 768)), [], [Wkvb])
        Gat = AFa.alloc(1024, "Gat")
        Gq = AFa.alloc(256, "Gq")
        Gkv = AFa.alloc(128, "Gkv")
        dma("sp", Gat[:, :], DAP(attn_norm, l * D, (0, 128), (1, D)), [], [Gat])
        dma("sp", Gq[:, :], DAP(mla_q_norm, l * 256, (0, 128), (1, 256)), [], [Gq])
        dma("sp", Gkv[:, :], DAP(mla_kv_norm, l * 128, (0, 128), (1, 128)), [], [Gkv])
        cs = AFa.alloc(NLOC * 64, "cs", "p (i c) -> p i c", c=64)
        dma("sp", cs[:, :, :], cs_tab[:, :].rearrange("p (i c) -> p i c", c=64), [], [cs])
        junk = AFa.alloc(1024, "junk")
        junk2 = AFa.alloc(256, "junk2")
        Hh = [AFa.alloc(1024, "Hh%d" % i) for i in range(2)]
        st = [AFa.alloc(16, "st%d" % i) for i in range(2)]
        st2 = [AFa.alloc(16, "st2_%d" % i) for i in range(2)]
        ra = [AFa.alloc(6 * 32, "ra%d" % i, "p (h c) -> p h c", c=32) for i in range(2)]
        rb_ = [AFa.alloc(6 * 32, "rb%d" % i, "p (h c) -> p h c", c=32) for i in range(2)]
        Craw = [AFa.alloc(416, "Craw%d" % i) for i in range(2)]
        Hn = [ABa.alloc(1024, "Hn%d" % i) for i in range(2)]
        HnT = [ABa.alloc(1024, "HnT%d" % i) for i in range(2)]
        Cn = [ABa.alloc(384, "Cn%d" % i) for i in range(2)]
        CT = [ABa.alloc(384, "CT%d" % i) for i in range(2)]
        Qf = [ABa.alloc(576, "Qf%d" % i, "p (h c) -> p h c", c=96) for i in range(2)]
        Kf = [ABa.alloc(576, "Kf%d" % i, "p (h c) -> p h c", c=96) for i in range(2)]
        Vf = [ABa.alloc(390, "Vf%d" % i, "p (h c) -> p h c", c=65) for i in range(2)]
        krot = [ABa.alloc(32, "krot%d" % i) for i in range(2)]
        QT = [ABa.alloc(768, "QT%d" % i) for i in range(2)]
        KT = [ABa.alloc(768, "KT%d" % i) for i in range(2)]
        Df = [ABa.alloc(512, "Df%d" % i) for i in range(2)]
        DT = [ABa.alloc(512, "DT%d" % i) for i in range(2)]
        Vdf = [ABa.alloc(260, "Vdf%d" % i, "p (h c) -> p h c", c=65) for i in range(2)]
        Sf = [ABa.alloc(512, "Sf%d" % i) for i in range(2)]
        ST = [ABa.alloc(512, "ST%d" % i) for i in range(2)]
        Vsf = [ABa.alloc(130, "Vsf%d" % i, "p (h c) -> p h c", c=65) for i in range(2)]
        for i in range(2):
            memset("pool", Vf[i][:, :, 64:65], 1.0, [Vf[i]])
            memset("pool", Vdf[i][:, :, 64:65], 1.0, [Vdf[i]])
            memset("pool", Vsf[i][:, :, 64:65], 1.0, [Vsf[i]])

        def rope(src3, dst3, H, i, db, src_t, dst_t):
            A = ra[db][:, 0:H, :]
            B = rb_[db][:, 0:H, :]
            cA = cs[:, i, 0:32].unsqueeze(1).broadcast_to([128, H, 32])
            sB0 = cs[:, i, 32:48].unsqueeze(1).broadcast_to([128, H, 16])
            sB1 = cs[:, i, 48:64].unsqueeze(1).broadcast_to([128, H, 16])
            tt("dve", A, src3, cA, ALU.mult, [src_t, cs], [ra[db]])
            tt("dve", B[:, :, 0:16], src3[:, :, 16:32], sB0, ALU.mult, [src_t, cs], [rb_[db]])
            tt("dve", B[:, :, 16:32], src3[:, :, 0:16], sB1, ALU.mult, [src_t, cs], [rb_[db]])
            tt("dve", dst3, A, B, ALU.add, [ra[db], rb_[db]], [dst_t])

        def front(i):
            db = i % 2
            H, S_, hn, hnT = Hh[db], st[db], Hn[db], HnT[db]
            dma("sp", H[:, :], hsrc(l, i), [], [H])
            ttr(junk, junk[:, :], H[:, :], H[:, :], S_[:, 0:1], [H], [S_])
            rstd_ops(S_, 0, 1024)
            stt("dve", hn[:, :], H[:, :], S_[:, 2:3], Gat[:, :], ALU.mult, ALU.mult, [H, S_, Gat], [hn])
            for k in range(8):
                tr(psb(6)[:, k * 128:(k + 1) * 128], hn[:, k * 128:(k + 1) * 128], ident_b[:, :], [hn, ident_b], [PS[6]])
            cp("act", hnT[:, :], psb(6)[:, :], [PS[6]], [hnT])
            for g, (c0, c1) in enumerate(PGRP):
                for k in range(8):
                    mm(PS[g][:, 0:c1 - c0], hnT[:, k * 128:(k + 1) * 128], Win[:, k, c0:c1], k == 0, k == 7,
                       [hnT, Win], [PS[g]])
            cr = Craw[db]
            cp("act", cr[:, 0:416], PS[0][:, 0:416], [PS[0]], [cr])
            cp("dve", Df[db][:, :], PS[1][:, 0:512], [PS[1]], [Df[db]])
            cp("act", Sf[db][:, 0:384].rearrange("p (r g c) -> p g r c", r=3, g=2),
               PS[2][:, 0:384].rearrange("p (g r c) -> p g r c", g=2, r=3), [PS[2]], [Sf[db]])
            cp("act", Sf[db][:, 384:512], PS[2][:, 384:512], [PS[2]], [Sf[db]])
            cp("dve", Vdf[db][:, :, 0:64], PS[3][:, 0:256].rearrange("p (h c) -> p h c", c=64), [PS[3]], [Vdf[db]])
            cp("dve", Vsf[db][:, :, 0:64], PS[3][:, 256:384].rearrange("p (h c) -> p h c", c=64), [PS[3]], [Vsf[db]])

        def tail(i):
            db = i % 2
            S_ = st2[db]
            cr = Craw[db]
            ttr(junk2, junk2[:, 0:256], cr[:, 0:256], cr[:, 0:256], S_[:, 4:5], [cr], [S_])
            ttr(junk2, junk2[:, 0:128], cr[:, 256:384], cr[:, 256:384], S_[:, 8:9], [cr], [S_])
            rstd_ops(S_, 4, 256)
            rstd_ops(S_, 8, 128)
            stt("dve", Cn[db][:, 0:256], cr[:, 0:256], S_[:, 6:7], Gq[:, :], ALU.mult, ALU.mult, [cr, S_, Gq], [Cn[db]])
            stt("dve", Cn[db][:, 256:384], cr[:, 256:384], S_[:, 10:11], Gkv[:, :], ALU.mult, ALU.mult,
                [cr, S_, Gkv], [Cn[db]])
            rope(cr[:, 384:416].unsqueeze(1), krot[db][:, :].unsqueeze(1), 1, i, db, cr, krot[db])
            for k in range(3):
                tr(psb(7)[:, k * 128:(k + 1) * 128], Cn[db][:, k * 128:(k + 1) * 128], ident_b[:, :], [Cn[db], ident_b], [PS[7]])
            cp("act", CT[db][:, :], psb(7)[:, 0:384], [PS[7]], [CT[db]])
            for half in range(2):
                for k in range(2):
                    mm(PS[4 + half][:, 0:288], CT[db][:, k * 128:(k + 1) * 128], Wqb[:, k, half * 288:(half + 1) * 288],
                       k == 0, k == 1, [CT[db], Wqb], [PS[4 + half]])
            for half in range(2):
                q3 = PS[4 + half][:, 0:288].rearrange("p (h c) -> p h c", c=96)
                cp("act", Qf[db][:, 3 * half:3 * half + 3, 0:64], q3[:, :, 0:64], [PS[4 + half]], [Qf[db]])
                rope(q3[:, :, 64:96], Qf[db][:, 3 * half:3 * half + 3, 64:96], 3, i, db, PS[4 + half], Qf[db])
            for half in range(2):
                mm(PS[4 + half][:, 0:384], CT[db][:, 256:384], Wkvb[:, half * 384:(half + 1) * 384], True, True,
                   [CT[db], Wkvb], [PS[4 + half]])
            for half in range(2):
                kv3 = PS[4 + half][:, 0:384].rearrange("p (h c) -> p h c", c=128)
                cp("act", Kf[db][:, 3 * half:3 * half + 3, 0:64], kv3[:, :, 0:64], [PS[4 + half]], [Kf[db]])
                cp("act", Vf[db][:, 3 * half:3 * half + 3, 0:64], kv3[:, :, 64:128], [PS[4 + half]], [Vf[db]])
            cp("pool", Kf[db][:, :, 64:96], krot[db][:, :].unsqueeze(1).broadcast_to([128, 6, 32]), [krot[db]], [Kf[db]])
            for h in range(6):
                tr(psb(6)[0:96, h * 128:(h + 1) * 128], Qf[db][:, h, :], ident_b[:, :], [Qf[db], ident_b], [PS[6]])
            cp("act", QT[db][0:96, :], psb(6)[0:96, 0:768], [PS[6]], [QT[db]])
            for h in range(6):
                tr(psb(7)[0:96, h * 128:(h + 1) * 128], Kf[db][:, h, :], ident_b[:, :], [Kf[db], ident_b], [PS[7]])
            cp("dve", KT[db][0:96, :], psb(7)[0:96, 0:768], [PS[7]], [KT[db]])
            dma("pool", DAP(QTm, i * 128, (T, 96), (96 * T, 6), (1, 128)),
                QT[db][0:96, :].rearrange("p (h c) -> p h c", c=128), [QT[db]], [])
            dma("pool", DAP(kvown, OFF_M + i * 128, (T, 96), (RM * T, 6), (1, 128)),
                KT[db][0:96, :].rearrange("p (h c) -> p h c", c=128), [KT[db]], [])
            dma("pool", DAP(kvown, OFF_M + 96 * T + i * 65, (VW, 128), (RM * T, 6), (1, 65)), Vf[db][:, :, :], [Vf[db]], [])
            for m in range(4):
                tr(psb(6)[:, m * 128:(m + 1) * 128], Df[db][:, m * 128:(m + 1) * 128], ident_b[:, :], [Df[db], ident_b], [PS[6]])
            cp("act", DT[db][:, :], psb(6)[:, 0:512], [PS[6]], [DT[db]])
            dma("pool", DAP(QTd, i * 128, (T, 128), (128 * T, 2), (1, 128)),
                DT[db][:, 0:256].rearrange("p (h c) -> p h c", c=128), [DT[db]], [])
            for ph in range(2):
                dma("pool", DAP(kvown, OFF_D + ph * RD * T + i * 128, (T, 64), (2 * RD * T, 2), (1, 128)),
                    DT[db][64 * ph:64 * ph + 64, 256:512].rearrange("p (h c) -> p h c", c=128), [DT[db]], [])
            dma("pool", DAP(kvown, OFF_D + 64 * T + i * 65, (VW, 128), (RD * T, 4), (1, 65)), Vdf[db][:, :, :], [Vdf[db]], [])
            for r in range(4):
                tr(psb(7)[:, r * 128:(r + 1) * 128], Sf[db][:, r * 128:(r + 1) * 128], ident_b[:, :], [Sf[db], ident_b], [PS[7]])
            cp("dve", ST[db][:, :], psb(7)[:, 0:512], [PS[7]], [ST[db]])
            dma("pool", DAP(QTs, i * 384, (3 * T, 128), (1, 384)), ST[db][:, 0:384], [ST[db]], [])
            dma("pool", DAP(kvown, OFF_S0 + i * 128, (T, 128), (1, 128)), ST[db][:, 384:512], [ST[db]], [])
            dma("pool", DAP(kvown, OFF_S1 + i * 130, (2 * VW, 128), (1, 130)),
                Vsf[db][:, :, :].rearrange("p h c -> p (h c)"), [Vsf[db]], [])

        front(0)
        for i in range(nblk):
            if i + 1 < nblk:
                front(i + 1)
            tail(i)

    def phase_gather(l):
        P.barrier()
        for k, (nm, r0, nr) in enumerate(PIECES):
            o = Op("pool", lambda eng, nm=nm, r0=r0, nr=nr: eng.collective_compute(
                "AllGather", ALU.bypass, replica_groups=[[2 * i, 2 * i + 1] for i in range(ncores // 2)],
                ins=[kvown[r0:r0 + nr, :].opt()], outs=[kvall[nm].ap().opt()]), False)
            o.signal = True
            o.cc = ccsems[l * len(PIECES) + k]
            P.ops["pool"].append(o)
            pbuf[nm].ws = [o]
            pbuf[nm].r = []

    def kloc(j):
        if j == 0:
            return 0, 0
        return (j - 1) % 2, 1 + (j - 1) // 2

    def ffn_cast_jobs(l):
        jobs = []
        for (wsrc, wdst) in ((w_gate, Wb_g), (w_up, Wb_u)):
            for k in range(8):
                jobs.append((wsrc[l, k * 128:(k + 1) * 128, :], DFF,
                             DAP(wdst, l * NF * 128 * 1024 + k * 128, (1024, 128), (128 * 1024, NF), (1, 128)), True))
        for f in range(NF):
            jobs.append((w_down[l, f * 128:(f + 1) * 128, :], D, DAP(Wb_d, l * 128 * NF * D + f * D, (NF * D, 128), (1, D)), False))
        return jobs

    def background_setup(l):
        stg = [AFa.alloc(2816, "bgstg%d" % i) for i in range(2)]
        stb = [ABa.alloc(2816, "bgstb%d" % i) for i in range(2)]
        return {"jobs": ffn_cast_jobs(l), "stg": stg, "stb": stb, "n": 0, "tick": 0}

    def bg_one(bg):
        if not bg["jobs"]:
            return
        src, ncols, dst, blocked = bg["jobs"].pop(0)
        i = bg["n"] % 2
        bg["n"] += 1
        stg, stb = bg["stg"][i], bg["stb"][i]
        dma("act", stg[:, 0:ncols], src, [], [stg])
        cp("pool", stb[:, 0:ncols], stg[:, 0:ncols], [stg], [stb])
        if blocked:
            dma("pool", dst, stb[:, 0:ncols].rearrange("p (f c) -> p f c", c=128), [stb], [])
        else:
            dma("pool", dst, stb[:, 0:ncols], [stb], [])

    def bg_step(bg):
        bg["tick"] += 1
        if bg["tick"] % 12 == 0:
            bg_one(bg)

    def bg_finish(bg):
        while bg["jobs"]:
            bg_one(bg)

    def attention_core(units, lookahead=2, post_delay=1):
        n = len(units)
        for t in range(n + lookahead + post_delay):
            if t < n:
                units[t]["s"]()
                units[t]["e"]()
            if 0 <= t - lookahead < n:
                units[t - lookahead]["o"]()
            tp_ = t - lookahead - post_delay
            if 0 <= tp_ < n and units[tp_].get("post"):
                units[tp_]["post"]()

    def groups():
        gl = [(-1, 0, 128, [0])]
        for g in range(8):
            gl.append((g, (1 + 4 * g) * 128, 512, list(range(0, 8 * g + 9))))
        return gl

    def pair_view(t_ap, nk, N):
        if N == 512:
            return t_ap[0:nk, 0:1024]
        return t_ap[0:nk, 0:1024].rearrange("p (c n) -> p c n", c=2)[:, :, 0:N]

    def phase_attn_mla(l):
        AFa.reset(); ABa.reset()
        mA = AFa.alloc(9 * 512, "mA", "p (s c) -> p s c", c=512)
        dma("sp", mA[:, :, :], mA_in[:, :].rearrange("p (s c) -> p s c", c=512), [], [mA])
        KTt = [ABa.alloc(2 * T, "KTt%d" % i) for i in range(2)]
        Vt = [ABa.alloc(2 * VW, "Vt%d" % i) for i in range(2)]
        QTt = [ABa.alloc(T, "QTt%d" % i) for i in range(2)]
        Pt = [ABa.alloc(1024, "Pt%d" % i) for i in range(3)]
        yo = [ABa.alloc(512, "yo%d" % i) for i in range(2)]
        rd = [AFa.alloc(512, "rd%d" % i) for i in range(2)]
        bcs = [AFa.alloc(512, "bcs%d" % i) for i in range(2)]
        bg = background_setup(l)
        cnt = {"s": 0, "p": 0, "o": 0, "y": 0}
        for h in range(6):
            hb = h % 2
            for r in range(NR):
                dma("sp", KTt[hb][0:96, r * T:(r + 1) * T], DAP(kvall["m%d" % h], r * RM * T, (T, 96), (1, T)),
                    [pbuf["m%d" % h]], [KTt[hb]])
                dma("sp", Vt[hb][:, r * VW:(r + 1) * VW], DAP(kvall["m%d" % h], r * RM * T + 96 * T, (VW, 128), (1, VW)),
                    [pbuf["m%d" % h]], [Vt[hb]])
            dma("sp", QTt[hb][0:96, :], DAP(QTm, h * 96 * T, (T, 96), (1, T)), [], [QTt[hb]])
            units = []
            for (g, q0, N, js) in groups():
                ob = cnt["o"] % 2
                cnt["o"] += 1
                Ob = PS[4 + ob]
                subs = [[js[0]]] + [js[k:k + 2] for k in range(1, len(js), 2)]
                for jl in subs:
                    sp_ = cnt["s"] % 2
                    cnt["s"] += 1
                    pt = Pt[cnt["p"] % 3]
                    cnt["p"] += 1
                    nk = 16 if jl[0] == 0 else 128
                    info = []
                    for j in jl:
                        rk, lc = kloc(j)
                        info.append((rk * T + lc * 128, rk * VW + lc * 65))
                    if g == -1:
                        mk = mA[0:nk, 8, 0:N]
                    elif jl[0] >= 8 * g + 1:
                        s0 = jl[0] - (8 * g + 1)
                        mk = mA[0:nk, s0:s0 + len(jl), :].rearrange("p s c -> p (s c)")
                    else:
                        mk = None
                    banks = [PS[2 * sp_ + bi] for bi in range(len(jl))]
                    u = {}

                    def s_(banks=banks, info=info, nk=nk, q0=q0, N=N, hb=hb):
                        for bi, (kc, vc) in enumerate(info):
                            mm(banks[bi][0:nk, 0:N], KTt[hb][0:96, kc:kc + nk], QTt[hb][0:96, q0:q0 + N], True, True,
                               [KTt[hb], QTt[hb]], [banks[bi]])

                    def e_(banks=banks, sp_=sp_, pt=pt, nk=nk, N=N, mk=mk, nb=len(jl)):
                        if nb == 2:
                            src = pair_view(psum_t[:, 2 * sp_ * 512:(2 * sp_ + 2) * 512], nk, N)
                            dst = pair_view(pt[:, :], nk, N)
                        else:
                            src = banks[0][0:nk, 0:N]
                            dst = pt[0:nk, 0:N]
                        act(dst, src, AF.Exp, banks, [pt], scale=SC_MLA)
                        if mk is not None:
                            tt("dve", dst, dst, mk, ALU.mult, [pt, mA], [pt])

                    def o_(Ob=Ob, pt=pt, nk=nk, info=info, N=N, hb=hb, jl=jl, js=js):
                        for bi, (kc, vc) in enumerate(info):
                            mm(Ob[0:65, 0:N], Vt[hb][0:nk, vc:vc + 65], pt[0:nk, bi * 512:bi * 512 + N],
                               jl[bi] == js[0], jl[bi] == js[-1], [Vt[hb], pt], [Ob])
                        bg_step(bg)

                    u["s"], u["e"], u["o"] = s_, e_, o_
                    if jl[-1] == js[-1]:
                        yb = cnt["y"] % 2
                        cnt["y"] += 1

                        def post(Ob=Ob, N=N, q0=q0, yb=yb, h=h):
                            act(rd[yb][64:65, 0:N], Ob[64:65, 0:N], AF.Ln, [Ob], [rd[yb]])
                            act(rd[yb][64:65, 0:N], rd[yb][64:65, 0:N], AF.Exp, [rd[yb]], [rd[yb]], scale=-1.0)
                            mm(PS[6][0:64, 0:N], ones_f[64:65, 0:64], rd[yb][64:65, 0:N], True, True, [ones_f, rd[yb]], [PS[6]])
                            cp("dve", bcs[yb][0:64, 0:N], PS[6][0:64, 0:N], [PS[6]], [bcs[yb]])
                            tt("dve", yo[yb][0:64, 0:N], Ob[0:64, 0:N], bcs[yb][0:64, 0:N], ALU.mult, [Ob, bcs[yb]], [yo[yb]])
                            dma("pool", DAP(yT, h * 64 * T + q0, (T, 64), (1, N)), yo[yb][0:64, 0:N], [yo[yb]], [])

                        u["post"] = post
                    units.append(u)
            attention_core(units, lookahead=1)
        bg_finish(bg)

    def phase_attn_diff(l):
        AFa.reset(); ABa.reset()
        KD = [ABa.alloc(2 * T, "KD%d" % i) for i in range(2)]
        VD = [ABa.alloc(2 * VW, "VD%d" % i) for i in range(2)]
        QD = [ABa.alloc(T, "QD%d" % i) for i in range(2)]
        Pt = [ABa.alloc(1024, "Pt%d" % i) for i in range(3)]
        yo = [ABa.alloc(512, "yo%d" % i) for i in range(2)]
        MD = AFa.alloc(NME * 128, "MD")
        Ef = [AFa.alloc(1024, "Ef%d" % i) for i in range(2)]
        rd = [AFa.alloc(512, "rd%d" % i) for i in range(2)]
        Oc = [AFa.alloc(512, "Oc%d" % i) for i in range(2)]
        d0 = AFa.alloc(512, "d0")
        d1 = AFa.alloc(512, "d1")
        rs = AFa.alloc(512, "rs")
        cnt = {"s": 0, "p": 0, "e": 0}
        for h in range(4):
            hb = h % 2
            for r in range(NR):
                dma("sp", KD[hb][0:64, r * T:(r + 1) * T],
                    DAP(kvall["d%d" % h], r * RD * T, (T, 64), (1, T)), [pbuf["d%d" % h]], [KD[hb]])
                dma("sp", VD[hb][:, r * VW:(r + 1) * VW], DAP(kvall["d%d" % h], r * RD * T + 64 * T, (VW, 128), (1, VW)),
                    [pbuf["d%d" % h]], [VD[hb]])
            dma("sp", QD[hb][0:64, :], DAP(QTd, 64 * h * T, (T, 64), (1, T)), [], [QD[hb]])
            dma("sp", MD[:, :], DAP(maskD, h * 128 * NME * 128, (NME * 128, 128), (1, NME * 128)), [], [MD])
            units = []
            for (g, q0, N, js) in groups():
                for j in js:
                    rk, lc = kloc(j)
                    nk = 16 if j == 0 else 128
                    kc = rk * T + lc * 128
                    vc = rk * VW + lc * 65
                    sp_ = cnt["s"] % 2
                    cnt["s"] += 1
                    pt = Pt[cnt["p"] % 3]
                    cnt["p"] += 1
                    if g == -1:
                        mk = MD[0:nk, 40 * 128:40 * 128 + N]
                    elif g == 0 and j == 0:
                        mk = MD[0:nk, 36 * 128:36 * 128 + N]
                    elif j >= 8 * g and j > 0:
                        s0 = j - 8 * g
                        mk = MD[0:nk, s0 * 512:s0 * 512 + N]
                    else:
                        mk = None
                    banks = [PS[2 * sp_], PS[2 * sp_ + 1]]
                    u = {}

                    def s_(banks=banks, nk=nk, kc=kc, q0=q0, N=N, hb=hb):
                        for c in range(2):
                            mm(banks[c][0:nk, 0:N], KD[hb][32 * c:32 * c + 32, kc:kc + nk], QD[hb][32 * c:32 * c + 32, q0:q0 + N],
                               True, True, [KD[hb], QD[hb]], [banks[c]])

                    def e_(banks=banks, sp_=sp_, pt=pt, nk=nk, N=N, mk=mk):
                        src = pair_view(psum_t[:, 2 * sp_ * 512:(2 * sp_ + 2) * 512], nk, N)
                        dst = pair_view(pt[:, :], nk, N)
                        if mk is None:
                            act(dst, src, AF.Exp, banks, [pt], scale=SC_DIFF)
                        else:
                            ef = Ef[cnt["e"] % 2]
                            cnt["e"] += 1
                            efv = pair_view(ef[:, :], nk, N)
                            act(efv, src, AF.Exp, banks, [ef], scale=SC_DIFF)
                            for c in range(2):
                                tt("dve", pt[0:nk, c * 512:c * 512 + N], ef[0:nk, c * 512:c * 512 + N], mk, ALU.mult, [ef, MD], [pt])

                    def o_(pt=pt, nk=nk, vc=vc, N=N, hb=hb, first=(j == js[0]), last=(j == js[-1])):
                        for c in range(2):
                            mm(PS[4 + c][0:65, 0:N], VD[hb][0:nk, vc:vc + 65], pt[0:nk, c * 512:c * 512 + N], first, last,
                               [VD[hb], pt], [PS[4 + c]])

                    u["s"], u["e"], u["o"] = s_, e_, o_
                    if j == js[-1]:
                        def post(N=N, q0=q0, h=h):
                            O0, O1 = PS[4], PS[5]
                            cp("dve", Oc[0][0:65, 0:N], O0[0:65, 0:N], [O0], [Oc[0]])
                            cp("dve", Oc[1][0:65, 0:N], O1[0:65, 0:N], [O1], [Oc[1]])
                            for c in range(2):
                                act(rd[c][64:65, 0:N], Oc[c][64:65, 0:N], AF.Ln, [Oc[c]], [rd[c]])
                                act(rd[c][64:65, 0:N], rd[c][64:65, 0:N], AF.Exp, [rd[c]], [rd[c]], scale=-1.0)
                            ts("dve", rd[1][64:65, 0:N], rd[1][64:65, 0:N], lam_t[64:65, l:l + 1], None, ALU.mult, None,
                               [rd[1], lam_t], [rd[1]])
                            mm(PS[6][0:64, 0:N], ones_f[64:65, 0:64], rd[0][64:65, 0:N], True, True, [ones_f, rd[0]], [PS[6]])
                            mm(PS[7][0:64, 0:N], ones_f[64:65, 0:64], rd[1][64:65, 0:N], True, True, [ones_f, rd[1]], [PS[7]])
                            tt("dve", d0[0:64, 0:N], Oc[0][0:64, 0:N], PS[6][0:64, 0:N], ALU.mult, [Oc[0], PS[6]], [d0])
                            tt("dve", d1[0:64, 0:N], Oc[1][0:64, 0:N], PS[7][0:64, 0:N], ALU.mult, [Oc[1], PS[7]], [d1])
                            tt("pool", d0[0:64, 0:N], d0[0:64, 0:N], d1[0:64, 0:N], ALU.subtract, [d0, d1], [d0])
                            tt("pool", d1[0:64, 0:N], d0[0:64, 0:N], d0[0:64, 0:N], ALU.mult, [d0], [d1])
                            mm(PS[6][0:64, 0:N], ones_f[0:64, 0:64], d1[0:64, 0:N], True, True, [ones_f, d1], [PS[6]])
                            act(rs[0:64, 0:N], PS[6][0:64, 0:N], AF.Ln, [PS[6]], [rs], scale=1.0 / 64, bias=EPS)
                            act(rs[0:64, 0:N], rs[0:64, 0:N], AF.Exp, [rs], [rs], scale=-0.5)
                            yb = yo[(q0 // 128) % 2]
                            stt("dve", yb[0:64, 0:N], d0[0:64, 0:N], gsub_t[0:64, l:l + 1], rs[0:64, 0:N], ALU.mult, ALU.mult,
                                [d0, gsub_t, rs], [yb])
                            dma("pool", DAP(yT, (384 + h * 64) * T + q0, (T, 64), (1, N)), yb[0:64, 0:N], [yb], [])

                        u["post"] = post
                    units.append(u)
            attention_core(units, lookahead=1, post_delay=0)

    def phase_attn_swa(l):
        AFa.reset(); ABa.reset()
        KS = ABa.alloc(2 * T, "KS")
        VS = ABa.alloc(2 * 2 * VW, "VS")
        QS = ABa.alloc(3 * T, "QS")
        Pt = [ABa.alloc(384, "Pt%d" % i) for i in range(4)]
        yo = [ABa.alloc(384, "yo%d" % i) for i in range(2)]
        MS = AFa.alloc(2 * 4 * 384, "MS", "p (g e c) -> p g e c", g=2, e=4)
        MSq = AFa.alloc(6 * 128, "MSq")
        Ef = [AFa.alloc(384, "Ef%d" % i) for i in range(2)]
        rd = [AFa.alloc(384, "rd%d" % i) for i in range(2)]
        bcs = [AFa.alloc(384, "bcs%d" % i) for i in range(2)]
        for r in range(NR):
            dma("sp", KS[:, r * T:(r + 1) * T], DAP(kvall["s0"], r * RS0 * T, (T, 128), (1, T)), [pbuf["s0"]], [KS])
            dma("sp", VS[:, r * 2 * VW:(r + 1) * 2 * VW], DAP(kvall["s1"], r * RS1 * T, (2 * VW, 128), (1, 2 * VW)), [pbuf["s1"]], [VS])
        dma("sp", QS[:, :], DAP(QTs, 0, (3 * T, 128), (1, 3 * T)), [], [QS])
        dma("sp", MS[:, :, :, :], maskS[:, :].rearrange("p (g e c) -> p g e c", g=2, e=4), [], [MS])
        dma("sp", MSq[0:16, :], DAP(zper, 0, (768, 16), (1, 768)), [], [MSq])
        cnt = {"s": 0, "p": 0, "e": 0, "o": 0, "y": 0}
        units = []
        for i in range(NLOC):
            for g in range(2):
                kts = []
                if i == 0:
                    kts.append((0, MSq[0:16, 3 * g * 128:(3 * g + 3) * 128]))
                else:
                    kts.append((0, MS[0:16, g, 3, :] if i == 1 else None))
                    for s in range(3):
                        j = 2 * (i - 1) + s
                        if j == 0 or j > 64:
                            continue
                        kts.append((j, MS[:, g, s, :]))
                ob = cnt["o"] % 2
                cnt["o"] += 1
                Ob = PS[3 + ob]
                for ti, (j, mk) in enumerate(kts):
                    rk, lc = kloc(j)
                    nk = 16 if j == 0 else 128
                    kc = rk * T + lc * 128
                    vc = rk * 2 * VW + lc * 130 + g * 65
                    sb = PS[cnt["s"] % 3]
                    cnt["s"] += 1
                    pt = Pt[cnt["p"] % 4]
                    cnt["p"] += 1
                    u = {}

                    def s_(sb=sb, nk=nk, kc=kc, i=i, g=g):
                        mm(sb[0:nk, 0:384], KS[64 * g:64 * g + 64, kc:kc + nk], QS[64 * g:64 * g + 64, i * 384:(i + 1) * 384],
                           True, True, [KS, QS], [sb])

                    def e_(sb=sb, pt=pt, nk=nk, mk=mk):
                        if mk is None:
                            act(pt[0:nk, 0:384], sb[0:nk, 0:384], AF.Exp, [sb], [pt], scale=SC_SWA)
                        else:
                            ef = Ef[cnt["e"] % 2]
                            cnt["e"] += 1
                            act(ef[0:nk, 0:384], sb[0:nk, 0:384], AF.Exp, [sb], [ef], scale=SC_SWA)
                            tt("dve", pt[0:nk, 0:384], ef[0:nk, 0:384], mk, ALU.mult, [ef, MS, MSq], [pt])

                    def o_(Ob=Ob, pt=pt, nk=nk, vc=vc, first=(ti == 0), last=(ti == len(kts) - 1)):
                        mm(Ob[0:65, 0:384], VS[0:nk, vc:vc + 65], pt[0:nk, 0:384], first, last, [VS, pt], [Ob])

                    u["s"], u["e"], u["o"] = s_, e_, o_
                    if ti == len(kts) - 1:
                        yb = cnt["y"] % 2
                        cnt["y"] += 1

                        def post(Ob=Ob, i=i, g=g, yb=yb):
                            sr = sinkrow[64:65, l * 6 + 3 * g:l * 6 + 3 * g + 3, :].rearrange("p a b -> p (a b)")
                            tt("dve", rd[yb][64:65, 0:384], Ob[64:65, 0:384], sr, ALU.add, [Ob, sinkrow], [rd[yb]])
                            P.op("dve", lambda eng: eng.reciprocal(rd[yb][64:65, 0:384], rd[yb][64:65, 0:384]), [rd[yb]], [rd[yb]])
                            mm(PS[5][0:64, 0:384], ones_f[64:65, 0:64], rd[yb][64:65, 0:384], True, True, [ones_f, rd[yb]], [PS[5]])
                            cp("dve", bcs[yb][0:64, 0:384], PS[5][0:64, 0:384], [PS[5]], [bcs[yb]])
                            tt("dve", yo[yb][0:64, 0:384], Ob[0:64, 0:384], bcs[yb][0:64, 0:384], ALU.mult, [Ob, bcs[yb]], [yo[yb]])
                            dma("pool", DAP(yT, (640 + 3 * g * 64) * T + i * 128, (T, 64), (64 * T, 3), (1, 128)),
                                yo[yb][0:64, 0:384].rearrange("p (r c) -> p r c", c=128), [yo[yb]], [])

                        u["post"] = post
                    units.append(u)
        attention_core(units)

    def phase_ffn(l, last):
        AFa.reset(); ABa.reset()
        Wd = ABa.alloc(NF * D, "Wd", "p (f c) -> p f c", c=D)
        Wo = ABa.alloc(8 * D, "Wo", "p (k c) -> p k c", c=D)
        for f0 in range(0, NF, 6):
            f1 = min(NF, f0 + 6)
            dma("sp", Wd[:, f0:f1, :], DAP(Wb_d, l * 128 * NF * D + f0 * D, (NF * D, 128), (D, f1 - f0), (1, D)), [], [Wd])
        dma("sp", Wo[:, :, :], DAP(Wb_out, l * 128 * 8 * D, (8 * D, 128), (D, 8), (1, D)), [], [Wo])
        AT = ABa.alloc(NF * 512, "AT", "p (f c) -> p f c", c=512)
        HnT = ABa.alloc(8 * 512, "HnT", "p (k c) -> p k c", c=512)
        WGU = [ABa.alloc(2 * 1024, "WGU%d" % i, "p (u k c) -> p u k c", u=2, k=8) for i in range(2)]
        YT = [ABa.alloc(1024, "YT%d" % i, "p (k c) -> p k c", c=128) for i in range(2)]
        Hn = [ABa.alloc(1024, "Hn%d" % i) for i in range(2)]
        HT = AFa.alloc(4 * 1024, "HT", "p (b c) -> p b c", c=1024)
        Gff = AFa.alloc(1024, "Gff")
        dma("sp", Gff[:, :], DAP(ffn_norm, l * D, (0, 128), (1, D)), [], [Gff])
        if last:
            Gfin = AFa.alloc(1024, "Gfin")
            dma("sp", Gfin[:, :], DAP(final_norm, 0, (0, 128), (1, D)), [], [Gfin])
            OUTT = [AFa.alloc(1024, "OUTT%d" % i) for i in range(2)]
        SG = [AFa.alloc(512, "SG%d" % i) for i in range(2)]
        junk = AFa.alloc(1024, "junk")
        st = [AFa.alloc(16, "st%d" % i) for i in range(2)]
        tiles = [[0]] + [list(range(1 + 4 * t, 5 + 4 * t)) for t in range(8)]
        cnt = {"y": 0, "w": 0, "g": 0, "o": 0}
        for bl in tiles:
            nb = len(bl)
            NT = nb * 128
            for bi, i in enumerate(bl):
                yb = cnt["y"] % 2
                cnt["y"] += 1
                dma("sp", HT[:, bi, :], hsrc(l, i), [], [HT])
                dma("sp", YT[yb][:, :, :], DAP(yT, i * 128, (T, 128), (128 * T, 8), (1, 128)), [], [YT[yb]])
                for half in range(2):
                    for k in range(8):
                        mm(PS[half][:, 0:512], YT[yb][:, k, :], Wo[:, k, half * 512:(half + 1) * 512], k == 0, k == 7,
                           [YT[yb], Wo], [PS[half]])
                for half in range(2):
                    tt("dve", HT[:, bi, half * 512:(half + 1) * 512], HT[:, bi, half * 512:(half + 1) * 512], PS[half][:, 0:512],
                       ALU.add, [HT, PS[half]], [HT])
                S_ = st[yb]
                ttr(junk, junk[:, :], HT[:, bi, :], HT[:, bi, :], S_[:, 0:1], [HT], [S_])
                rstd_ops(S_, 0, 1024)
                stt("dve", Hn[yb][:, :], HT[:, bi, :], S_[:, 2:3], Gff[:, :], ALU.mult, ALU.mult, [HT, S_, Gff], [Hn[yb]])
                for k in range(8):
                    tr(psb(2)[:, k * 128:(k + 1) * 128], Hn[yb][:, k * 128:(k + 1) * 128], ident_b[:, :], [Hn[yb], ident_b], [PS[2]])
                cp("act", HnT[:, :, bi * 128:(bi + 1) * 128], psb(2)[:, :].rearrange("p (k c) -> p k c", c=128), [PS[2]], [HnT])
            for f in range(NF):
                wb = cnt["w"] % 2
                cnt["w"] += 1
                dma("sp", WGU[wb][:, 0, :, :], DAP(Wb_g, (l * NF + f) * 128 * 1024, (1024, 128), (128, 8), (1, 128)), [], [WGU[wb]])
                dma("sp", WGU[wb][:, 1, :, :], DAP(Wb_u, (l * NF + f) * 128 * 1024, (1024, 128), (128, 8), (1, 128)), [], [WGU[wb]])
                gb = cnt["g"] % 2
                cnt["g"] += 1
                PG, PU = PS[3 + 2 * gb], PS[4 + 2 * gb]
                for k in range(8):
                    mm(PG[:, 0:NT], WGU[wb][:, 0, k, :], HnT[:, k, 0:NT], k == 0, k == 7, [WGU[wb], HnT], [PG])
                for k in range(8):
                    mm(PU[:, 0:NT], WGU[wb][:, 1, k, :], HnT[:, k, 0:NT], k == 0, k == 7, [WGU[wb], HnT], [PU])
                act(SG[gb][:, 0:NT], PG[:, 0:NT], AF.Silu, [PG], [SG[gb]])
                tt("dve", AT[:, f, 0:NT], SG[gb][:, 0:NT], PU[:, 0:NT], ALU.mult, [SG[gb], PU], [AT])
            for bi, i in enumerate(bl):
                for half in range(2):
                    for f in range(NF):
                        mm(PS[half][:, 0:512], AT[:, f, bi * 128:(bi + 1) * 128], Wd[:, f, half * 512:(half + 1) * 512],
                           f == 0, f == NF - 1, [AT, Wd], [PS[half]])
                for half in range(2):
                    tt("dve", HT[:, bi, half * 512:(half + 1) * 512], HT[:, bi, half * 512:(half + 1) * 512], PS[half][:, 0:512],
                       ALU.add, [HT, PS[half]], [HT])
                if not last:
                    dma("pool", hres[i, :, :], HT[:, bi, :], [HT], [])
                if last and i >= 1:
                    ob = cnt["o"] % 2
                    cnt["o"] += 1
                    S_ = st[ob]
                    ttr(junk, junk[:, :], HT[:, bi, :], HT[:, bi, :], S_[:, 4:5], [HT], [S_])
                    rstd_ops(S_, 4, 1024)
                    stt("dve", OUTT[ob][:, :], HT[:, bi, :], S_[:, 6:7], Gfin[:, :], ALU.mult, ALU.mult, [HT, S_, Gfin], [OUTT[ob]])
                    dma("pool", out[i - 1, :, :], OUTT[ob][:, :], [OUTT[ob]], [])

    steps = [("weights", phase_weights), ("masks", lambda: (phase_masks(), phase_init()))]
    for l in range(2):
        steps.append(("proj%d" % l, lambda l=l: phase_proj(l)))
        steps.append(("gather%d" % l, lambda l=l: phase_gather(l)))
        steps.append(("mla%d" % l, lambda l=l: phase_attn_mla(l)))
        steps.append(("diff%d" % l, lambda l=l: phase_attn_diff(l)))
        steps.append(("swa%d" % l, lambda l=l: phase_attn_swa(l)))
        steps.append(("ffn%d" % l, lambda l=l: phase_ffn(l, l == 1)))
    barrier_before = ("masks", "proj0", "proj1", "ffn0", "ffn1", "diff0", "diff1", "swa0", "swa1")
    for si, (name, fn) in enumerate(steps):
        if stop is not None and si >= stop:
            break
        if name in barrier_before:
            P.barrier()
        fn()
    P.barrier()
    if debug and "kvown" in dump:
        dma("sp", kvodbg[:, :], kvown[:, :], [], [])
        P.barrier()

    ninst = emit_all(P, sems, dsems)
    for cm in reversed(ctx):
        cm.__exit__(None, None, None)
    return nc, ninst


def emit_all(P, sems, dsems):
    nc = P.nc
    engs = {"pe": nc.tensor, "act": nc.scalar, "dve": nc.vector, "pool": nc.gpsimd, "sp": nc.sync}
    for e in Prog.ENG:
        c = 0
        nd = 0
        for o in P.ops[e]:
            if o.dma:
                o.sem = dsems[e][nd % NDMASEM]
                o.val = 16 * (nd // NDMASEM + 1)
                nd += 1
            elif o.cc is not None:
                o.sem = o.cc
                o.val = 1
            elif o.signal:
                c += 1
                o.sem = sems[e]
                o.val = c
    ninst = 0
    for e in Prog.ENG:
        eng = engs[e]
        waited = {}
        for o in P.ops[e]:
            for d in o.deps:
                key = id(d.sem)
                if waited.get(key, 0) < d.val:
                    eng.wait_ge(d.sem, d.val)
                    waited[key] = d.val
                    ninst += 1
            if o.fn is not None:
                ins = o.fn(eng)
                ninst += 1
                if o.dma:
                    ins.then_inc(o.sem, 16)
                elif o.cc is not None:
                    ins.then_inc(o.sem)
                elif o.signal:
                    ins.then_inc(o.sem, 1)
    return ninst


_CACHE = {}


def _get_program(debug=False):
    if debug not in _CACHE:
        _CACHE[debug] = build_program(debug)
    return _CACHE[debug]


PARAM_NAMES = ["meta_tokens", "rel_bias", "attn_norm", "w_in", "mla_q_norm", "mla_w_qb", "mla_kv_norm", "mla_w_kvb",
               "diff_lambda", "diff_subln", "swa_sinks", "w_out", "ffn_norm", "w_gate", "w_up", "w_down", "final_norm"]


def make_in_maps(inputs):
    x = np.ascontiguousarray(np.asarray(inputs["x"], dtype=np.float32))
    params = {k: np.ascontiguousarray(np.asarray(inputs[k], dtype=np.float32)) for k in PARAM_NAMES}
    consts = [make_consts(r) for r in range(NR)]
    in_maps = []
    for c in range(8):
        b, r = c // 2, c % 2
        xb = x[b].reshape(64, 128, D)[r::2]
        m = {"xin": np.ascontiguousarray(xb)}
        m.update(params)
        m.update(consts[r])
        in_maps.append(m)
    return in_maps


def kernel(**inputs):
    nc, _ = _get_program(False)
    in_maps = make_in_maps(inputs)
    res = run_bass_kernel_spmd(nc, in_maps, core_ids=list(range(8)))
    outp = np.empty((4, 64, 128, D), np.float32)
    for c in range(8):
        b, r = c // 2, c % 2
        outp[b, r::2] = np.asarray(res.results[c]["out"]).reshape(32, 128, D)
    return outp.reshape(4, 8192, D)
```

```python
import math
import numpy as np
import concourse.bass as bass
import concourse.mybir as mybir
from concourse.bass_utils import run_bass_kernel_spmd

F32 = mybir.dt.float32
BF16 = mybir.dt.bfloat16
AF = mybir.ActivationFunctionType
ALU = mybir.AluOpType

D = 1024
DIN = 1824
DFF = 2816
NF = DFF // 128
NR = 2
NLOC = 33
T = NLOC * 128
VW = NLOC * 65
KVROWS = 1740
RM, RD, RS0, RS1 = 161, 129, 128, 130
OFF_M = 0
OFF_D = OFF_M + 6 * RM * T
OFF_S0 = OFF_D + 4 * RD * T
OFF_S1 = OFF_S0 + RS0 * T
assert OFF_S1 + RS1 * T == KVROWS * T
PIECES = [("m%d" % h, h * RM, RM) for h in range(6)] + [("d%d" % h, 6 * RM + h * RD, RD) for h in range(4)] + \
         [("s0", 6 * RM + 4 * RD, RS0), ("s1", 6 * RM + 4 * RD + RS0, RS1)]
NME = 41
SC_MLA = 96 ** -0.5
SC_DIFF = 32 ** -0.5
SC_SWA = 64 ** -0.5
EPS = 1e-6
WARM = True
LAM_INIT = [0.8 - 0.6 * math.exp(-0.3 * l) for l in range(2)]
WIN_SEGS = [(0, 928), (1184, 1696), (928, 1184), (1696, 1824)]
PGRP = [(0, 416), (416, 928), (928, 1440), (1440, 1824)]


class Buf:
    __slots__ = ("name", "ws", "r", "psum", "pr")

    def __init__(self, name):
        self.name = name
        self.ws = []
        self.r = []
        self.pr = []
        self.psum = False


class Op:
    __slots__ = ("e", "fn", "dma", "deps", "signal", "sem", "val", "idx", "cc")

    def __init__(self, e, fn, dma):
        self.e = e
        self.fn = fn
        self.dma = dma
        self.deps = []
        self.signal = False
        self.sem = None
        self.val = 0
        self.cc = None


class Tile:
    def __init__(self, ap, name):
        self.ap = ap
        self.buf = Buf(name)

    def __getitem__(self, k):
        return self.ap[k]


NDMASEM = 20


class Prog:
    ENG = ("pe", "act", "dve", "pool", "sp")

    def __init__(self, nc):
        self.nc = nc
        self.ops = {e: [] for e in self.ENG}
        self.dma_ops = {e: [] for e in self.ENG}
        self.bufs = []

    def buf(self, name):
        b = Buf(name)
        self.bufs.append(b)
        return b

    def tile(self, ap, name):
        t = Tile(ap, name)
        self.bufs.append(t.buf)
        return t

    def op(self, e, fn, reads=(), writes=(), dma=False):
        o = Op(e, fn, dma)
        deps = []
        seen = set()

        def add(d):
            if d is None or id(d) in seen:
                return
            seen.add(id(d))
            if d.e == "pe" and e == "pe" and not d.dma and not dma:
                return
            deps.append(d)

        rb = [t.buf if isinstance(t, Tile) else t for t in reads]
        wb = [t.buf if isinstance(t, Tile) else t for t in writes]
        for b in rb:
            for x in b.ws:
                add(x)
            if b.psum:
                for x in b.r:
                    if x.e != e:
                        add(x)
        for b in wb:
            for x in b.r:
                add(x)
            for x in b.pr:
                add(x)
            for x in b.ws:
                if not (x.dma and dma):
                    add(x)
        if dma:
            lst = self.dma_ops[e]
            k = len(lst)
            if k >= NDMASEM:
                add(lst[k - NDMASEM])
            lst.append(o)
        for d in deps:
            d.signal = True
        o.deps = deps
        wset = set(id(b) for b in wb)
        for b in wb:
            if b.r or any(b is x for x in rb):
                b.pr = list(b.r)
                b.ws = [o]
                b.r = []
            else:
                b.ws.append(o)
        for b in rb:
            if id(b) not in wset:
                b.r.append(o)
        self.ops[e].append(o)
        return o

    def barrier(self):
        lasts = []
        for e in self.ENG:
            for o in reversed(self.ops[e]):
                if o.fn is not None and not o.dma:
                    lasts.append(o)
                    break
            lasts.extend(self.dma_ops[e][-NDMASEM:])
        for e in self.ENG:
            o = Op(e, None, False)
            o.deps = [d for d in lasts]
            self.ops[e].append(o)
        for d in lasts:
            d.signal = True
        for b in self.bufs:
            b.ws = []
            b.r = []
            b.pr = []


def DAP(t, off, *dims):
    return bass.AP(t, off, [[s, n] for (s, n) in dims])


class Arena:
    def __init__(self, P, ap, size, name):
        self.P = P
        self.ap = ap
        self.size = size
        self.off = 0
        self.name = name

    def reset(self):
        self.off = 0

    def alloc(self, n, name, pat=None, **kw):
        n2 = (n + 15) // 16 * 16
        assert self.off + n2 <= self.size, (self.name, name, self.off, n2, self.size)
        ap = self.ap[:, self.off:self.off + n]
        self.off += n2
        if pat is not None:
            ap = ap.rearrange(pat, **kw)
        return self.P.tile(ap, name)


def t5_bucket_np(n):
    n = np.maximum(n, 0)
    nf = np.maximum(n, 16).astype(np.float32)
    large = 16 + (np.log(nf / np.float32(16)) / np.float32(math.log(128 / 16)) * np.float32(16)).astype(np.int32)
    large = np.minimum(large, 31)
    return np.where(n < 16, n, large)


def make_consts(r):
    c = {}
    inv_freq = (10000.0 ** (-np.arange(0, 32, 2, dtype=np.float32) / np.float32(32))).astype(np.float32)
    cs = np.zeros((128, NLOC, 64), np.float32)
    for i in range(NLOC):
        p = np.arange(128)
        if i == 0:
            pos = np.minimum(p, 15)
        else:
            G = NR * (i - 1) + 1 + r
            pos = (G - 1) * 128 + p + 16
        ang = pos.astype(np.float32)[:, None] * inv_freq[None, :]
        co = np.cos(ang).astype(np.float32)
        si = np.sin(ang).astype(np.float32)
        cs[:, i, 0:16] = co
        cs[:, i, 16:32] = co
        cs[:, i, 32:48] = -si
        cs[:, i, 48:64] = si
    c["cs_tab"] = cs.reshape(128, NLOC * 64)
    k = np.arange(128)[:, None]
    q = np.arange(128)[None, :]
    tri = (k <= q).astype(np.float32)
    one = np.ones((128, 128), np.float32)
    zero = np.zeros((128, 128), np.float32)
    mA = np.zeros((128, 9, 4, 128), np.float32)
    for s in range(8):
        for qb in range(4):
            j = 1 + s
            G = 2 * qb + 1 + r
            mA[:, s, qb, :] = one if j < G else (tri if j == G else zero)
    mA[:, 8, 0, :] = tri
    c["mA"] = mA.reshape(128, 9 * 512)
    cd = np.zeros((4, NME), np.float32)
    for s in range(9):
        for qb in range(4):
            j = s
            G = 2 * qb + 1 + r
            e = s * 4 + qb
            if j < G - 1:
                cd[0, e] = 1
            elif j == G - 1:
                cd[2, e] = 1
            elif j == G:
                cd[1, e] = 1
    for qb in range(4):
        G = 2 * qb + 1 + r
        if G == 1:
            cd[3, 36 + qb] = 1
        else:
            cd[0, 36 + qb] = 1
    cd[1, 40] = 1
    c["cdiff"] = np.broadcast_to(cd.reshape(1, 4 * NME), (128, 4 * NME)).copy()
    csw = np.zeros((4, 4), np.float32)
    if r == 0:
        csw[1, 0] = 1; csw[0, 1] = 1
        csw[3, 3] = 1
    else:
        csw[1, 1] = 1; csw[0, 2] = 1
        csw[2, 3] = 1
    c["cswa"] = np.broadcast_to(csw.reshape(1, 16), (128, 16)).copy()
    oh = np.zeros((32, 768), np.float32)
    al = np.zeros((10, 768), np.float32)
    for j in range(256):
        if j < 128:
            oh[t5_bucket_np(np.array(j))[()], j] = 1; al[:, j] = 1
        if j < 128:
            oh[31, 256 + j] = 1; al[0:4, 256 + j] = 1
        elif j > 128:
            oh[t5_bucket_np(np.array(j - 128))[()], 256 + j] = 1; al[:, 256 + j] = 1
        if j < 128:
            oh[t5_bucket_np(np.array(j + 16))[()], 512 + j] = 1; al[:, 512 + j] = 1
        elif j >= 241:
            oh[t5_bucket_np(np.array(j - 240))[()], 512 + j] = 1; al[:, 512 + j] = 1
    c["oh"] = oh
    c["aallow"] = al
    c["ident"] = np.eye(128, dtype=np.float32)
    return c


def build_program(debug=False, stop=None, dump=(), ncores=8, nblk=NLOC, cut=99):
    nc = bass.Bass("TRN2", target_bir_lowering=False)
    P = Prog(nc)

    def din(name, shape, dt=F32):
        return nc.dram_tensor(name, list(shape), dt, kind="ExternalInput")

    xin = din("xin", [32, 128, D])
    meta_tokens = din("meta_tokens", [16, D])
    rel_bias = din("rel_bias", [32, 10])
    attn_norm = din("attn_norm", [2, D])
    w_in = din("w_in", [2, D, DIN])
    mla_q_norm = din("mla_q_norm", [2, 256])
    mla_w_qb = din("mla_w_qb", [2, 256, 576])
    mla_kv_norm = din("mla_kv_norm", [2, 128])
    mla_w_kvb = din("mla_w_kvb", [2, 128, 768])
    diff_lambda = din("diff_lambda", [2, 4, 32])
    diff_subln = din("diff_subln", [2, 64])
    swa_sinks = din("swa_sinks", [2, 6])
    w_out = din("w_out", [2, D, D])
    ffn_norm = din("ffn_norm", [2, D])
    w_gate = din("w_gate", [2, D, DFF])
    w_up = din("w_up", [2, D, DFF])
    w_down = din("w_down", [2, DFF, D])
    final_norm = din("final_norm", [D])
    cs_tab = din("cs_tab", [128, NLOC * 64])
    mA_in = din("mA", [128, 9 * 512])
    cdiff_in = din("cdiff", [128, 4 * NME])
    cswa_in = din("cswa", [128, 16])
    oh_in = din("oh", [32, 768])
    aallow_in = din("aallow", [10, 768])
    ident_in = din("ident", [128, 128])
    out = nc.dram_tensor("out", [32, 128, D], F32, kind="ExternalOutput")

    def dscr(name, shape, dt):
        if debug and name in dump:
            return nc.dram_tensor(name, list(shape), dt, kind="ExternalOutput")
        return nc.dram_tensor(name, list(shape), dt)

    hres = dscr("hres", [NLOC, 128, D], F32)
    kvown = nc.dram_tensor("kvown", [KVROWS, T], BF16)
    kvall = {nm: nc.dram_tensor("kvall_" + nm, [NR * nr, T], BF16) for (nm, r0, nr) in PIECES}
    pbuf = {nm: P.buf("piece_" + nm) for (nm, r0, nr) in PIECES}
    QTm = dscr("QTm", [6 * 96, T], BF16)
    QTd = dscr("QTd", [256, T], BF16)
    QTs = dscr("QTs", [128, 3 * T], BF16)
    yT = dscr("yT", [D, T], BF16)
    Wb_in = dscr("Wb_in", [2, 128, 8 * DIN], BF16)
    Wb_qb = dscr("Wb_qb", [2, 128, 2 * 576], BF16)
    Wb_kvb = dscr("Wb_kvb", [2, 128, 768], BF16)
    Wb_out = dscr("Wb_out", [2, 128, 8 * D], BF16)
    Wb_g = dscr("Wb_g", [2, NF, 128, 8 * 128], BF16)
    Wb_u = dscr("Wb_u", [2, NF, 128, 8 * 128], BF16)
    Wb_d = dscr("Wb_d", [2, 128, NF * D], BF16)
    Gd = dscr("Gd", [10, 768], F32)
    zper = dscr("zper", [30, 129 * 256], F32)
    maskD = dscr("maskD", [4, 128, NME * 128], F32)
    maskS = dscr("maskS", [128, 2 * 4 * 384], F32)
    if debug and "kvown" in dump:
        kvodbg = nc.dram_tensor("kvodbg", [KVROWS, T], BF16, kind="ExternalOutput")

    ctx = []

    def enter(cm):
        ctx.append(cm)
        return cm.__enter__()

    NFA = 17 * 1024
    NBA = 56 * 1024
    arena_f_t = enter(nc.sbuf_tensor("arena_f", [128, NFA], F32))
    arena_b_t = enter(nc.sbuf_tensor("arena_b", [128, NBA], BF16))
    psum_t = enter(nc.psum_tensor("psum", [128, 8 * 512], F32))
    sems = {e: enter(nc.semaphore("sem_" + e)) for e in Prog.ENG}
    dsems = {e: [enter(nc.semaphore("dsem_%s_%d" % (e, i))) for i in range(NDMASEM)] for e in ("sp", "act", "pool")}
    ccsems = [enter(nc.semaphore("ccsem%d" % i)) for i in range(2 * len(PIECES))]
    blk = enter(nc.Block())

    AFa = Arena(P, arena_f_t[:, :], NFA, "f32")
    ABa = Arena(P, arena_b_t[:, :], NBA, "bf16")
    PS = [P.tile(psum_t[:, b * 512:(b + 1) * 512], "ps%d" % b) for b in range(8)]
    for t_ in PS:
        t_.buf.psum = True

    def psb(b):
        return psum_t[:, b * 512:(b + 1) * 512].bitcast(BF16)

    def dma(q, out_ap, in_ap, reads=(), writes=()):
        return P.op(q, lambda eng, o=out_ap, i=in_ap: eng.dma_start(out=o, in_=i), reads, writes, dma=True)

    def mm(out_ap, lhsT, rhs, start, stop, reads, writes, tp=None):
        if tp is not None:
            return P.op("pe", lambda eng: eng.matmul(out_ap, lhsT, rhs, start=start, stop=stop, tile_position=tp), reads, writes)
        return P.op("pe", lambda eng: eng.matmul(out_ap, lhsT, rhs, start=start, stop=stop), reads, writes)

    def tr(out_ap, in_ap, ident_ap, reads, writes):
        return P.op("pe", lambda eng: eng.transpose(out_ap, in_ap, ident_ap), reads, writes)

    def act(out_ap, in_ap, func, reads, writes, scale=1.0, bias=0.0, accum=None):
        if accum is not None:
            return P.op("act", lambda eng: eng.activation(out_ap, in_ap, func, bias=bias, scale=scale, accum_out=accum),
                        reads, writes)
        return P.op("act", lambda eng: eng.activation(out_ap, in_ap, func, bias=bias, scale=scale), reads, writes)

    def tt(e, out_ap, in0, in1, op, reads, writes):
        return P.op(e, lambda eng: eng.tensor_tensor(out_ap, in0, in1, op), reads, writes)

    def ts(e, out_ap, in0, s1, s2, op0, op1, reads, writes):
        if op1 is None:
            return P.op(e, lambda eng: eng.tensor_scalar(out_ap, in0, s1, None, op0), reads, writes)
        return P.op(e, lambda eng: eng.tensor_scalar(out_ap, in0, s1, s2, op0, op1), reads, writes)

    def stt(e, out_ap, in0, scalar, in1, op0, op1, reads, writes):
        return P.op(e, lambda eng: eng.scalar_tensor_tensor(out_ap, in0, scalar, in1, op0, op1), reads, writes)

    def cp(e, out_ap, in_ap, reads, writes):
        if e == "act":
            return P.op("act", lambda eng: eng.copy(out_ap, in_ap), reads, writes)
        return P.op(e, lambda eng: eng.tensor_copy(out_ap, in_ap), reads, writes)

    def ttr(jt, out_ap, in0, in1, acc, reads, writes):
        return act(out_ap, in0, AF.Square, list(reads) + [jt], list(writes) + [jt], accum=acc)

    def memset(e, ap, val, writes):
        return P.op(e, lambda eng: eng.memset(ap, val), (), writes)

    def rstd_ops(st, c0, n, reads_extra=()):
        act(st[:, c0 + 1:c0 + 2], st[:, c0:c0 + 1], AF.Ln, [st], [st], scale=1.0 / n, bias=EPS)
        act(st[:, c0 + 2:c0 + 3], st[:, c0 + 1:c0 + 2], AF.Exp, [st], [st], scale=-0.5)

    NPERS_F = 128 + 64 + 2 + 2 + 12 * 128 + 16
    pers_f = Arena(P, arena_f_t[:, NFA - 2048:NFA], 2048, "persf")
    NFA_USE = NFA - 2048
    AFa.size = NFA_USE
    pers_b = Arena(P, arena_b_t[:, NBA - 256:NBA], 256, "persb")
    ABa.size = NBA - 256
    ones_f = pers_f.alloc(128, "ones_f")
    lam_t = pers_f.alloc(16, "lam_t")
    gsub_t = pers_f.alloc(16, "gsub_t")
    sinkrow = pers_f.alloc(12 * 128, "sinkrow", "p (a b) -> p a b", b=128)
    ident_b = pers_b.alloc(128, "ident_b")
    sel_f = pers_f.alloc(64, "sel_f")

    def phase_weights():
        AFa.reset(); ABa.reset()
        stg = [AFa.alloc(2816, "wstg%d" % i) for i in range(2)]
        stb = [ABa.alloc(2816, "wstb%d" % i) for i in range(2)]
        cnt = [0]

        def one(src_aps, ncols, dst_ap):
            i = cnt[0] % 2
            cnt[0] += 1
            for (so, do, n, ap) in src_aps:
                dma("sp", stg[i][:, do:do + n], ap, [], [stg[i]])
            e = "dve" if (cnt[0] % 2 == 0) else "act"
            cp(e, stb[i][:, 0:ncols], stg[i][:, 0:ncols], [stg[i]], [stb[i]])
            dma("pool", dst_ap, stb[i][:, 0:ncols], [stb[i]], [])

        for l in range(2):
            for k in range(8):
                srcs = []
                do = 0
                for (a, b) in WIN_SEGS:
                    srcs.append((a, do, b - a, w_in[l, k * 128:(k + 1) * 128, a:b]))
                    do += b - a
                one(srcs, DIN, DAP(Wb_in, l * 128 * 8 * DIN + k * DIN, (8 * DIN, 128), (1, DIN)))
            for k in range(2):
                one([(0, 0, 576, mla_w_qb[l, k * 128:(k + 1) * 128, :])], 576,
                    DAP(Wb_qb, l * 128 * 1152 + k * 576, (1152, 128), (1, 576)))
            one([(0, 0, 768, mla_w_kvb[l, :, :])], 768, DAP(Wb_kvb, l * 128 * 768, (768, 128), (1, 768)))
            for k in range(8):
                one([(0, 0, D, w_out[l, k * 128:(k + 1) * 128, :])], D,
                    DAP(Wb_out, l * 128 * 8 * D + k * D, (8 * D, 128), (1, D)))

    def phase_masks():
        AFa.reset(); ABa.reset()
        memset("dve", ones_f[:, :], 1.0, [ones_f])
        idf = AFa.alloc(128, "idf")
        dma("sp", idf[:, :], ident_in[:, :], [], [idf])
        cp("dve", ident_b[:, :], idf[:, :], [idf], [ident_b])
        tt("dve", sel_f[:, 0:64], idf[:, 0:64], idf[:, 64:128], ALU.subtract, [idf], [sel_f])
        lp = AFa.alloc(256, "lp")
        dma("sp", lp[:, :], DAP(diff_lambda, 0, (0, 128), (1, 256)), [], [lp])
        junk = AFa.alloc(64, "junk")
        lst = AFa.alloc(16, "lst")
        for l in range(2):
            for pr in range(2):
                a = l * 128 + pr * 64
                tt("dve", junk[:, 0:32], lp[:, a:a + 32], lp[:, a + 32:a + 64], ALU.mult, [lp], [junk])
                P.op("dve", lambda eng, o_=lst[:, 2 * l + pr:2 * l + pr + 1]: eng.reduce_sum(o_, junk[:, 0:32], mybir.AxisListType.X),
                     [junk], [lst])
        lex = AFa.alloc(16, "lex")
        act(lex[:, 0:4], lst[:, 0:4], AF.Exp, [lst], [lex])
        for l in range(2):
            tt("dve", lam_t[:, l:l + 1], lex[:, 2 * l:2 * l + 1], lex[:, 2 * l + 1:2 * l + 2], ALU.subtract, [lex], [lam_t])
            ts("dve", lam_t[:, l:l + 1], lam_t[:, l:l + 1], LAM_INIT[l], None, ALU.add, None, [lam_t], [lam_t])
        gs0 = AFa.alloc(16, "gs0")
        for l in range(2):
            dma("sp", gs0[0:64, l:l + 1], DAP(diff_subln, 64 * l, (1, 64), (1, 1)), [], [gs0])
        for l in range(2):
            ts("dve", gsub_t[0:64, l:l + 1], gs0[0:64, l:l + 1], 1.0 - LAM_INIT[l], None, ALU.mult, None, [gs0], [gsub_t])
        rb = AFa.alloc(16, "rb")
        dma("sp", rb[0:32, 0:10], rel_bias[:, :], [], [rb])
        oh = AFa.alloc(768, "oh")
        dma("sp", oh[0:32, :], oh_in[:, :], [], [oh])
        al = AFa.alloc(768, "al")
        dma("sp", al[0:10, :], aallow_in[:, :], [], [al])
        b31 = AFa.alloc(16, "b31")
        dma("sp", b31[0:10, 0:1], DAP(rel_bias, 310, (1, 10), (1, 1)), [], [b31])
        ts("dve", b31[0:10, 1:2], b31[0:10, 0:1], -1.0, None, ALU.mult, None, [b31], [b31])
        mm(PS[0][0:10, 0:512], rb[0:32, 0:10], oh[0:32, 0:512], True, True, [rb, oh], [PS[0]])
        mm(PS[1][0:10, 0:256], rb[0:32, 0:10], oh[0:32, 512:768], True, True, [rb, oh], [PS[1]])
        gl = AFa.alloc(768, "gl")
        act(gl[0:10, 0:512], PS[0][0:10, 0:512], AF.Exp, [PS[0], b31], [gl], bias=b31[0:10, 1:2])
        act(gl[0:10, 512:768], PS[1][0:10, 0:256], AF.Exp, [PS[1], b31], [gl], bias=b31[0:10, 1:2])
        tt("dve", gl[0:10, :], gl[0:10, :], al[0:10, :], ALU.mult, [gl, al], [gl])
        o1 = dma("sp", Gd[:, :], gl[0:10, :], [gl], [])
        gdb = P.buf("gd")
        gdb.ws = [o1]
        zb = P.buf("zper")
        for hv in range(30):
            dma("pool", DAP(zper, hv * 129 * 256, (256, 129), (1, 256)), DAP(Gd, hv * 256, (0, 129), (1, 256)), [gdb], [zb])
        P.barrier()
        AFa.reset()
        Tall = AFa.alloc(30 * 128, "Tall", "p (a b) -> p a b", b=128)
        for hv in range(30):
            dma("sp" if hv % 2 == 0 else "pool", Tall[:, hv, :], DAP(zper, hv * 129 * 256, (255, 128), (1, 128)), [], [Tall])
        sk = AFa.alloc(32, "sk")
        dma("sp", sk[64:65, 0:12], DAP(swa_sinks, 0, (0, 1), (1, 12)), [], [sk])
        dma("sp", sk[64:65, 16:22], DAP(rel_bias, 314, (0, 1), (1, 6)), [], [sk])
        for l in range(2):
            tt("dve", sk[64:65, 6 * l:6 * l + 6], sk[64:65, 6 * l:6 * l + 6], sk[64:65, 16:22], ALU.subtract, [sk], [sk])
        act(sk[64:65, 0:12], sk[64:65, 0:12], AF.Exp, [sk], [sk])
        cp("dve", sinkrow[64:65, :, :], sk[64:65, 0:12].unsqueeze(2).broadcast_to([1, 12, 128]), [sk], [sinkrow])
        cdf = AFa.alloc(4 * NME, "cdf")
        dma("sp", cdf[:, :], cdiff_in[:, :], [], [cdf])
        NH = 21
        Mt = [AFa.alloc(NH * 128, "Mt%d" % i, "p (a b) -> p a b", b=128) for i in range(2)]

        for h in range(4):
            for (e0, e1) in ((0, NH), (NH, NME)):
                ne = e1 - e0

                def cb(kind):
                    return cdf[:, kind * NME + e0:kind * NME + e1].unsqueeze(2).broadcast_to([128, ne, 128])

                def tb(v):
                    return Tall[:, h * 3 + v, :].unsqueeze(1).broadcast_to([128, ne, 128])

                M, M2 = Mt[0][:, 0:ne, :], Mt[1][:, 0:ne, :]
                tt("dve", M, cb(1), tb(0), ALU.mult, [cdf, Tall], [Mt[0]])
                tt("pool", M2, cb(2), tb(1), ALU.mult, [cdf, Tall], [Mt[1]])
                tt("dve", M, M, M2, ALU.add, [Mt[0], Mt[1]], [Mt[0]])
                tt("pool", M2, cb(3), tb(2), ALU.mult, [cdf, Tall], [Mt[1]])
                tt("dve", M, M, M2, ALU.add, [Mt[0], Mt[1]], [Mt[0]])
                tt("dve", M, M, cb(0), ALU.add, [Mt[0], cdf], [Mt[0]])
                dma("sp", DAP(maskD, h * 128 * NME * 128 + e0 * 128, (NME * 128, 128), (1, ne * 128)),
                    M.rearrange("p a b -> p (a b)"), [Mt[0]], [])
        csf = AFa.alloc(16, "csf")
        dma("sp", csf[:, :], cswa_in[:, :], [], [csf])
        MS = AFa.alloc(2 * 4 * 384, "MSb", "p (g e r c) -> p g e r c", g=2, e=4, r=3)
        for g in range(2):
            h0 = 4 + 3 * g
            Tc = Tall[:, :, :].rearrange("p (h v) c -> p h v c", v=3)[:, h0:h0 + 3, 0, :]
            Tp = Tall[:, :, :].rearrange("p (h v) c -> p h v c", v=3)[:, h0:h0 + 3, 1, :]
            Tm = Tall[:, :, :].rearrange("p (h v) c -> p h v c", v=3)[:, h0:h0 + 3, 2, :]
            for e in range(3):
                ts("dve", MS[:, g, e, :, :], Tc, csf[:, 0 + e:0 + e + 1], None, ALU.mult, None, [Tall, csf], [MS])
                stt("dve", MS[:, g, e, :, :], Tp, csf[:, 4 + e:4 + e + 1], MS[:, g, e, :, :], ALU.mult, ALU.add,
                    [Tall, csf, MS], [MS])
            ts("dve", MS[:, g, 3, :, :], Tm, csf[:, 12 + 3:12 + 4], None, ALU.mult, None, [Tall, csf], [MS])
            ts("dve", MS[:, g, 3, :, :], MS[:, g, 3, :, :], csf[:, 8 + 3:8 + 4], None, ALU.add, None, [MS, csf], [MS])
        dma("sp", maskS[:, :], MS[:, :, :, :, :].rearrange("p g e r c -> p (g e r c)"), [MS], [])
        dma("sp", DAP(zper, 0, (6 * 128, 16), (128, 6), (1, 128)),
            Tall[:, :, :].rearrange("p (h v) c -> p h v c", v=3)[0:16, 4:10, 0, :], [Tall], [])

    def phase_init():
        AFa.reset(); ABa.reset()
        z = AFa.alloc(1024, "zinit")
        memset("dve", z[:, :], 0.0, [z])
        dma("sp", hres[0, 16:128, :], z[16:128, :], [z], [])
        dma("sp", hres[0, 0:16, :], meta_tokens[:, :], [], [])

    def hsrc(l, i):
        if l == 0 and i >= 1:
            return xin[i - 1, :, :]
        return hres[i, :, :]

    def phase_proj(l):
        AFa.reset(); ABa.reset()
        Win = ABa.alloc(8 * DIN, "Win", "p (k c) -> p k c", c=DIN)
        Wqb = ABa.alloc(2 * 576, "Wqb", "p (k c) -> p k c", c=576)
        Wkvb = ABa.alloc(768, "Wkvb")
        dma("sp", Win[:, :, :], DAP(Wb_in, l * 128 * 8 * DIN, (8 * DIN, 128), (DIN, 8), (1, DIN)), [], [Win])
        dma("sp", Wqb[:, :, :], DAP(Wb_qb, l * 128 * 1152, (1152, 128), (576, 2), (1, 576)), [], [Wqb])
        dma("sp", Wkvb[:, :], DAP(Wb_kvb, l * 128 * 768, (768, 128), (1, 768)), [], [Wkvb])
        Gat = AFa.alloc(1024, "Gat")
        Gq = AFa.alloc(256, "Gq")
        Gkv = AFa.alloc(128, "Gkv")
        dma("sp", Gat[:, :], DAP(attn_norm, l * D, (0, 128), (1, D)), [], [Gat])
        dma("sp", Gq[:, :], DAP(mla_q_norm, l * 256, (0, 128), (1, 256)), [], [Gq])
        dma("sp", Gkv[:, :], DAP(mla_kv_norm, l * 128, (0, 128), (1, 128)), [], [Gkv])
        cs = AFa.alloc(NLOC * 64, "cs", "p (i c) -> p i c", c=64)
        dma("sp", cs[:, :, :], cs_tab[:, :].rearrange("p (i c) -> p i c", c=64), [], [cs])
        junk = AFa.alloc(1024, "junk")
        junk2 = AFa.alloc(256, "junk2")
        Hh = [AFa.alloc(1024, "Hh%d" % i) for i in range(2)]
        st = [AFa.alloc(16, "st%d" % i) for i in range(2)]
        st2 = [AFa.alloc(16, "st2_%d" % i) for i in range(2)]
        ra = [AFa.alloc(6 * 32, "ra%d" % i, "p (h c) -> p h c", c=32) for i in range(2)]
        rb_ = [AFa.alloc(6 * 32, "rb%d" % i, "p (h c) -> p h c", c=32) for i in range(2)]
        Craw = [AFa.alloc(416, "Craw%d" % i) for i in range(2)]
        Hn = [ABa.alloc(1024, "Hn%d" % i) for i in range(2)]
        HnT = [ABa.alloc(1024, "HnT%d" % i) for i in range(2)]
        Cn = [ABa.alloc(384, "Cn%d" % i) for i in range(2)]
        CT = [ABa.alloc(384, "CT%d" % i) for i in range(2)]
        Qf = [ABa.alloc(576, "Qf%d" % i, "p (h c) -> p h c", c=96) for i in range(2)]
        Kf = [ABa.alloc(576, "Kf%d" % i, "p (h c) -> p h c", c=96) for i in range(2)]
        Vf = [ABa.alloc(390, "Vf%d" % i, "p (h c) -> p h c", c=65) for i in range(2)]
        krot = [ABa.alloc(32, "krot%d" % i) for i in range(2)]
        QT = [ABa.alloc(768, "QT%d" % i) for i in range(2)]
        KT = [ABa.alloc(768, "KT%d" % i) for i in range(2)]
        Df = [ABa.alloc(512, "Df%d" % i) for i in range(2)]
        DT = [ABa.alloc(512, "DT%d" % i) for i in range(2)]
        Vdf = [ABa.alloc(260, "Vdf%d" % i, "p (h c) -> p h c", c=65) for i in range(2)]
        Sf = [ABa.alloc(512, "Sf%d" % i) for i in range(2)]
        ST = [ABa.alloc(512, "ST%d" % i) for i in range(2)]
        Vsf = [ABa.alloc(130, "Vsf%d" % i, "p (h c) -> p h c", c=65) for i in range(2)]
        for i in range(2):
            memset("pool", Vf[i][:, :, 64:65], 1.0, [Vf[i]])
            memset("pool", Vdf[i][:, :, 64:65], 1.0, [Vdf[i]])
            memset("pool", Vsf[i][:, :, 64:65], 1.0, [Vsf[i]])

        def rope(src3, dst3, H, i, db, src_t, dst_t):
            A = ra[db][:, 0:H, :]
            B = rb_[db][:, 0:H, :]
            cA = cs[:, i, 0:32].unsqueeze(1).broadcast_to([128, H, 32])
            sB0 = cs[:, i, 32:48].unsqueeze(1).broadcast_to([128, H, 16])
            sB1 = cs[:, i, 48:64].unsqueeze(1).broadcast_to([128, H, 16])
            tt("dve", A, src3, cA, ALU.mult, [src_t, cs], [ra[db]])
            tt("dve", B[:, :, 0:16], src3[:, :, 16:32], sB0, ALU.mult, [src_t, cs], [rb_[db]])
            tt("dve", B[:, :, 16:32], src3[:, :, 0:16], sB1, ALU.mult, [src_t, cs], [rb_[db]])
            tt("dve", dst3, A, B, ALU.add, [ra[db], rb_[db]], [dst_t])

        def front(i):
            db = i % 2
            H, S_, hn, hnT = Hh[db], st[db], Hn[db], HnT[db]
            dma("sp", H[:, :], hsrc(l, i), [], [H])
            ttr(junk, junk[:, :], H[:, :], H[:, :], S_[:, 0:1], [H], [S_])
            rstd_ops(S_, 0, 1024)
            stt("dve", hn[:, :], H[:, :], S_[:, 2:3], Gat[:, :], ALU.mult, ALU.mult, [H, S_, Gat], [hn])
            for k in range(8):
                tr(psb(6)[:, k * 128:(k + 1) * 128], hn[:, k * 128:(k + 1) * 128], ident_b[:, :], [hn, ident_b], [PS[6]])
            cp("act", hnT[:, :], psb(6)[:, :], [PS[6]], [hnT])
            for g, (c0, c1) in enumerate(PGRP):
                for k in range(8):
                    mm(PS[g][:, 0:c1 - c0], hnT[:, k * 128:(k + 1) * 128], Win[:, k, c0:c1], k == 0, k == 7,
                       [hnT, Win], [PS[g]])
            cr = Craw[db]
            cp("act", cr[:, 0:416], PS[0][:, 0:416], [PS[0]], [cr])
            cp("dve", Df[db][:, :], PS[1][:, 0:512], [PS[1]], [Df[db]])
            cp("act", Sf[db][:, 0:384].rearrange("p (r g c) -> p g r c", r=3, g=2),
               PS[2][:, 0:384].rearrange("p (g r c) -> p g r c", g=2, r=3), [PS[2]], [Sf[db]])
            cp("act", Sf[db][:, 384:512], PS[2][:, 384:512], [PS[2]], [Sf[db]])
            cp("dve", Vdf[db][:, :, 0:64], PS[3][:, 0:256].rearrange("p (h c) -> p h c", c=64), [PS[3]], [Vdf[db]])
            cp("dve", Vsf[db][:, :, 0:64], PS[3][:, 256:384].rearrange("p (h c) -> p h c", c=64), [PS[3]], [Vsf[db]])

        def tail(i):
            db = i % 2
            S_ = st2[db]
            cr = Craw[db]
            ttr(junk2, junk2[:, 0:256], cr[:, 0:256], cr[:, 0:256], S_[:, 4:5], [cr], [S_])
            ttr(junk2, junk2[:, 0:128], cr[:, 256:384], cr[:, 256:384], S_[:, 8:9], [cr], [S_])
            rstd_ops(S_, 4, 256)
            rstd_ops(S_, 8, 128)
            stt("dve", Cn[db][:, 0:256], cr[:, 0:256], S_[:, 6:7], Gq[:, :], ALU.mult, ALU.mult, [cr, S_, Gq], [Cn[db]])
            stt("dve", Cn[db][:, 256:384], cr[:, 256:384], S_[:, 10:11], Gkv[:, :], ALU.mult, ALU.mult,
                [cr, S_, Gkv], [Cn[db]])
            rope(cr[:, 384:416].unsqueeze(1), krot[db][:, :].unsqueeze(1), 1, i, db, cr, krot[db])
            for k in range(3):
                tr(psb(7)[:, k * 128:(k + 1) * 128], Cn[db][:, k * 128:(k + 1) * 128], ident_b[:, :], [Cn[db], ident_b], [PS[7]])
            cp("act", CT[db][:, :], psb(7)[:, 0:384], [PS[7]], [CT[db]])
            for half in range(2):
                for k in range(2):
                    mm(PS[4 + half][:, 0:288], CT[db][:, k * 128:(k + 1) * 128], Wqb[:, k, half * 288:(half + 1) * 288],
                       k == 0, k == 1, [CT[db], Wqb], [PS[4 + half]])
            for half in range(2):
                q3 = PS[4 + half][:, 0:288].rearrange("p (h c) -> p h c", c=96)
                cp("act", Qf[db][:, 3 * half:3 * half + 3, 0:64], q3[:, :, 0:64], [PS[4 + half]], [Qf[db]])
                rope(q3[:, :, 64:96], Qf[db][:, 3 * half:3 * half + 3, 64:96], 3, i, db, PS[4 + half], Qf[db])
            for half in range(2):
                mm(PS[4 + half][:, 0:384], CT[db][:, 256:384], Wkvb[:, half * 384:(half + 1) * 384], True, True,
                   [CT[db], Wkvb], [PS[4 + half]])
            for half in range(2):
                kv3 = PS[4 + half][:, 0:384].rearrange("p (h c) -> p h c", c=128)
                cp("act", Kf[db][:, 3 * half:3 * half + 3, 0:64], kv3[:, :, 0:64], [PS[4 + half]], [Kf[db]])
                cp("act", Vf[db][:, 3 * half:3 * half + 3, 0:64], kv3[:, :, 64:128], [PS[4 + half]], [Vf[db]])
            cp("pool", Kf[db][:, :, 64:96], krot[db][:, :].unsqueeze(1).broadcast_to([128, 6, 32]), [krot[db]], [Kf[db]])
            for h in range(6):
                tr(psb(6)[0:96, h * 128:(h + 1) * 128], Qf[db][:, h, :], ident_b[:, :], [Qf[db], ident_b], [PS[6]])
            cp("act", QT[db][0:96, :], psb(6)[0:96, 0:768], [PS[6]], [QT[db]])
            for h in range(6):
                tr(psb(7)[0:96, h * 128:(h + 1) * 128], Kf[db][:, h, :], ident_b[:, :], [Kf[db], ident_b], [PS[7]])
            cp("dve", KT[db][0:96, :], psb(7)[0:96, 0:768], [PS[7]], [KT[db]])
            dma("pool", DAP(QTm, i * 128, (T, 96), (96 * T, 6), (1, 128)),
                QT[db][0:96, :].rearrange("p (h c) -> p h c", c=128), [QT[db]], [])
            dma("pool", DAP(kvown, OFF_M + i * 128, (T, 96), (RM * T, 6), (1, 128)),
                KT[db][0:96, :].rearrange("p (h c) -> p h c", c=128), [KT[db]], [])
            dma("pool", DAP(kvown, OFF_M + 96 * T + i * 65, (VW, 128), (RM * T, 6), (1, 65)), Vf[db][:, :, :], [Vf[db]], [])
            for m in range(4):
                tr(psb(6)[:, m * 128:(m + 1) * 128], Df[db][:, m * 128:(m + 1) * 128], ident_b[:, :], [Df[db], ident_b], [PS[6]])
            cp("act", DT[db][:, :], psb(6)[:, 0:512], [PS[6]], [DT[db]])
            dma("pool", DAP(QTd, i * 128, (T, 128), (128 * T, 2), (1, 128)),
                DT[db][:, 0:256].rearrange("p (h c) -> p h c", c=128), [DT[db]], [])
            for ph in range(2):
                dma("pool", DAP(kvown, OFF_D + ph * RD * T + i * 128, (T, 64), (2 * RD * T, 2), (1, 128)),
                    DT[db][64 * ph:64 * ph + 64, 256:512].rearrange("p (h c) -> p h c", c=128), [DT[db]], [])
            dma("pool", DAP(kvown, OFF_D + 64 * T + i * 65, (VW, 128), (RD * T, 4), (1, 65)), Vdf[db][:, :, :], [Vdf[db]], [])
            for r in range(4):
                tr(psb(7)[:, r * 128:(r + 1) * 128], Sf[db][:, r * 128:(r + 1) * 128], ident_b[:, :], [Sf[db], ident_b], [PS[7]])
            cp("dve", ST[db][:, :], psb(7)[:, 0:512], [PS[7]], [ST[db]])
            dma("pool", DAP(QTs, i * 384, (3 * T, 128), (1, 384)), ST[db][:, 0:384], [ST[db]], [])
            dma("pool", DAP(kvown, OFF_S0 + i * 128, (T, 128), (1, 128)), ST[db][:, 384:512], [ST[db]], [])
            dma("pool", DAP(kvown, OFF_S1 + i * 130, (2 * VW, 128), (1, 130)),
                Vsf[db][:, :, :].rearrange("p h c -> p (h c)"), [Vsf[db]], [])

        front(0)
        for i in range(nblk):
            if i + 1 < nblk:
                front(i + 1)
            tail(i)

    def phase_gather(l):
        P.barrier()
        for k, (nm, r0, nr) in enumerate(PIECES):
            o = Op("pool", lambda eng, nm=nm, r0=r0, nr=nr: eng.collective_compute(
                "AllGather", ALU.bypass, replica_groups=[[2 * i, 2 * i + 1] for i in range(ncores // 2)],
                ins=[kvown[r0:r0 + nr, :].opt()], outs=[kvall[nm].ap().opt()]), False)
            o.signal = True
            o.cc = ccsems[l * len(PIECES) + k]
            P.ops["pool"].append(o)
            pbuf[nm].ws = [o]
            pbuf[nm].r = []

    def kloc(j):
        if j == 0:
            return 0, 0
        return (j - 1) % 2, 1 + (j - 1) // 2

    def ffn_cast_jobs(l):
        jobs = []
        for (wsrc, wdst) in ((w_gate, Wb_g), (w_up, Wb_u)):
            for k in range(8):
                jobs.append((wsrc[l, k * 128:(k + 1) * 128, :], DFF,
                             DAP(wdst, l * NF * 128 * 1024 + k * 128, (1024, 128), (128 * 1024, NF), (1, 128)), True))
        for f in range(NF):
            jobs.append((w_down[l, f * 128:(f + 1) * 128, :], D, DAP(Wb_d, l * 128 * NF * D + f * D, (NF * D, 128), (1, D)), False))
        return jobs

    def background_setup(l):
        stg = [AFa.alloc(2816, "bgstg%d" % i) for i in range(2)]
        stb = [ABa.alloc(2816, "bgstb%d" % i) for i in range(2)]
        return {"jobs": ffn_cast_jobs(l), "stg": stg, "stb": stb, "n": 0, "tick": 0}

    def bg_one(bg):
        if not bg["jobs"]:
            return
        src, ncols, dst, blocked = bg["jobs"].pop(0)
        i = bg["n"] % 2
        bg["n"] += 1
        stg, stb = bg["stg"][i], bg["stb"][i]
        dma("act", stg[:, 0:ncols], src, [], [stg])
        cp("pool", stb[:, 0:ncols], stg[:, 0:ncols], [stg], [stb])
        if blocked:
            dma("pool", dst, stb[:, 0:ncols].rearrange("p (f c) -> p f c", c=128), [stb], [])
        else:
            dma("pool", dst, stb[:, 0:ncols], [stb], [])

    def bg_step(bg):
        bg["tick"] += 1
        if bg["tick"] % 12 == 0:
            bg_one(bg)

    def bg_finish(bg):
        while bg["jobs"]:
            bg_one(bg)

    def attention_core(units, lookahead=2, post_delay=1):
        n = len(units)
        for t in range(n + lookahead + post_delay):
            if t < n:
                units[t]["s"]()
                units[t]["e"]()
            if 0 <= t - lookahead < n:
                units[t - lookahead]["o"]()
            tp_ = t - lookahead - post_delay
            if 0 <= tp_ < n and units[tp_].get("post"):
                units[tp_]["post"]()

    def groups():
        gl = [(-1, 0, 128, [0])]
        for g in range(8):
            gl.append((g, (1 + 4 * g) * 128, 512, list(range(0, 8 * g + 9))))
        return gl

    def pair_view(t_ap, nk, N):
        if N == 512:
            return t_ap[0:nk, 0:1024]
        return t_ap[0:nk, 0:1024].rearrange("p (c n) -> p c n", c=2)[:, :, 0:N]

    def phase_attn_mla(l):
        AFa.reset(); ABa.reset()
        mA = AFa.alloc(9 * 512, "mA", "p (s c) -> p s c", c=512)
        dma("sp", mA[:, :, :], mA_in[:, :].rearrange("p (s c) -> p s c", c=512), [], [mA])
        KTt = [ABa.alloc(2 * T, "KTt%d" % i) for i in range(2)]
        Vt = [ABa.alloc(2 * VW, "Vt%d" % i) for i in range(2)]
        QTt = [ABa.alloc(T, "QTt%d" % i) for i in range(2)]
        Pt = [ABa.alloc(1024, "Pt%d" % i) for i in range(3)]
        yo = [ABa.alloc(512, "yo%d" % i) for i in range(2)]
        rd = [AFa.alloc(512, "rd%d" % i) for i in range(2)]
        bcs = [AFa.alloc(512, "bcs%d" % i) for i in range(2)]
        bg = background_setup(l)
        cnt = {"s": 0, "p": 0, "o": 0, "y": 0}
        for h in range(6):
            hb = h % 2
            for r in range(NR):
                dma("sp", KTt[hb][0:96, r * T:(r + 1) * T], DAP(kvall["m%d" % h], r * RM * T, (T, 96), (1, T)),
                    [pbuf["m%d" % h]], [KTt[hb]])
                dma("sp", Vt[hb][:, r * VW:(r + 1) * VW], DAP(kvall["m%d" % h], r * RM * T + 96 * T, (VW, 128), (1, VW)),
                    [pbuf["m%d" % h]], [Vt[hb]])
            dma("sp", QTt[hb][0:96, :], DAP(QTm, h * 96 * T, (T, 96), (1, T)), [], [QTt[hb]])
            units = []
            for (g, q0, N, js) in groups():
                ob = cnt["o"] % 2
                cnt["o"] += 1
                Ob = PS[4 + ob]
                subs = [[js[0]]] + [js[k:k + 2] for k in range(1, len(js), 2)]
                for jl in subs:
                    sp_ = cnt["s"] % 2
                    cnt["s"] += 1
                    pt = Pt[cnt["p"] % 3]
                    cnt["p"] += 1
                    nk = 16 if jl[0] == 0 else 128
                    info = []
                    for j in jl:
                        rk, lc = kloc(j)
                        info.append((rk * T + lc * 128, rk * VW + lc * 65))
                    if g == -1:
                        mk = mA[0:nk, 8, 0:N]
                    elif jl[0] >= 8 * g + 1:
                        s0 = jl[0] - (8 * g + 1)
                        mk = mA[0:nk, s0:s0 + len(jl), :].rearrange("p s c -> p (s c)")
                    else:
                        mk = None
                    banks = [PS[2 * sp_ + bi] for bi in range(len(jl))]
                    u = {}

                    def s_(banks=banks, info=info, nk=nk, q0=q0, N=N, hb=hb):
                        for bi, (kc, vc) in enumerate(info):
                            mm(banks[bi][0:nk, 0:N], KTt[hb][0:96, kc:kc + nk], QTt[hb][0:96, q0:q0 + N], True, True,
                               [KTt[hb], QTt[hb]], [banks[bi]])

                    def e_(banks=banks, sp_=sp_, pt=pt, nk=nk, N=N, mk=mk, nb=len(jl)):
                        if nb == 2:
                            src = pair_view(psum_t[:, 2 * sp_ * 512:(2 * sp_ + 2) * 512], nk, N)
                            dst = pair_view(pt[:, :], nk, N)
                        else:
                            src = banks[0][0:nk, 0:N]
                            dst = pt[0:nk, 0:N]
                        act(dst, src, AF.Exp, banks, [pt], scale=SC_MLA)
                        if mk is not None:
                            tt("dve", dst, dst, mk, ALU.mult, [pt, mA], [pt])

                    def o_(Ob=Ob, pt=pt, nk=nk, info=info, N=N, hb=hb, jl=jl, js=js):
                        for bi, (kc, vc) in enumerate(info):
                            mm(Ob[0:65, 0:N], Vt[hb][0:nk, vc:vc + 65], pt[0:nk, bi * 512:bi * 512 + N],
                               jl[bi] == js[0], jl[bi] == js[-1], [Vt[hb], pt], [Ob])
                        bg_step(bg)

                    u["s"], u["e"], u["o"] = s_, e_, o_
                    if jl[-1] == js[-1]:
                        yb = cnt["y"] % 2
                        cnt["y"] += 1

                        def post(Ob=Ob, N=N, q0=q0, yb=yb, h=h):
                            act(rd[yb][64:65, 0:N], Ob[64:65, 0:N], AF.Ln, [Ob], [rd[yb]])
                            act(rd[yb][64:65, 0:N], rd[yb][64:65, 0:N], AF.Exp, [rd[yb]], [rd[yb]], scale=-1.0)
                            mm(PS[6][0:64, 0:N], ones_f[64:65, 0:64], rd[yb][64:65, 0:N], True, True, [ones_f, rd[yb]], [PS[6]])
                            cp("dve", bcs[yb][0:64, 0:N], PS[6][0:64, 0:N], [PS[6]], [bcs[yb]])
                            tt("dve", yo[yb][0:64, 0:N], Ob[0:64, 0:N], bcs[yb][0:64, 0:N], ALU.mult, [Ob, bcs[yb]], [yo[yb]])
                            dma("pool", DAP(yT, h * 64 * T + q0, (T, 64), (1, N)), yo[yb][0:64, 0:N], [yo[yb]], [])

                        u["post"] = post
                    units.append(u)
            attention_core(units, lookahead=1)
        bg_finish(bg)

    def phase_attn_diff(l):
        AFa.reset(); ABa.reset()
        KD = [ABa.alloc(2 * T, "KD%d" % i) for i in range(2)]
        VD = [ABa.alloc(2 * VW, "VD%d" % i) for i in range(2)]
        QD = [ABa.alloc(T, "QD%d" % i) for i in range(2)]
        Pt = [ABa.alloc(1024, "Pt%d" % i) for i in range(3)]
        yo = [ABa.alloc(512, "yo%d" % i) for i in range(2)]
        MD = AFa.alloc(NME * 128, "MD")
        Ef = [AFa.alloc(1024, "Ef%d" % i) for i in range(2)]
        rd = [AFa.alloc(512, "rd%d" % i) for i in range(2)]
        Oc = [AFa.alloc(512, "Oc%d" % i) for i in range(2)]
        d0 = AFa.alloc(512, "d0")
        d1 = AFa.alloc(512, "d1")
        rs = AFa.alloc(512, "rs")
        cnt = {"s": 0, "p": 0, "e": 0}
        for h in range(4):
            hb = h % 2
            for r in range(NR):
                dma("sp", KD[hb][0:64, r * T:(r + 1) * T],
                    DAP(kvall["d%d" % h], r * RD * T, (T, 64), (1, T)), [pbuf["d%d" % h]], [KD[hb]])
                dma("sp", VD[hb][:, r * VW:(r + 1) * VW], DAP(kvall["d%d" % h], r * RD * T + 64 * T, (VW, 128), (1, VW)),
                    [pbuf["d%d" % h]], [VD[hb]])
            dma("sp", QD[hb][0:64, :], DAP(QTd, 64 * h * T, (T, 64), (1, T)), [], [QD[hb]])
            dma("sp", MD[:, :], DAP(maskD, h * 128 * NME * 128, (NME * 128, 128), (1, NME * 128)), [], [MD])
            units = []
            for (g, q0, N, js) in groups():
                for j in js:
                    rk, lc = kloc(j)
                    nk = 16 if j == 0 else 128
                    kc = rk * T + lc * 128
                    vc = rk * VW + lc * 65
                    sp_ = cnt["s"] % 2
                    cnt["s"] += 1
                    pt = Pt[cnt["p"] % 3]
                    cnt["p"] += 1
                    if g == -1:
                        mk = MD[0:nk, 40 * 128:40 * 128 + N]
                    elif g == 0 and j == 0:
                        mk = MD[0:nk, 36 * 128:36 * 128 + N]
                    elif j >= 8 * g and j > 0:
                        s0 = j - 8 * g
                        mk = MD[0:nk, s0 * 512:s0 * 512 + N]
                    else:
                        mk = None
                    banks = [PS[2 * sp_], PS[2 * sp_ + 1]]
                    u = {}

                    def s_(banks=banks, nk=nk, kc=kc, q0=q0, N=N, hb=hb):
                        for c in range(2):
                            mm(banks[c][0:nk, 0:N], KD[hb][32 * c:32 * c + 32, kc:kc + nk], QD[hb][32 * c:32 * c + 32, q0:q0 + N],
                               True, True, [KD[hb], QD[hb]], [banks[c]])

                    def e_(banks=banks, sp_=sp_, pt=pt, nk=nk, N=N, mk=mk):
                        src = pair_view(psum_t[:, 2 * sp_ * 512:(2 * sp_ + 2) * 512], nk, N)
                        dst = pair_view(pt[:, :], nk, N)
                        if mk is None:
                            act(dst, src, AF.Exp, banks, [pt], scale=SC_DIFF)
                        else:
                            ef = Ef[cnt["e"] % 2]
                            cnt["e"] += 1
                            efv = pair_view(ef[:, :], nk, N)
                            act(efv, src, AF.Exp, banks, [ef], scale=SC_DIFF)
                            for c in range(2):
                                tt("dve", pt[0:nk, c * 512:c * 512 + N], ef[0:nk, c * 512:c * 512 + N], mk, ALU.mult, [ef, MD], [pt])

                    def o_(pt=pt, nk=nk, vc=vc, N=N, hb=hb, first=(j == js[0]), last=(j == js[-1])):
                        for c in range(2):
                            mm(PS[4 + c][0:65, 0:N], VD[hb][0:nk, vc:vc + 65], pt[0:nk, c * 512:c * 512 + N], first, last,
                               [VD[hb], pt], [PS[4 + c]])
                        if WARM:
                            mm(PS[7][:, 0:512], VD[hb][:, 0:128], VD[hb][:, 128:640], True, True, [VD[hb]], [PS[7]])

                    u["s"], u["e"], u["o"] = s_, e_, o_
                    if j == js[-1]:
                        def post(N=N, q0=q0, h=h):
                            O0, O1 = PS[4], PS[5]
                            cp("dve", Oc[0][0:65, 0:N], O0[0:65, 0:N], [O0], [Oc[0]])
                            cp("dve", Oc[1][0:65, 0:N], O1[0:65, 0:N], [O1], [Oc[1]])
                            for c in range(2):
                                act(rd[c][64:65, 0:N], Oc[c][64:65, 0:N], AF.Ln, [Oc[c]], [rd[c]])
                                act(rd[c][64:65, 0:N], rd[c][64:65, 0:N], AF.Exp, [rd[c]], [rd[c]], scale=-1.0)
                            ts("dve", rd[1][64:65, 0:N], rd[1][64:65, 0:N], lam_t[64:65, l:l + 1], None, ALU.mult, None,
                               [rd[1], lam_t], [rd[1]])
                            mm(PS[6][0:64, 0:N], ones_f[64:65, 0:64], rd[0][64:65, 0:N], True, True, [ones_f, rd[0]], [PS[6]])
                            mm(PS[7][0:64, 0:N], ones_f[64:65, 0:64], rd[1][64:65, 0:N], True, True, [ones_f, rd[1]], [PS[7]])
                            tt("dve", d0[0:64, 0:N], Oc[0][0:64, 0:N], PS[6][0:64, 0:N], ALU.mult, [Oc[0], PS[6]], [d0])
                            tt("dve", d1[0:64, 0:N], Oc[1][0:64, 0:N], PS[7][0:64, 0:N], ALU.mult, [Oc[1], PS[7]], [d1])
                            tt("pool", d0[0:64, 0:N], d0[0:64, 0:N], d1[0:64, 0:N], ALU.subtract, [d0, d1], [d0])
                            tt("pool", d1[0:64, 0:N], d0[0:64, 0:N], d0[0:64, 0:N], ALU.mult, [d0], [d1])
                            mm(PS[6][0:64, 0:N], ones_f[0:64, 0:64], d1[0:64, 0:N], True, True, [ones_f, d1], [PS[6]])
                            act(rs[0:64, 0:N], PS[6][0:64, 0:N], AF.Ln, [PS[6]], [rs], scale=1.0 / 64, bias=EPS)
                            act(rs[0:64, 0:N], rs[0:64, 0:N], AF.Exp, [rs], [rs], scale=-0.5)
                            yb = yo[(q0 // 128) % 2]
                            stt("dve", yb[0:64, 0:N], d0[0:64, 0:N], gsub_t[0:64, l:l + 1], rs[0:64, 0:N], ALU.mult, ALU.mult,
                                [d0, gsub_t, rs], [yb])
                            dma("pool", DAP(yT, (384 + h * 64) * T + q0, (T, 64), (1, N)), yb[0:64, 0:N], [yb], [])

                        u["post"] = post
                    units.append(u)
            attention_core(units, lookahead=1, post_delay=0)

    def phase_attn_swa(l):
        AFa.reset(); ABa.reset()
        KS = ABa.alloc(2 * T, "KS")
        VS = ABa.alloc(2 * 2 * VW, "VS")
        QS = ABa.alloc(3 * T, "QS")
        Pt = [ABa.alloc(384, "Pt%d" % i) for i in range(4)]
        yo = [ABa.alloc(384, "yo%d" % i) for i in range(2)]
        MS = AFa.alloc(2 * 4 * 384, "MS", "p (g e c) -> p g e c", g=2, e=4)
        MSq = AFa.alloc(6 * 128, "MSq")
        Ef = [AFa.alloc(384, "Ef%d" % i) for i in range(2)]
        rd = [AFa.alloc(384, "rd%d" % i) for i in range(2)]
        bcs = [AFa.alloc(384, "bcs%d" % i) for i in range(2)]
        for r in range(NR):
            dma("sp", KS[:, r * T:(r + 1) * T], DAP(kvall["s0"], r * RS0 * T, (T, 128), (1, T)), [pbuf["s0"]], [KS])
            dma("sp", VS[:, r * 2 * VW:(r + 1) * 2 * VW], DAP(kvall["s1"], r * RS1 * T, (2 * VW, 128), (1, 2 * VW)), [pbuf["s1"]], [VS])
        dma("sp", QS[:, :], DAP(QTs, 0, (3 * T, 128), (1, 3 * T)), [], [QS])
        dma("sp", MS[:, :, :, :], maskS[:, :].rearrange("p (g e c) -> p g e c", g=2, e=4), [], [MS])
        dma("sp", MSq[0:16, :], DAP(zper, 0, (768, 16), (1, 768)), [], [MSq])
        cnt = {"s": 0, "p": 0, "e": 0, "o": 0, "y": 0}
        units = []
        for i in range(NLOC):
            for g in range(2):
                kts = []
                if i == 0:
                    kts.append((0, MSq[0:16, 3 * g * 128:(3 * g + 3) * 128]))
                else:
                    kts.append((0, MS[0:16, g, 3, :] if i == 1 else None))
                    for s in range(3):
                        j = 2 * (i - 1) + s
                        if j == 0 or j > 64:
                            continue
                        kts.append((j, MS[:, g, s, :]))
                ob = cnt["o"] % 2
                cnt["o"] += 1
                Ob = PS[3 + ob]
                for ti, (j, mk) in enumerate(kts):
                    rk, lc = kloc(j)
                    nk = 16 if j == 0 else 128
                    kc = rk * T + lc * 128
                    vc = rk * 2 * VW + lc * 130 + g * 65
                    sb = PS[cnt["s"] % 3]
                    cnt["s"] += 1
                    pt = Pt[cnt["p"] % 4]
                    cnt["p"] += 1
                    u = {}

                    def s_(sb=sb, nk=nk, kc=kc, i=i, g=g):
                        mm(sb[0:nk, 0:384], KS[64 * g:64 * g + 64, kc:kc + nk], QS[64 * g:64 * g + 64, i * 384:(i + 1) * 384],
                           True, True, [KS, QS], [sb])

                    def e_(sb=sb, pt=pt, nk=nk, mk=mk):
                        if mk is None:
                            act(pt[0:nk, 0:384], sb[0:nk, 0:384], AF.Exp, [sb], [pt], scale=SC_SWA)
                        else:
                            ef = Ef[cnt["e"] % 2]
                            cnt["e"] += 1
                            act(ef[0:nk, 0:384], sb[0:nk, 0:384], AF.Exp, [sb], [ef], scale=SC_SWA)
                            tt("dve", pt[0:nk, 0:384], ef[0:nk, 0:384], mk, ALU.mult, [ef, MS, MSq], [pt])

                    def o_(Ob=Ob, pt=pt, nk=nk, vc=vc, first=(ti == 0), last=(ti == len(kts) - 1)):
                        mm(Ob[0:65, 0:384], VS[0:nk, vc:vc + 65], pt[0:nk, 0:384], first, last, [VS, pt], [Ob])

                    u["s"], u["e"], u["o"] = s_, e_, o_
                    if ti == len(kts) - 1:
                        yb = cnt["y"] % 2
                        cnt["y"] += 1

                        def post(Ob=Ob, i=i, g=g, yb=yb):
                            sr = sinkrow[64:65, l * 6 + 3 * g:l * 6 + 3 * g + 3, :].rearrange("p a b -> p (a b)")
                            tt("dve", rd[yb][64:65, 0:384], Ob[64:65, 0:384], sr, ALU.add, [Ob, sinkrow], [rd[yb]])
                            P.op("dve", lambda eng: eng.reciprocal(rd[yb][64:65, 0:384], rd[yb][64:65, 0:384]), [rd[yb]], [rd[yb]])
                            mm(PS[5][0:64, 0:384], ones_f[64:65, 0:64], rd[yb][64:65, 0:384], True, True, [ones_f, rd[yb]], [PS[5]])
                            cp("dve", bcs[yb][0:64, 0:384], PS[5][0:64, 0:384], [PS[5]], [bcs[yb]])
                            tt("dve", yo[yb][0:64, 0:384], Ob[0:64, 0:384], bcs[yb][0:64, 0:384], ALU.mult, [Ob, bcs[yb]], [yo[yb]])
                            dma("pool", DAP(yT, (640 + 3 * g * 64) * T + i * 128, (T, 64), (64 * T, 3), (1, 128)),
                                yo[yb][0:64, 0:384].rearrange("p (r c) -> p r c", c=128), [yo[yb]], [])

                        u["post"] = post
                    units.append(u)
        attention_core(units)

    def phase_ffn(l, last):
        AFa.reset(); ABa.reset()
        Wd = ABa.alloc(NF * D, "Wd", "p (f c) -> p f c", c=D)
        Wo = ABa.alloc(8 * D, "Wo", "p (k c) -> p k c", c=D)
        for f0 in range(0, NF, 6):
            f1 = min(NF, f0 + 6)
            dma("sp", Wd[:, f0:f1, :], DAP(Wb_d, l * 128 * NF * D + f0 * D, (NF * D, 128), (D, f1 - f0), (1, D)), [], [Wd])
        dma("sp", Wo[:, :, :], DAP(Wb_out, l * 128 * 8 * D, (8 * D, 128), (D, 8), (1, D)), [], [Wo])
        AT = ABa.alloc(NF * 512, "AT", "p (f c) -> p f c", c=512)
        HnT = ABa.alloc(8 * 512, "HnT", "p (k c) -> p k c", c=512)
        WGU = [ABa.alloc(2 * 1024, "WGU%d" % i, "p (u k c) -> p u k c", u=2, k=8) for i in range(2)]
        YT = [ABa.alloc(1024, "YT%d" % i, "p (k c) -> p k c", c=128) for i in range(2)]
        Hn = [ABa.alloc(1024, "Hn%d" % i) for i in range(2)]
        HT = AFa.alloc(4 * 1024, "HT", "p (b c) -> p b c", c=1024)
        Gff = AFa.alloc(1024, "Gff")
        dma("sp", Gff[:, :], DAP(ffn_norm, l * D, (0, 128), (1, D)), [], [Gff])
        if last:
            Gfin = AFa.alloc(1024, "Gfin")
            dma("sp", Gfin[:, :], DAP(final_norm, 0, (0, 128), (1, D)), [], [Gfin])
            OUTT = [AFa.alloc(1024, "OUTT%d" % i) for i in range(2)]
        SG = [AFa.alloc(512, "SG%d" % i) for i in range(2)]
        junk = AFa.alloc(1024, "junk")
        st = [AFa.alloc(16, "st%d" % i) for i in range(2)]
        tiles = [[0]] + [list(range(1 + 4 * t, 5 + 4 * t)) for t in range(8)]
        cnt = {"y": 0, "w": 0, "g": 0, "o": 0}
        for bl in tiles:
            nb = len(bl)
            NT = nb * 128
            for bi, i in enumerate(bl):
                yb = cnt["y"] % 2
                cnt["y"] += 1
                dma("sp", HT[:, bi, :], hsrc(l, i), [], [HT])
                dma("sp", YT[yb][:, :, :], DAP(yT, i * 128, (T, 128), (128 * T, 8), (1, 128)), [], [YT[yb]])
                for half in range(2):
                    for k in range(8):
                        mm(PS[half][:, 0:512], YT[yb][:, k, :], Wo[:, k, half * 512:(half + 1) * 512], k == 0, k == 7,
                           [YT[yb], Wo], [PS[half]])
                for half in range(2):
                    tt("dve", HT[:, bi, half * 512:(half + 1) * 512], HT[:, bi, half * 512:(half + 1) * 512], PS[half][:, 0:512],
                       ALU.add, [HT, PS[half]], [HT])
                S_ = st[yb]
                ttr(junk, junk[:, :], HT[:, bi, :], HT[:, bi, :], S_[:, 0:1], [HT], [S_])
                rstd_ops(S_, 0, 1024)
                stt("dve", Hn[yb][:, :], HT[:, bi, :], S_[:, 2:3], Gff[:, :], ALU.mult, ALU.mult, [HT, S_, Gff], [Hn[yb]])
                for k in range(8):
                    tr(psb(2)[:, k * 128:(k + 1) * 128], Hn[yb][:, k * 128:(k + 1) * 128], ident_b[:, :], [Hn[yb], ident_b], [PS[2]])
                cp("act", HnT[:, :, bi * 128:(bi + 1) * 128], psb(2)[:, :].rearrange("p (k c) -> p k c", c=128), [PS[2]], [HnT])
            for f in range(NF):
                wb = cnt["w"] % 2
                cnt["w"] += 1
                dma("sp", WGU[wb][:, 0, :, :], DAP(Wb_g, (l * NF + f) * 128 * 1024, (1024, 128), (128, 8), (1, 128)), [], [WGU[wb]])
                dma("sp", WGU[wb][:, 1, :, :], DAP(Wb_u, (l * NF + f) * 128 * 1024, (1024, 128), (128, 8), (1, 128)), [], [WGU[wb]])
                gb = cnt["g"] % 2
                cnt["g"] += 1
                PG, PU = PS[3 + 2 * gb], PS[4 + 2 * gb]
                for k in range(8):
                    mm(PG[:, 0:NT], WGU[wb][:, 0, k, :], HnT[:, k, 0:NT], k == 0, k == 7, [WGU[wb], HnT], [PG])
                for k in range(8):
                    mm(PU[:, 0:NT], WGU[wb][:, 1, k, :], HnT[:, k, 0:NT], k == 0, k == 7, [WGU[wb], HnT], [PU])
                act(SG[gb][:, 0:NT], PG[:, 0:NT], AF.Silu, [PG], [SG[gb]])
                tt("dve", AT[:, f, 0:NT], SG[gb][:, 0:NT], PU[:, 0:NT], ALU.mult, [SG[gb], PU], [AT])
            for bi, i in enumerate(bl):
                for half in range(2):
                    for f in range(NF):
                        mm(PS[half][:, 0:512], AT[:, f, bi * 128:(bi + 1) * 128], Wd[:, f, half * 512:(half + 1) * 512],
                           f == 0, f == NF - 1, [AT, Wd], [PS[half]])
                for half in range(2):
                    tt("dve", HT[:, bi, half * 512:(half + 1) * 512], HT[:, bi, half * 512:(half + 1) * 512], PS[half][:, 0:512],
                       ALU.add, [HT, PS[half]], [HT])
                if not last:
                    dma("pool", hres[i, :, :], HT[:, bi, :], [HT], [])
                if last and i >= 1:
                    ob = cnt["o"] % 2
                    cnt["o"] += 1
                    S_ = st[ob]
                    ttr(junk, junk[:, :], HT[:, bi, :], HT[:, bi, :], S_[:, 4:5], [HT], [S_])
                    rstd_ops(S_, 4, 1024)
                    stt("dve", OUTT[ob][:, :], HT[:, bi, :], S_[:, 6:7], Gfin[:, :], ALU.mult, ALU.mult, [HT, S_, Gfin], [OUTT[ob]])
                    dma("pool", out[i - 1, :, :], OUTT[ob][:, :], [OUTT[ob]], [])

    steps = [("weights", phase_weights), ("masks", lambda: (phase_masks(), phase_init()))]
    for l in range(2):
        steps.append(("proj%d" % l, lambda l=l: phase_proj(l)))
        steps.append(("gather%d" % l, lambda l=l: phase_gather(l)))
        steps.append(("mla%d" % l, lambda l=l: phase_attn_mla(l)))
        steps.append(("diff%d" % l, lambda l=l: phase_attn_diff(l)))
        steps.append(("swa%d" % l, lambda l=l: phase_attn_swa(l)))
        steps.append(("ffn%d" % l, lambda l=l: phase_ffn(l, l == 1)))
    barrier_before = ("masks", "proj0", "proj1", "ffn0", "ffn1", "diff0", "diff1", "swa0", "swa1")
    for si, (name, fn) in enumerate(steps):
        if stop is not None and si >= stop:
            break
        if name in barrier_before:
            P.barrier()
        fn()
    P.barrier()
    if debug and "kvown" in dump:
        dma("sp", kvodbg[:, :], kvown[:, :], [], [])
        P.barrier()

    ninst = emit_all(P, sems, dsems)
    for cm in reversed(ctx):
        cm.__exit__(None, None, None)
    return nc, ninst


def emit_all(P, sems, dsems):
    nc = P.nc
    engs = {"pe": nc.tensor, "act": nc.scalar, "dve": nc.vector, "pool": nc.gpsimd, "sp": nc.sync}
    for e in Prog.ENG:
        c = 0
        nd = 0
        for o in P.ops[e]:
            if o.dma:
                o.sem = dsems[e][nd % NDMASEM]
                o.val = 16 * (nd // NDMASEM + 1)
                nd += 1
            elif o.cc is not None:
                o.sem = o.cc
                o.val = 1
            elif o.signal:
                c += 1
                o.sem = sems[e]
                o.val = c
    ninst = 0
    for e in Prog.ENG:
        eng = engs[e]
        waited = {}
        for o in P.ops[e]:
            for d in o.deps:
                key = id(d.sem)
                if waited.get(key, 0) < d.val:
                    eng.wait_ge(d.sem, d.val)
                    waited[key] = d.val
                    ninst += 1
            if o.fn is not None:
                ins = o.fn(eng)
                ninst += 1
                if o.dma:
                    ins.then_inc(o.sem, 16)
                elif o.cc is not None:
                    ins.then_inc(o.sem)
                elif o.signal:
                    ins.then_inc(o.sem, 1)
    return ninst


_CACHE = {}


def _get_program(debug=False):
    if debug not in _CACHE:
        _CACHE[debug] = build_program(debug)
    return _CACHE[debug]


PARAM_NAMES = ["meta_tokens", "rel_bias", "attn_norm", "w_in", "mla_q_norm", "mla_w_qb", "mla_kv_norm", "mla_w_kvb",
               "diff_lambda", "diff_subln", "swa_sinks", "w_out", "ffn_norm", "w_gate", "w_up", "w_down", "final_norm"]


def make_in_maps(inputs):
    x = np.ascontiguousarray(np.asarray(inputs["x"], dtype=np.float32))
    params = {k: np.ascontiguousarray(np.asarray(inputs[k], dtype=np.float32)) for k in PARAM_NAMES}
    consts = [make_consts(r) for r in range(NR)]
    in_maps = []
    for c in range(8):
        b, r = c // 2, c % 2
        xb = x[b].reshape(64, 128, D)[r::2]
        m = {"xin": np.ascontiguousarray(xb)}
        m.update(params)
        m.update(consts[r])
        in_maps.append(m)
    return in_maps


def kernel(**inputs):
    nc, _ = _get_program(False)
    in_maps = make_in_maps(inputs)
    res = run_bass_kernel_spmd(nc, in_maps, core_ids=list(range(8)))
    outp = np.empty((4, 64, 128, D), np.float32)
    for c in range(8):
        b, r = c // 2, c % 2
        outp[b, r::2] = np.asarray(res.results[c]["out"]).reshape(32, 128, D)
    return outp.reshape(4, 8192, D)
```
